# Optimizing a Trainium2 kernel written in Bass

```python
import math
import jax, jax.numpy as jnp
from jax import lax
import numpy as np

D_MODEL = 1024
BATCH = 8
SEQ = 2048
DEPTH = 1

N_ATTN_HEADS = 8
HEAD_DIM = 64
D_ATTN = N_ATTN_HEADS * HEAD_DIM
MOBA_BLOCK = 256
MOBA_TOPK = 3
MOBA_Q_CHUNK = 32
N_SSM_HEADS = 16
SSM_HEAD_DIM = 64
D_SSM = N_SSM_HEADS * SSM_HEAD_DIM
N_SSM_GROUPS = 2
D_STATE = 128
SSM_CONV = 4
SSD_CHUNK = 256
D_XBC = D_SSM + 2 * N_SSM_GROUPS * D_STATE
D_MIX = D_ATTN + D_SSM
D_IN_PROJ = 3 * D_ATTN + D_SSM + D_XBC + N_SSM_HEADS
D_FF = 2816
FFN_CONV = 3
DEEPNORM_ALPHA = (2.0 * DEPTH) ** 0.25
DEEPNORM_BETA = (8.0 * DEPTH) ** -0.25
LN_EPS = 1e-5
RMS_EPS = 1e-5
PAD_MULT = max(MOBA_BLOCK, SSD_CHUNK)

kernel_name = "hymba_moba_ssd_convffn_deepnorm_adaln"


def layer_norm(x, g, b):
    xf = x.astype(jnp.float32)
    mu = jnp.mean(xf, -1, keepdims=True)
    var = jnp.mean(jnp.square(xf - mu), -1, keepdims=True)
    return ((xf - mu) * lax.rsqrt(var + LN_EPS) * g + b).astype(x.dtype)


def causal_dwconv(x, w, b):
    k = w.shape[0]
    y = lax.conv_general_dilated(x, w[:, None, :].astype(x.dtype), window_strides=(1,),
                                 padding=[(k - 1, 0)], dimension_numbers=('NWC', 'WIO', 'NWC'),
                                 feature_group_count=x.shape[-1])
    return y + b.astype(x.dtype)


def alibi_slopes(n):
    return jnp.asarray(2.0 ** (-8.0 * np.arange(1, n + 1) / n), dtype=jnp.float32)


def moba_attention(q, k, v):
    b, h, sp, dh = q.shape
    nb = sp // MOBA_BLOCK
    ksel = min(MOBA_TOPK, nb)
    scale = dh ** -0.5
    slopes = alibi_slopes(h)
    kb = k.reshape(b, h, nb, MOBA_BLOCK, dh)
    vb = v.reshape(b, h, nb, MOBA_BLOCK, dh)
    k_mean = jnp.mean(kb.astype(jnp.float32), axis=3)
    q_blk = jnp.arange(sp) // MOBA_BLOCK
    gate = jnp.einsum('bhsd,bhnd->bhsn', q.astype(jnp.float32), k_mean)
    past = jnp.arange(nb)[None, :] < q_blk[:, None]
    gate = jnp.where(past, gate, -jnp.inf)
    _, idx = lax.top_k(gate, ksel)
    valid = idx < q_blk[:, None]

    nc = sp // MOBA_Q_CHUNK

    def to_chunks(a):
        a = a.reshape((b, h, nc, MOBA_Q_CHUNK) + a.shape[3:])
        return jnp.moveaxis(a, 2, 0)

    bi = jnp.arange(b)[:, None, None, None]
    hi = jnp.arange(h)[None, :, None, None]
    off = jnp.arange(MOBA_BLOCK)

    def chunk_attn(args):
        qc, idxc, validc, ci = args
        t = ci * MOBA_Q_CHUNK + jnp.arange(MOBA_Q_CHUNK)
        own = (ci * MOBA_Q_CHUNK) // MOBA_BLOCK
        kg = kb[bi, hi, idxc]
        vg = vb[bi, hi, idxc]
        s_sel = idxc[..., None] * MOBA_BLOCK + off
        sc_sel = (jnp.einsum('bhqd,bhqjkd->bhqjk', qc, kg).astype(jnp.float32) * scale
                  - slopes[:, None, None, None] * (t[:, None, None] - s_sel))
        sc_sel = jnp.where(validc[..., None], sc_sel, -jnp.inf)
        k_own = lax.dynamic_index_in_dim(kb, own, axis=2, keepdims=False)
        v_own = lax.dynamic_index_in_dim(vb, own, axis=2, keepdims=False)
        s_own = own * MOBA_BLOCK + off
        dist = t[:, None] - s_own[None, :]
        sc_own = (jnp.einsum('bhqd,bhkd->bhqk', qc, k_own).astype(jnp.float32) * scale
                  - slopes[:, None, None] * dist)
        sc_own = jnp.where(dist >= 0, sc_own, -jnp.inf)
        sc = jnp.concatenate([sc_sel.reshape(b, h, MOBA_Q_CHUNK, ksel * MOBA_BLOCK), sc_own], -1)
        p = jax.nn.softmax(sc, axis=-1).astype(v.dtype)
        p_sel = p[..., :ksel * MOBA_BLOCK].reshape(b, h, MOBA_Q_CHUNK, ksel, MOBA_BLOCK)
        p_own = p[..., ksel * MOBA_BLOCK:]
        return (jnp.einsum('bhqjk,bhqjkd->bhqd', p_sel, vg)
                + jnp.einsum('bhqk,bhkd->bhqd', p_own, v_own))

    out = lax.map(chunk_attn, (to_chunks(q), to_chunks(idx), to_chunks(valid), jnp.arange(nc)))
    return jnp.moveaxis(out, 0, 2).reshape(b, h, sp, dh)


def segsum_exp(a):
    cs = jnp.cumsum(a, -1)
    diff = cs[..., :, None] - cs[..., None, :]
    n = a.shape[-1]
    mask = jnp.tril(jnp.ones((n, n), dtype=bool))
    return jnp.exp(jnp.where(mask, diff, -jnp.inf))


def ssd_scan(x, dt, a, bmat, cmat):
    b, sp, h, p = x.shape
    n = bmat.shape[-1]
    nc = sp // SSD_CHUNK
    xc = (x.astype(jnp.float32) * dt[..., None]).reshape(b, nc, SSD_CHUNK, h, p)
    bc = bmat.astype(jnp.float32).reshape(b, nc, SSD_CHUNK, h, n)
    cc = cmat.astype(jnp.float32).reshape(b, nc, SSD_CHUNK, h, n)
    dac = jnp.moveaxis((dt * a).reshape(b, nc, SSD_CHUNK, h), -1, 1)
    da_cs = jnp.cumsum(dac, -1)
    lmat = segsum_exp(dac)
    scores = jnp.einsum('bclhn,bcshn->bhcls', cc, bc) * lmat
    y_diag = jnp.einsum('bhcls,bcshp->bclhp', scores, xc)
    decay = jnp.exp(da_cs[..., -1:] - da_cs)
    states = jnp.einsum('bclhn,bhcl,bclhp->bchpn', bc, decay, xc)
    chunk_decay = jnp.exp(da_cs[..., -1])

    def step(carry, inp):
        st, dec = inp
        return carry * dec[..., None, None] + st, carry

    init = jnp.zeros((b, h, p, n), jnp.float32)
    _, prev = lax.scan(step, init, (jnp.moveaxis(states, 1, 0), jnp.moveaxis(chunk_decay, 2, 0)))
    prev = jnp.moveaxis(prev, 0, 1)
    y_off = jnp.einsum('bclhn,bchpn,bhcl->bclhp', cc, prev, jnp.exp(da_cs))
    return (y_diag + y_off).reshape(b, sp, h, p)


def hybrid_mixer(h, w_in, conv_w, conv_b, dt_bias, a_log, d_skip, norm_w, w_out):
    b, s, _ = h.shape
    sp = -(-s // PAD_MULT) * PAD_MULT
    proj = jnp.pad(h @ w_in, ((0, 0), (0, sp - s), (0, 0)))
    q, k, v, z, xbc, dt_raw = jnp.split(
        proj, [D_ATTN, 2 * D_ATTN, 3 * D_ATTN, 3 * D_ATTN + D_SSM, 3 * D_ATTN + D_SSM + D_XBC], axis=-1)

    def heads(t):
        return t.reshape(b, sp, N_ATTN_HEADS, HEAD_DIM).transpose(0, 2, 1, 3)
    o_attn = moba_attention(heads(q), heads(k), heads(v))
    o_attn = o_attn.transpose(0, 2, 1, 3).reshape(b, sp, D_ATTN)

    xbc = jax.nn.silu(causal_dwconv(xbc, conv_w, conv_b))
    xs, bm, cm = jnp.split(xbc, [D_SSM, D_SSM + N_SSM_GROUPS * D_STATE], axis=-1)
    xs = xs.reshape(b, sp, N_SSM_HEADS, SSM_HEAD_DIM)
    rep = N_SSM_HEADS // N_SSM_GROUPS
    bm = jnp.repeat(bm.reshape(b, sp, N_SSM_GROUPS, D_STATE), rep, axis=2)
    cm = jnp.repeat(cm.reshape(b, sp, N_SSM_GROUPS, D_STATE), rep, axis=2)
    dt = jax.nn.softplus(dt_raw.astype(jnp.float32) + dt_bias)
    a = -jnp.exp(a_log.astype(jnp.float32))
    y = ssd_scan(xs, dt, a, bm, cm) + xs.astype(jnp.float32) * d_skip[:, None]
    y = y.reshape(b, sp, D_SSM) * jax.nn.silu(z.astype(jnp.float32))
    yg = y.reshape(b, sp, N_SSM_GROUPS, D_SSM // N_SSM_GROUPS)
    yg = yg * lax.rsqrt(jnp.mean(jnp.square(yg), -1, keepdims=True) + RMS_EPS)
    y = yg.reshape(b, sp, D_SSM) * norm_w

    o = jnp.concatenate([o_attn, y.astype(o_attn.dtype)], axis=-1)[:, :s]
    return o @ w_out


def conv_ffn(h, w_up, conv_w, conv_b, w_down):
    u = causal_dwconv(h @ w_up, conv_w, conv_b)
    g, val = jnp.split(u, 2, axis=-1)
    return (jax.nn.silu(g) * val) @ w_down


def setup_inputs(seed: int = 0) -> dict:
    key = jax.random.key(seed)
    ks = jax.random.split(key, 24)
    f32 = jnp.float32
    nrm = lambda k, shape, std: jax.random.normal(k, shape, f32) * std
    dt0 = jnp.exp(jax.random.uniform(ks[8], (DEPTH, N_SSM_HEADS), f32, math.log(1e-3), math.log(1e-1)))
    return {
        "x": nrm(ks[0], (BATCH, SEQ, D_MODEL), 1.0),
        "c": nrm(ks[1], (BATCH, D_MODEL), 1.0),
        "ada_w": nrm(ks[2], (DEPTH, D_MODEL, 6 * D_MODEL), 0.1 * D_MODEL ** -0.5),
        "ada_b": nrm(ks[3], (DEPTH, 6 * D_MODEL), 0.01),
        "mix_in_w": nrm(ks[4], (DEPTH, D_MODEL, D_IN_PROJ), D_MODEL ** -0.5),
        "ssm_conv_w": nrm(ks[5], (DEPTH, SSM_CONV, D_XBC), SSM_CONV ** -0.5),
        "ssm_conv_b": nrm(ks[6], (DEPTH, D_XBC), 0.01),
        "ssm_dt_bias": dt0 + jnp.log(-jnp.expm1(-dt0)),
        "ssm_a_log": jnp.log(jax.random.uniform(ks[9], (DEPTH, N_SSM_HEADS), f32, 1.0, 16.0)),
        "ssm_d": 1.0 + nrm(ks[10], (DEPTH, N_SSM_HEADS), 0.1),
        "ssm_norm_w": 1.0 + nrm(ks[11], (DEPTH, D_SSM), 0.02),
        "mix_out_w": nrm(ks[12], (DEPTH, D_MIX, D_MODEL), DEEPNORM_BETA * math.sqrt(2.0 / (D_MIX + D_MODEL))),
        "ln1_g": 1.0 + nrm(ks[13], (DEPTH, D_MODEL), 0.02),
        "ln1_b": nrm(ks[14], (DEPTH, D_MODEL), 0.01),
        "ffn_up_w": nrm(ks[15], (DEPTH, D_MODEL, 2 * D_FF), D_MODEL ** -0.5),
        "ffn_conv_w": nrm(ks[16], (DEPTH, FFN_CONV, 2 * D_FF), FFN_CONV ** -0.5),
        "ffn_conv_b": nrm(ks[17], (DEPTH, 2 * D_FF), 0.01),
        "ffn_down_w": nrm(ks[18], (DEPTH, D_FF, D_MODEL), DEEPNORM_BETA * math.sqrt(2.0 / (D_FF + D_MODEL))),
        "ln2_g": 1.0 + nrm(ks[19], (DEPTH, D_MODEL), 0.02),
        "ln2_b": nrm(ks[20], (DEPTH, D_MODEL), 0.01),
    }


def reference(x, c, ada_w, ada_b, mix_in_w, ssm_conv_w, ssm_conv_b, ssm_dt_bias, ssm_a_log,
              ssm_d, ssm_norm_w, mix_out_w, ln1_g, ln1_b, ffn_up_w, ffn_conv_w, ffn_conv_b,
              ffn_down_w, ln2_g, ln2_b):
    c_act = jax.nn.silu(c)
    for i in range(DEPTH):
        mod = (c_act @ ada_w[i] + ada_b[i])[:, None, :]
        sh1, sc1, g1, sh2, sc2, g2 = jnp.split(mod, 6, axis=-1)
        h = x * (1.0 + sc1) + sh1
        y = hybrid_mixer(h, mix_in_w[i], ssm_conv_w[i], ssm_conv_b[i], ssm_dt_bias[i], ssm_a_log[i],
                         ssm_d[i], ssm_norm_w[i], mix_out_w[i])
        x = layer_norm(DEEPNORM_ALPHA * x + (1.0 + g1) * y, ln1_g[i], ln1_b[i])
        h = x * (1.0 + sc2) + sh2
        y = conv_ffn(h, ffn_up_w[i], ffn_conv_w[i], ffn_conv_b[i], ffn_down_w[i])
        x = layer_norm(DEEPNORM_ALPHA * x + (1.0 + g2) * y, ln2_g[i], ln2_b[i])
    return x
```

```python
import contextlib
import numpy as np
import ml_dtypes
import concourse.bass as bass
import concourse.mybir as mybir
from concourse.bass_utils import run_bass_kernel_spmd

F32 = mybir.dt.float32
BF16 = mybir.dt.bfloat16
AF = mybir.ActivationFunctionType
ALU = mybir.AluOpType
AX = mybir.AxisListType

S = 2048
D = 1024
NT = S // 128
DFF = 2816
NJ = DFF // 128
NEG = -30000.0
ALPHA = 2.0 ** 0.25
STRICT = {"act", "dve", "pool"}


class _Op:
    __slots__ = ("eng", "fn", "deps", "signal", "sem", "val", "dma", "idx", "prev_val")


class Sched:
    ENGS = ("pe", "act", "dve", "pool", "sp")

    def __init__(self):
        self.ops = []
        self.lw = {}
        self.rd = {}

    frozen = False

    def add(self, eng, fn, r=(), w=(), dma=False, force=False):
        if self.frozen and not force:
            return None
        groups = getattr(self, "groups", {})
        if groups:
            r = [m for k in r for m in groups.get(k, (k,))]
        op = _Op()
        op.eng, op.fn, op.dma, op.signal, op.idx = eng, fn, dma, dma, len(self.ops)
        deps = set()
        for k in r:
            x = self.lw.get(k)
            if x is not None:
                deps.add(x)
        for k in w:
            x = self.lw.get(k)
            if x is not None:
                deps.add(x)
            deps.update(self.rd.get(k, ()))
        for k in r:
            self.rd.setdefault(k, []).append(op.idx)
        for k in w:
            self.lw[k] = op.idx
            self.rd[k] = []
        op.deps = deps
        self.ops.append(op)
        return op

    def barrier(self):
        if self.frozen:
            return
        start = getattr(self, "_bar_from", 0)
        deps = set()
        last = {}
        for op in self.ops[start:]:
            if op.dma:
                deps.add(op.idx)
            last[op.eng] = op.idx
        deps.update(last.values())
        deps.update(getattr(self, "_bar_ops", ()))
        n0 = len(self.ops)
        bar_ops = []
        for eng in self.ENGS:
            op = self.add(eng, lambda e: e.wait_ge(self.esem["pe"], 0))
            op.deps = set(deps)
            bar_ops.append(op.idx)
        self._bar_from = n0
        self._bar_ops = bar_ops

    def emit(self, nc, stack, n_dma_sems=8):
        ops = self.ops
        for op in ops:
            for d in op.deps:
                dop = ops[d]
                if dop.eng != op.eng or dop.dma or dop.eng in STRICT:
                    dop.signal = True
        esem = {e: stack.enter_context(nc.semaphore("s_" + e)) for e in self.ENGS}
        self.esem = esem
        dsem = {e: [stack.enter_context(nc.semaphore("d_%s%d" % (e, i))) for i in range(n_dma_sems)]
                for e in ("sp", "pool", "act")}
        cnt = {e: 0 for e in self.ENGS}
        dcnt = {e: 0 for e in dsem}
        for op in ops:
            if not op.signal:
                continue
            if op.dma:
                k = dcnt[op.eng]
                dcnt[op.eng] += 1
                op.sem = dsem[op.eng][k % n_dma_sems]
                op.val = 16 * (k // n_dma_sems + 1)
            else:
                cnt[op.eng] += 1
                op.sem = esem[op.eng]
                op.val = cnt[op.eng]
        by_eng = {e: [op for op in ops if op.eng == e] for e in self.ENGS}

        def run(eng_name, e):
            waited = {}
            for op in by_eng[eng_name]:
                need = {}
                for d in op.deps:
                    dop = ops[d]
                    if not dop.signal:
                        continue
                    if dop.eng == op.eng and not dop.dma and dop.eng not in STRICT:
                        continue
                    key = id(dop.sem)
                    if key not in need or need[key][1] < dop.val:
                        need[key] = (dop.sem, dop.val)
                if op.dma and op.val > 16:
                    key = id(op.sem)
                    if key not in need or need[key][1] < op.val - 16:
                        need[key] = (op.sem, op.val - 16)
                for key, (sem, val) in need.items():
                    if waited.get(key, 0) >= val:
                        continue
                    e.wait_ge(sem, val)
                    waited[key] = val
                ins = op.fn(e)
                if op.signal:
                    ins.then_inc(op.sem, 16 if op.dma else 1)

        with nc.Block() as block:
            @block.tensor
            def _(e):
                run("pe", e)

            @block.scalar
            def _(e):
                run("act", e)

            @block.vector
            def _(e):
                run("dve", e)

            @block.gpsimd
            def _(e):
                run("pool", e)

            @block.sync
            def _(e):
                run("sp", e)


def _consts():
    c = {}
    c["identf"] = np.eye(128, dtype=np.float32)
    p = np.arange(128)
    tri = np.where(p[None, :] >= p[:, None], 0.0, NEG).astype(np.float32)
    c["negtri"] = tri
    c["negfull"] = np.full((128, 128), NEG, np.float32)
    qb = (np.arange(16) // 2)[:, None]
    kb = np.arange(8)[None, :]
    past = kb < qb
    c["pastneg"] = np.broadcast_to(np.where(past, 0.0, -1e30).astype(np.float32)[None], (128, 16, 8)).copy()
    base = np.where(kb == qb, 0.0, NEG).astype(np.float32)
    c["basetab"] = np.broadcast_to(base[None], (128, 16, 8)).copy()
    slopes = 2.0 ** (-(np.arange(8) + 1.0))
    A = np.zeros((8, 16, 8), np.float32)
    for h in range(8):
        A[h] = np.where(past, -slopes[h] * 256.0 * (qb - kb) - NEG, 0.0)
    c["atab"] = np.broadcast_to(A[None], (128, 8, 16, 8)).copy()
    t = np.arange(S)
    r = (t % 256).astype(np.float32)
    qst = np.zeros((2, 8, S), np.float32)
    kst = np.zeros((10, 8, S), np.float32)
    for h in range(8):
        qst[0, h] = 1.0
        qst[1, h] = -slopes[h] * r
        for b in range(8):
            kst[b, h] = (t // 256 == b)
        kst[8, h] = slopes[h] * r
        kst[9, h] = 1.0
    c["qstat"] = qst.astype(ml_dtypes.bfloat16)
    c["kstat"] = kst.astype(ml_dtypes.bfloat16)
    op = np.zeros((128, 2, 128), np.float32)
    op[:, 0, 0:64] = 1.0
    op[:, 1, 64:128] = 1.0
    c["onespad"] = op.astype(ml_dtypes.bfloat16)
    l = np.arange(256)
    tinc = np.zeros((128, 2, 256), np.float32)
    tinc[:, 0, :] = (p[:, None] <= l[None, :])
    tinc[:, 1, :] = (p[:, None] + 128 <= l[None, :])
    c["tinc"] = tinc
    c["onesf"] = np.ones((128, 128), np.float32)
    sel = np.zeros((64, 16), np.float32)
    for h in range(16):
        sel[h, h] = 1.0
        sel[32 + h, h] = 1.0
    c["sel"] = sel
    lb = np.zeros((64, 256), np.float32)
    lb[0:16] = 1.0
    c["lbase0"] = lb
    rb = np.zeros((64, 256), np.float32)
    rb[32:48] = 1.0
    c["rbase0"] = rb
    c["rbase0b"] = rb.astype(ml_dtypes.bfloat16)
    c["lbase0b"] = lb.astype(ml_dtypes.bfloat16)
    n384 = np.zeros((128, 384), np.float32)
    n384[:, 0:128] = tri
    n384[:, 256:384] = tri
    c["neg384"] = n384.astype(ml_dtypes.bfloat16)
    return c


_DT = {np.dtype(np.float32): F32, np.dtype(ml_dtypes.bfloat16): BF16}


def _shared_inputs(inp):
    g = lambda n: np.ascontiguousarray(inp[n][0], dtype=np.float32)
    sh = {}
    sh["ada_w"] = np.ascontiguousarray(g("ada_w").reshape(8, 128, 6144).transpose(1, 0, 2))
    sh["ada_b"] = g("ada_b").reshape(1, 6144)
    sh["ada_b_col"] = np.ascontiguousarray(g("ada_b").reshape(48, 128).T)
    w_in = g("mix_in_w").reshape(8, 128, 4112).transpose(1, 0, 2)
    sh["w_inA"] = np.ascontiguousarray(np.stack([np.concatenate([w_in[:, :, o + gI * 256:o + gI * 256 + 256] for o in (0, 512, 1024)], axis=2).reshape(128, 8 * 768)
                                                 for gI in range(2)], axis=0))
    sh["w_inBx"] = np.ascontiguousarray(w_in[:, :, 2560:4096].reshape(128, 8 * 1536))
    sh["w_inBz"] = np.ascontiguousarray(np.concatenate([w_in[:, :, 1536:2560], w_in[:, :, 4096:4112]], axis=2).reshape(128, 8 * 1040))
    sh["w_out"] = np.ascontiguousarray(g("mix_out_w").reshape(12, 128, 1024).transpose(1, 0, 2))
    wu = g("ffn_up_w").reshape(8, 128, 2, NJ, 128)
    sh["w_up"] = np.ascontiguousarray(wu.transpose(3, 1, 0, 2, 4).reshape(NJ, 128, 8, 256))
    sh["w_dn"] = np.ascontiguousarray(g("ffn_down_w").reshape(NJ, 128, 1024).transpose(1, 0, 2))
    fcw = g("ffn_conv_w").reshape(3, 2, NJ, 128)
    sh["fcw"] = np.ascontiguousarray(fcw.transpose(3, 2, 1, 0))
    sh["fcb"] = np.ascontiguousarray(g("ffn_conv_b").reshape(2, NJ, 128).transpose(2, 1, 0))
    sh["scw"] = np.ascontiguousarray(g("ssm_conv_w").reshape(4, 12, 128).transpose(2, 1, 0))
    sh["scb"] = np.ascontiguousarray(g("ssm_conv_b").reshape(12, 128).transpose(1, 0))
    for n in ("ln1_g", "ln1_b", "ln2_g", "ln2_b", "ssm_norm_w"):
        sh[n] = g(n).reshape(1, 1024)
    for n in ("ssm_dt_bias", "ssm_a_log", "ssm_d"):
        sh[n] = g(n).reshape(1, 16)
    return sh


def build(shapes, stop_after=None, dbg=False):
    nc = bass.Bass("TRN2", target_bir_lowering=False)
    sc = Sched()
    A = sc.add
    ES = contextlib.ExitStack
    din = {n: nc.dram_tensor(n, list(shp), dt, kind="ExternalInput").ap() for n, (shp, dt) in shapes.items()}
    out = nc.dram_tensor("out", [S, D], F32, kind="ExternalOutput").ap()
    x1s = nc.dram_tensor("x1s", [S, D], F32, kind="Internal").ap()
    zss = nc.dram_tensor("zss", [S, D], F32, kind="Internal").ap()
    dbgo = {}
    if dbg:
        for n, shp, dt_ in (("d_oTa", [128, 4, S], BF16), ("d_Vp", [128, 16, 4, 128], BF16), ("d_ones", [128, 2, 128], BF16), ("d_rec", [128, 512], F32), ("d_O", [128, 512], F32), ("d_hT", [128, 8, S], BF16), ("d_wA", [128, 8, 768], BF16), ("d_kf", [64, S], F32), ("d_QT", [128, 4, S], BF16), ("d_KT", [128, 4, S], BF16), ("d_ge", [128, 16, 8], F32), ("d_gm", [128, 128], F32), ("d_m8", [128, 16, 8], F32), ("d_xo", [128, 16, 1024], BF16), ("d_x1", [S, D], F32), ("d_mod", [128, 6144], F32)):
            dbgo[n] = nc.dram_tensor(n, shp, dt_, kind="ExternalOutput").ap()

    def ckpt(name):
        if stop_after == name:
            sc.frozen = True

    def T(st, name, shape, dt=F32):
        return st.enter_context(nc.sbuf_tensor("sb_" + name, list(shape), dt))

    def dma(eng, o, i, r=(), w=()):
        return A(eng, lambda e: e.dma_start(out=o, in_=i), r=r, w=w, dma=True)

    def bload(tile, src, width, w):
        keys = []
        for c0 in range(0, width, 512):
            c1 = min(width, c0 + 512)
            key = "%s#%d" % (w[0], c0 // 512)
            dma("sp", tile[:, c0:c1], src[:, c0:c1].partition_broadcast(128), w=[key])
            keys.append(key)
        if not hasattr(sc, "groups"):
            sc.groups = {}
        sc.groups[w[0]] = keys

    def cast_load(dst_flat, src_flat, n, w):
        keys = []
        for c0 in range(0, n, 2048):
            c1 = min(n, c0 + 2048)
            key = "%s#%d" % (w[0], c0 // 2048)
            dma("pool", dst_flat[:, c0:c1], src_flat[:, c0:c1], w=[key])
            keys.append(key)
        if not hasattr(sc, "groups"):
            sc.groups = {}
        sc.groups[w[0]] = keys

    def cp(eng, o, i, r=(), w=()):
        if eng == "act":
            return A("act", lambda e: e.activation(out=o, in_=i, func=AF.Copy), r=r, w=w)
        return A(eng, lambda e: e.tensor_copy(out=o, in_=i), r=r, w=w)

    def mm(o, l, rr, start, stop, r=(), w=()):
        return A("pe", lambda e: e.matmul(o, l, rr, start=start, stop=stop), r=r, w=w)

    def tt(eng, o, a, b, op, r=(), w=()):
        return A(eng, lambda e: e.tensor_tensor(out=o, in0=a, in1=b, op=op), r=r, w=w)

    def ts(eng, o, a, s1, s2, op0, op1=None, r=(), w=()):
        if op1 is None:
            return A(eng, lambda e: e.tensor_scalar(out=o, in0=a, scalar1=s1, scalar2=None, op0=op0), r=r, w=w)
        return A(eng, lambda e: e.tensor_scalar(out=o, in0=a, scalar1=s1, scalar2=s2, op0=op0, op1=op1), r=r, w=w)

    def stt(eng, o, a, s, b, op0, op1, r=(), w=()):
        return A(eng, lambda e: e.scalar_tensor_tensor(out=o, in0=a, scalar=s, in1=b, op0=op0, op1=op1), r=r, w=w)

    def act(o, i, func, r=(), w=(), **kw):
        return A("act", lambda e: e.activation(out=o, in_=i, func=func, **kw), r=r, w=w)

    with ES() as top:
        pb = [top.enter_context(nc.psum_tensor("pb%d" % i, [128, 512], F32)) for i in range(7)]
        pT = top.enter_context(nc.psum_tensor("pT", [128, 8, 128], BF16))
        g2p1 = T(top, "g2p1", [128, 1024])
        identf = T(top, "identf", [128, 128]); identb = T(top, "identb", [128, 128], BF16)
        epsln = T(top, "epsln", [128, 1])
        dma("sp", identf[:], din["identf"], w=["identf"])
        cp("dve", identb[:], identf[:], r=["identf"], w=["identb"])
        A("dve", lambda e: e.memset(epsln[:], 1e-5), w=["eps"])
        hT = T(top, "hT", [128, 8, S], BF16)
        mix2 = top.enter_context(ES())
        oTa = T(mix2, "oTa", [128, 4, S], BF16)
        modB = T(mix2, "modB", [128, 3072])

        with ES() as ph:
            aw = [T(ph, "aw%d" % i, [128, 8, 512]) for i in range(2)]
            ccol = T(ph, "ccol", [128, 8]); cact = T(ph, "cact", [128, 8])
            colall = T(ph, "colall", [128, 48]); abcol = T(ph, "abcol", [128, 48])
            ones1 = T(ph, "ones1", [128, 128]); dg = [T(ph, "dg%d" % i, [128, 128]) for i in range(4)]
            xt = [T(ph, "xt%d" % i, [128, S]) for i in range(4)]
            dma("sp", ccol[:], din["c_col"], w=["ccol"])
            dma("sp", abcol[:], din["ada_b_col"], w=["abcol"])
            A("dve", lambda e: e.memset(ones1[:], 1.0), w=["ones1"])
            act(cact[:], ccol[:], AF.Silu, r=["ccol"], w=["cact"])
            awb = [T(ph, "awb%d" % i, [128, 8, 512], BF16) for i in range(2)]
            cactb = T(ph, "cactb", [128, 8], BF16)
            cp("dve", cactb[:], cact[:], r=["cact"], w=["cactb"])
            for k in range(4):
                dma("pool", xt[k][:], din["xT"][k * 128:(k + 1) * 128, :], w=["xt%d" % k])
            for n in range(12):
                a_ = aw[n % 2]; ak = "aw%d" % (n % 2); b_ = awb[n % 2]; bk = "awb%d" % (n % 2)
                dma("sp", a_[:], din["ada_w"][:, :, n * 512:(n + 1) * 512], w=[ak])
                cp("act", b_[:, 0:4, :], a_[:, 0:4, :], r=[ak], w=[bk + "a"])
                cp("dve", b_[:, 4:8, :], a_[:, 4:8, :], r=[ak], w=[bk + "b"])
                for fcl in range(4):
                    for kk in range(8):
                        mm(pb[2][:, n * 4 + fcl:n * 4 + fcl + 1], b_[:, kk, fcl * 128:(fcl + 1) * 128], cactb[:, kk:kk + 1], kk == 0, kk == 7,
                           r=[bk + "a", bk + "b", "cactb"], w=["pb2"])
                if n == 3:
                    tt("dve", colall[:, 0:16], pb[2][:, 0:16], abcol[:, 0:16], ALU.add, r=["pb2", "abcol"], w=["colA"])
                    ts("dve", colall[:, 8:16], colall[:, 8:16], 1.0, None, ALU.add, r=["colA"], w=["colA"])
            tt("dve", colall[:, 16:48], pb[2][:, 16:48], abcol[:, 16:48], ALU.add, r=["pb2", "abcol"], w=["colB"])
            ts("dve", colall[:, 16:24], colall[:, 16:24], 1.0, None, ALU.add, r=["colB"], w=["colB"])
            ts("dve", colall[:, 32:48], colall[:, 32:48], 1.0, None, ALU.add, r=["colB"], w=["colB"])
            colsh = colall[:, 0:8]; colsc = colall[:, 8:16]
            ckpt("s1a")
            for q4 in range(8):
                p_ = pb[q4 % 2]; pk = "pb%d" % (q4 % 2)
                for i in range(4):
                    c_ = 16 + q4 * 4 + i
                    d_ = dg[i]; dk_ = "dg%d" % i
                    ts("dve", d_[:], identf[:], colall[:, c_:c_ + 1], None, ALU.mult, r=["identf", "colB"], w=[dk_])
                    mm(p_[:, i * 128:(i + 1) * 128], ones1[:], d_[:], True, True, r=["ones1", dk_], w=[pk])
                if q4 < 6:
                    cp("act", modB[:, q4 * 512:(q4 + 1) * 512], p_[:], r=[pk], w=["modB"])
                else:
                    cp("act", g2p1[:, (q4 - 6) * 512:(q4 - 5) * 512], p_[:], r=[pk], w=["g2p1"])
            ckpt("s1b")
            for k in range(8):
                x_ = xt[k % 4]; xk = "xt%d" % (k % 4)
                if k >= 4:
                    dma("pool", x_[:], din["xT"][k * 128:(k + 1) * 128, :], w=[xk])
                ts("dve", hT[:, k, :], x_[:], colsc[:, k:k + 1], colsh[:, k:k + 1], ALU.mult, ALU.add, r=[xk, "colA"], w=["hT"])

        sc.barrier()
        ckpt("s12")
        with ES() as ph:
            wA = T(ph, "wA", [128, 8, 768], BF16)
            QT = T(ph, "QT", [128, 4, S], BF16); KT = T(ph, "KT", [128, 4, S], BF16)
            Vp = T(ph, "Vp", [128, 16, 4, 128], BF16)
            kf = T(ph, "kf", [64, S]); qf = T(ph, "qf", [64, S]); km = T(ph, "km", [64, 8])
            kfs = [kf, T(ph, "kfB", [64, S])]; qfs = [qf, T(ph, "qfB", [64, S])]; kms = [km, T(ph, "kmB", [64, 8])]
            gm = T(ph, "gm", [128, 128]); m8 = T(ph, "m8", [128, 16, 8]); ge = T(ph, "ge", [128, 16, 8])
            mpad = T(ph, "mpad", [128, 16, 72], BF16)
            pastneg = T(ph, "pastneg", [128, 128]); basetab = T(ph, "basetab", [128, 16, 8]); atab = T(ph, "atab", [128, 8, 16, 8])
            negtri = T(ph, "negtri", [128, 128]); negfull = T(ph, "negfull", [128, 128]); onesp = T(ph, "onesp", [128, 2, 128], BF16)
            PT = [T(ph, "PT%d" % i, [128, 512], BF16) for i in range(6)]
            rec = T(ph, "rec", [128, 512])
            dma("sp", pastneg[:], din["pastneg"].rearrange("p a b -> p (a b)"), w=["pastneg"])
            dma("sp", basetab[:], din["basetab"], w=["basetab"]); dma("sp", atab[:], din["atab"], w=["atab"])
            dma("sp", negtri[:], din["negtri"], w=["negtri"]); dma("sp", negfull[:], din["negfull"], w=["negfull"])
            dma("sp", onesp[:], din["onespad"], w=["onesp"])
            zf = T(ph, "zf", [128, 1152])
            A("dve", lambda e: e.memset(zf[:], 0.0), w=["zf"])
            for t_ in range(16):
                cp("dve", Vp[:, t_], zf[:, 0:512].rearrange("p (a b) -> p a b", b=128), r=["zf"], w=["Vp"])
            cp("dve", mpad[:], zf[:].rearrange("p (a b) -> p a b", b=72), r=["zf"], w=["mpad"])
            sring = 0
            pring = 0
            for gI in range(2):
                cast_load(wA[:].rearrange("p k c -> p (k c)"), din["w_inA"][gI], 8 * 768, ["wA"])
                dma("sp", QT[72:74, :, :], din["qstat"][:, 4 * gI:4 * gI + 4, :], w=["QT"])
                dma("sp", KT[64:74, :, :], din["kstat"][:, 4 * gI:4 * gI + 4, :], w=["KT"])
                for t_ in range(16):
                    p_ = pb[t_ % 2]; pk = "pb%d" % (t_ % 2)
                    for k in range(8):
                        mm(p_[:, 0:256], hT[:, k, t_ * 128:(t_ + 1) * 128], wA[:, k, 512:768], k == 0, k == 7, r=["hT", "wA"], w=[pk])
                    pv = p_[:, 0:256].rearrange("p (a t d) -> p a t d", t=2, d=64)
                    vv = Vp[:, t_].rearrange("p (a t) c -> p a t c", t=2)
                    cp("dve", vv[:, :, 0, 0:64], pv[:, :, 0, :], r=[pk], w=["Vp"])
                    cp("dve", vv[:, :, 1, 64:128], pv[:, :, 1, :], r=[pk], w=["Vp"])
                ckpt("a1")
                def prepA(hl):
                    kf_ = kfs[hl % 2]; qf_ = qfs[hl % 2]; km_ = kms[hl % 2]
                    kfk = "kf%d" % (hl % 2); qfk = "qf%d" % (hl % 2); kmk = "km%d" % (hl % 2)
                    for tb in range(4):
                        p_ = pb[tb % 2]; pk = "pb%d" % (tb % 2)
                        for k in range(8):
                            mm(p_[:], wA[:, k, 256 + hl * 64:256 + hl * 64 + 128], hT[:, k, tb * 512:(tb + 1) * 512], k == 0, k == 7, r=["hT", "wA"], w=[pk])
                        cp("act", kf_[:, tb * 512:(tb + 1) * 512], p_[0:64, :], r=[pk], w=[kfk])
                        cp("dve", KT[0:64, hl, tb * 512:(tb + 1) * 512], kf_[:, tb * 512:(tb + 1) * 512], r=[kfk], w=["KT"])
                    A("dve", lambda e, kf_=kf_, km_=km_: e.tensor_reduce(out=km_[:], in_=kf_[:].rearrange("p (b t) -> p b t", t=256), axis=AX.X, op=ALU.add), r=[kfk], w=[kmk])
                    ts("dve", km_[:], km_[:], 1.0 / 256.0, None, ALU.mult, r=[kmk], w=[kmk])
                    for tb in range(4):
                        p_ = pb[tb % 2]; pk = "pb%d" % (tb % 2)
                        for k in range(8):
                            mm(p_[:], wA[:, k, hl * 64:hl * 64 + 128], hT[:, k, tb * 512:(tb + 1) * 512], k == 0, k == 7, r=["hT", "wA"], w=[pk])
                        cp("act", qf_[:, tb * 512:(tb + 1) * 512], p_[0:64, :], r=[pk], w=[qfk])
                        ts("dve", QT[0:64, hl, tb * 512:(tb + 1) * 512], qf_[:, tb * 512:(tb + 1) * 512], 0.125, None, ALU.mult, r=[qfk], w=["QT"])

                def prepB(hl):
                    h = 4 * gI + hl
                    qf_ = qfs[hl % 2]; km_ = kms[hl % 2]; qfk = "qf%d" % (hl % 2); kmk = "km%d" % (hl % 2)
                    for t_ in range(16):
                        mm(pb[2][:, t_ * 8:(t_ + 1) * 8], qf_[:, t_ * 128:(t_ + 1) * 128], km_[:], True, True, r=[qfk, kmk], w=["pb2"])
                    tt("dve", gm[:], pb[2][:, 0:128], pastneg[:], ALU.add, r=["pb2", "pastneg"], w=["gm"])
                    for t_ in range(16):
                        A("dve", lambda e, t_=t_: e.max(out=m8[:, t_, :], in_=gm[:, t_ * 8:(t_ + 1) * 8]), r=["gm"], w=["m8"])
                    tt("dve", ge[:], gm[:].rearrange("p (a b) -> p a b", b=8), m8[:, :, 2:3].to_broadcast([128, 16, 8]), ALU.is_ge, r=["gm", "m8"], w=["ge"])
                    tt("dve", ge[:], ge[:], atab[:, h], ALU.mult, r=["ge", "atab"], w=["ge"])
                    tt("dve", mpad[:, :, 64:72], ge[:], basetab[:], ALU.add, r=["ge", "basetab"], w=["mpad"])

                def prepC(hl):
                    for r4 in range(4):
                        for i in range(4):
                            mm(pb[3][0:72, i * 128:(i + 1) * 128], mpad[:, r4 * 4 + i, :], identb[:], True, True, r=["mpad", "identb"], w=["pb3"])
                        cp("act", QT[64:72, hl, r4 * 512:(r4 + 1) * 512], pb[3][64:72, :], r=["pb3"], w=["QT"])

                prepA(0)
                for hl in range(4):
                    prepB(hl)
                    if hl + 1 < 4:
                        prepA(hl + 1)
                    prepC(hl)
                if dbg and "prep" in dbg and gI == 0:
                    dma("sp", dbgo["d_QT"], QT[:], r=["QT"], w=["d_QT"])
                    dma("sp", dbgo["d_hT"], hT[:], r=["hT"], w=["d_hT"])
                    dma("sp", dbgo["d_wA"], wA[:], r=["wA"], w=["d_wA"])
                    dma("sp", dbgo["d_kf"], kf[:], r=["kf"], w=["d_kf"])
                    dma("sp", dbgo["d_KT"], KT[:], r=["KT"], w=["d_KT"])
                    dma("sp", dbgo["d_ge"], ge[:], r=["ge"], w=["d_ge"])
                    dma("sp", dbgo["d_gm"], gm[:], r=["gm"], w=["d_gm"])
                    dma("sp", dbgo["d_m8"], m8[:], r=["m8"], w=["d_m8"])
                ckpt("a3")
                for pr in range(2):
                    for c in range(4):
                        ntk = 4 * c + 4
                        cs_ = slice(c * 512, (c + 1) * 512)
                        n_mm = 2 * ntk
                        cnt = 0
                        ob_, db_ = ((3, 2), (1, 0))[(pr * 4 + c) % 2]
                        steps = [(eo, 2 * pr + eo, i) for eo in range(2) for i in range(ntk)]
                        SK = 3
                        pts = {}
                        for s_ in range(n_mm + SK):
                            if s_ < n_mm:
                                eo, hl, i = steps[s_]
                                bi_ = 4 + sring % 3; sb = pb[bi_]; sk = "pb%d" % bi_; sring += 1
                                mm(sb[:], KT[0:74, hl, i * 128:(i + 1) * 128], QT[0:74, hl, cs_], True, True, r=["KT", "QT"], w=[sk])
                                if i >= 4 * c:
                                    j0 = i - 4 * c
                                    tt("dve", sb[:, j0 * 128:(j0 + 1) * 128], sb[:, j0 * 128:(j0 + 1) * 128], negtri[:], ALU.add, r=[sk, "negtri"], w=[sk])
                                    if i % 2 == 1:
                                        tt("dve", sb[:, (j0 - 1) * 128:j0 * 128], sb[:, (j0 - 1) * 128:j0 * 128], negfull[:], ALU.add, r=[sk, "negfull"], w=[sk])
                                pt = PT[pring % 6]; ptk = "PT%d" % (pring % 6); pring += 1
                                act(pt[:], sb[:], AF.Exp, r=[sk], w=[ptk])
                                pts[s_] = (pt, ptk)
                            if s_ >= SK:
                                eo, hl, i = steps[s_ - SK]
                                pt, ptk = pts.pop(s_ - SK)
                                mm(pb[ob_][:], Vp[:, i, hl, :], pt[:], cnt == 0, cnt == n_mm - 1, r=["Vp", ptk], w=["pb%d" % ob_])
                                mm(pb[db_][:], onesp[:, eo, :], pt[:], cnt == 0, cnt == n_mm - 1, r=["onesp", ptk], w=["pb%d" % db_])
                                cnt += 1
                        A("dve", lambda e, db_=db_: e.reciprocal(out=rec[:], in_=pb[db_][:]), r=["pb%d" % db_], w=["rec"])
                        if dbg and "last" in dbg and gI == 1 and pr == 1 and c == 3:
                            dbt = T(ph, "dbt", [128, 512])
                            cp("dve", dbt[:], pb[3][:], r=["pb3"], w=["dbt"])
                            dma("sp", dbgo["d_O"], dbt[:], r=["dbt"], w=["d_O"])
                            dma("sp", dbgo["d_rec"], rec[:], r=["rec"], w=["d_rec"])
                            dma("sp", dbgo["d_Vp"], Vp[:], r=["Vp"], w=["d_Vp"])
                            dma("sp", dbgo["d_ones"], onesp[:], r=["onesp"], w=["d_ones"])
                        tt("dve", oTa[:, 2 * gI + pr, cs_], pb[ob_][:], rec[:], ALU.mult, r=["pb%d" % ob_, "rec"], w=["oTa"])

        sc.barrier()
        if dbg and "oTa" in dbg:
            dma("sp", dbgo["d_oTa"], oTa[:], r=["oTa"], w=["d_oTa"])
        ckpt("attn")
        xo = T(mix2, "xo", [128, 16, 1024], BF16)
        Btok = T(mix2, "Btok", [128, 16, 256], BF16)
        BT = T(mix2, "BT", [128, 2, S], BF16); CT = T(mix2, "CT", [128, 2, S], BF16)
        dtt = T(mix2, "dtt", [128, 16, 16]); dAx = T(mix2, "dAx", [128, 16, 64])
        onec = T(mix2, "onec", [128, 1])
        A("dve", lambda e: e.memset(onec[:], 1.0), w=["onec"])
        with ES() as ph:
            wB = T(ph, "wB", [128, 8, 1536], BF16)
            scw = T(ph, "scw", [128, 12, 4]); scb = T(ph, "scb", [128, 12])
            xpre = T(ph, "xpre", [128, 3 + S]); xcv = T(ph, "xcv", [128, S]); xact = T(ph, "xact", [128, S], BF16)
            zt = [T(ph, "zt%d" % i, [128, 512]) for i in range(2)]
            dtb = T(ph, "dtb", [128, 16]); ab = T(ph, "ab", [128, 16])
            cast_load(wB[:].rearrange("p k c -> p (k c)"), din["w_inBx"], 8 * 1536, ["wB"])
            wBz_t = T(ph, "wBz", [128, 8, 1040], BF16)
            wBz = wBz_t[:]
            cast_load(wBz_t[:].rearrange("p k c -> p (k c)"), din["w_inBz"], 8 * 1040, ["wBz"])
            dma("sp", scw[:], din["scw"], w=["scw"]); dma("sp", scb[:], din["scb"], w=["scb"])
            dma("sp", dtb[:], din["ssm_dt_bias"].partition_broadcast(128), w=["dtb"])
            dma("sp", ab[:], din["ssm_a_log"].partition_broadcast(128), w=["ab"])
            act(ab[:], ab[:], AF.Exp, r=["ab"], w=["ab"])
            ts("dve", ab[:], ab[:], -1.0, None, ALU.mult, r=["ab"], w=["ab"])
            xpres = [xpre, T(ph, "xpreB", [128, 3 + S])]
            for i_ in range(2):
                A("dve", lambda e, i_=i_: e.memset(xpres[i_][:, 0:3], 0.0), w=["xpre%d" % i_])

            def projA(fc):
                xp = xpres[fc % 2]; xk_ = "xpre%d" % (fc % 2)
                for tb in range(4):
                    p_ = pb[tb % 2]; pk = "pb%d" % (tb % 2)
                    for k in range(8):
                        mm(p_[:], wB[:, k, fc * 128:(fc + 1) * 128], hT[:, k, tb * 512:(tb + 1) * 512], k == 0, k == 7, r=["wB", "hT"], w=[pk])
                    cp("act", xp[:, 3 + tb * 512:3 + (tb + 1) * 512], p_[:], r=[pk], w=[xk_])

            def projB(fc):
                xp = xpres[fc % 2]; xk_ = "xpre%d" % (fc % 2)
                ts("dve", xcv[:], xp[:, 3:3 + S], scw[:, fc, 3:4], scb[:, fc:fc + 1], ALU.mult, ALU.add, r=[xk_, "scw", "scb"], w=["xcv"])
                for j in (2, 1, 0):
                    stt("dve", xcv[:], xp[:, j:j + S], scw[:, fc, j:j + 1], xcv[:], ALU.mult, ALU.add, r=[xk_, "xcv", "scw"], w=["xcv"])
                if fc < 8:
                    dst, dk = xact[:], "xact"
                elif fc < 10:
                    dst, dk = BT[:, fc - 8, :], "BT"
                else:
                    dst, dk = CT[:, fc - 10, :], "CT"
                act(dst, xcv[:], AF.Silu, r=["xcv"], w=[dk])
                if fc < 10:
                    for half in range(2):
                        for i in range(8):
                            t_ = half * 8 + i
                            sl = slice(t_ * 128, (t_ + 1) * 128)
                            srcap = xact[:, sl] if fc < 8 else BT[:, fc - 8, sl]
                            A("pe", lambda e, i=i, srcap=srcap: e.transpose(pT[:, i, :], srcap, identb[:]), r=[dk, "identb"], w=["pT"])
                        if fc < 8:
                            cp("act", xo[:, half * 8:(half + 1) * 8, fc * 128:(fc + 1) * 128], pT[:], r=["pT"], w=["xo%d" % t for t in range(half * 8, half * 8 + 8)])
                        else:
                            cp("act", Btok[:, half * 8:(half + 1) * 8, (fc - 8) * 128:(fc - 7) * 128], pT[:], r=["pT"], w=["Btok"])

            projA(0)
            for fc in range(12):
                if fc + 1 < 12:
                    projA(fc + 1)
                projB(fc)
            for t_ in range(16):
                for half in range(2):
                    p_ = pb[2 + half]; pk = "pb%d" % (2 + half)
                    for k in range(8):
                        mm(p_[:], hT[:, k, t_ * 128:(t_ + 1) * 128], wBz[:, k, half * 512:(half + 1) * 512], k == 0, k == 7, r=["wBz", "hT"], w=[pk])
                    act(zt[half][:], p_[:], AF.Silu, r=[pk], w=["zt%d" % half])
                    dma("sp", zss[t_ * 128:(t_ + 1) * 128, half * 512:(half + 1) * 512], zt[half][:], r=["zt%d" % half], w=["zss%d" % t_])
            for t_ in range(16):
                for k in range(8):
                    mm(pb[4][:, t_ * 16:(t_ + 1) * 16], hT[:, k, t_ * 128:(t_ + 1) * 128], wBz[:, k, 1024:1040], k == 0, k == 7, r=["wBz", "hT"], w=["pb4"])
            tt("dve", dtt[:], pb[4][:, 0:256].rearrange("p (a b) -> p a b", b=16), dtb[:].unsqueeze(1).to_broadcast([128, 16, 16]), ALU.add, r=["pb4", "dtb"], w=["dtt"])
            act(dtt[:], dtt[:], AF.Exp, r=["dtt"], w=["dtt"])
            act(dtt[:], dtt[:], AF.Ln, r=["dtt", "onec"], w=["dtt"], bias=onec[:, 0:1], scale=1.0)
            A("dve", lambda e: e.memset(dAx[:], 0.0), w=["dAx"])
            tt("dve", dAx[:, :, 0:16], dtt[:], ab[:].unsqueeze(1).to_broadcast([128, 16, 16]), ALU.mult, r=["dtt", "ab", "dAx"], w=["dAx"])
            ts("dve", dAx[:, :, 32:48], dAx[:, :, 0:16], -1.0, None, ALU.mult, r=["dAx"], w=["dAx"])

        sc.barrier()
        ckpt("ssmin")
        with ES() as ph:
            tinc = T(ph, "tinc", [128, 2, 256]); onesf = T(ph, "onesf", [128, 128]); sel = T(ph, "sel", [64, 16])
            HI = T(ph, "HI", [64, 256], BF16); LO = T(ph, "LO", [64, 256], BF16)
            R1 = T(ph, "R1", [64, 256], BF16); R2 = T(ph, "R2", [64, 256], BF16); L1 = T(ph, "L1", [64, 256], BF16); L2 = T(ph, "L2", [64, 256], BF16)
            Lh1 = [T(ph, "Lh1_%d" % i, [64, 256], BF16) for i in range(3)]; Lh2 = [T(ph, "Lh2_%d" % i, [64, 256], BF16) for i in range(3)]
            cst = T(ph, "cst", [128, 2, 16]); ecs = T(ph, "ecs", [128, 2, 16]); dec = T(ph, "dec", [128, 2, 16])
            ctb = T(ph, "ctb", [128, 16]); cdb = T(ph, "cdb", [128, 16])
            GT = T(ph, "GT", [128, 2, 384]); xdt = T(ph, "xdt", [128, 2, 1024], BF16); xdd = T(ph, "xdd", [128, 2, 1024], BF16)
            prevT = T(ph, "prevT", [128, 1024]); prevb = T(ph, "prevb", [128, 1024], BF16)
            neg384 = T(ph, "neg384", [128, 384], BF16)
            LT = [T(ph, "LT%d" % i, [128, 384]) for i in range(2)]; MT = [T(ph, "MT%d" % i, [128, 384], BF16) for i in range(3)]
            yt = [T(ph, "yt%d" % i, [128, 1024]) for i in range(4)]; ytmp = T(ph, "ytmp", [128, 1024])
            zsb = [T(ph, "zsb%d" % i, [128, 1024]) for i in range(2)]; ynb = T(ph, "ynb", [128, 1024], BF16)
            nwb = T(ph, "nwb", [128, 1024]); Db = T(ph, "Db", [128, 16])
            st6 = T(ph, "st6", [128, 2, 6]); mv = T(ph, "mv", [128, 2]); ms = T(ph, "ms", [128, 1]); rs = T(ph, "rs", [128, 1])
            for tl_, nm, key in ((tinc, "tinc", "tinc"), (onesf, "onesf", "onesf"), (sel, "sel", "sel"), (R1, "rbase0b", "R1"), (R2, "rbase0b", "R2"),
                                 (L1, "lbase0b", "L1"), (L2, "lbase0b", "L2"), (neg384, "neg384", "neg384")):
                dma("sp", tl_[:], din[nm], w=[key])
            bload(nwb, din["ssm_norm_w"], 1024, ["nwb"])
            dma("sp", Db[:], din["ssm_d"].partition_broadcast(128), w=["Db"])
            A("dve", lambda e: e.memset(prevT[:], 0.0), w=["prevT"])
            GTs = [GT, T(ph, "GTB", [128, 2, 384])]; xdts = [xdt, T(ph, "xdtB", [128, 2, 1024], BF16)]
            R1s = [R1, T(ph, "R1B", [64, 256], BF16)]; R2s = [R2, T(ph, "R2B", [64, 256], BF16)]
            L1s = [L1, T(ph, "L1B", [64, 256], BF16)]; L2s = [L2, T(ph, "L2B", [64, 256], BF16)]
            for tl_, nm, key in ((R1s[1], "rbase0b", "R1_1"), (R2s[1], "rbase0b", "R2_1"), (L1s[1], "lbase0b", "L1_1"), (L2s[1], "lbase0b", "L2_1")):
                dma("sp", tl_[:], din[nm], w=[key])

            def prologue(c):
                par = c % 2; P_ = "_%d" % par if par else ""
                t0, t1 = 2 * c, 2 * c + 1
                tl2 = (t0, t1)
                csl = slice(c * 256, (c + 1) * 256)
                yb = par * 2
                GT_, xdt_, R1_, R2_, L1_, L2_ = GTs[par], xdts[par], R1s[par], R2s[par], L1s[par], L2s[par]
                kR1, kR2, kL1, kL2, kGT, kxdt = ("R1" + P_, "R2" + P_, "L1" + P_, "L2" + P_, "GT%d" % par, "xdt%d" % par)
                mm(pb[0][0:64, 0:256], dAx[:, t0, :], tinc[:, 0, :], True, False, r=["dAx", "tinc", "onesf", "sel", "neg384"], w=["pb0"])
                mm(pb[0][0:64, 0:256], dAx[:, t1, :], tinc[:, 1, :], False, True, r=["dAx", "tinc", "onesf", "sel", "neg384"], w=["pb0"])
                mm(pb[0][:, 256:272], tinc[:, 0, 0:128], dAx[:, t0, 0:16], True, True, r=["dAx"], w=["pb0"])
                mm(pb[0][:, 272:288], tinc[:, 0, 128:256], dAx[:, t0, 0:16], True, False, r=["dAx"], w=["pb0"])
                mm(pb[0][:, 272:288], tinc[:, 1, 128:256], dAx[:, t1, 0:16], False, True, r=["dAx"], w=["pb0"])
                mm(pb[0][:, 288:304], onesf[:], dAx[:, t0, 0:16], True, False, r=["dAx"], w=["pb0"])
                mm(pb[0][:, 288:304], onesf[:], dAx[:, t1, 0:16], False, True, r=["dAx"], w=["pb0"])
                cp("act", HI[:], pb[0][0:64, 0:256], r=["pb0"], w=["HI"])
                cp("act", cst[:].rearrange("p a b -> p (a b)"), pb[0][:, 256:288], r=["pb0"], w=["cst"])
                cp("act", ctb[:], pb[0][:, 288:304], r=["pb0"], w=["ctb"])
                tt("dve", LO[:], pb[0][0:64, 0:256], HI[:], ALU.subtract, r=["pb0", "HI", "cst", "ctb"], w=["LO"])
                cp("act", R1_[0:16, :], HI[0:16, :], r=["HI", "tinc", "onesf", "sel", "neg384"], w=[kR1])
                cp("act", L1_[32:48, :], HI[32:48, :], r=["HI", "tinc", "onesf", "sel", "neg384"], w=[kL1])
                cp("act", R2_[0:16, :], LO[0:16, :], r=["LO", "tinc", "onesf", "sel", "neg384"], w=[kR2])
                cp("act", L2_[32:48, :], LO[32:48, :], r=["LO", "tinc", "onesf", "sel", "neg384"], w=[kL2])
                act(ecs[:], cst[:], AF.Exp, r=["cst"], w=["ecs"])
                tt("dve", dec[:], ctb[:].unsqueeze(1).to_broadcast([128, 2, 16]), cst[:], ALU.subtract, r=["ctb", "cst"], w=["dec"])
                act(dec[:], dec[:], AF.Exp, r=["dec"], w=["dec"])
                act(cdb[:], ctb[:], AF.Exp, r=["ctb"], w=["cdb"])
                for g in range(2):
                    mm(pb[1][:, 0:256], BT[:, g, t0 * 128:(t0 + 1) * 128], CT[:, g, csl], True, True, r=["BT", "CT"], w=["pb1"])
                    mm(pb[1][:, 256:384], BT[:, g, t1 * 128:(t1 + 1) * 128], CT[:, g, t1 * 128:(t1 + 1) * 128], True, True, r=["BT", "CT"], w=["pb1"])
                    cp("act", GT_[:, g, :], pb[1][:, 0:384], r=["pb1"], w=[kGT])
                for i in range(2):
                    xv = xo[:, tl2[i], :].rearrange("p (h d) -> p h d", d=64)
                    tt("pool", xdt_[:, i, :].rearrange("p (h d) -> p h d", d=64), xv, dtt[:, tl2[i], :].unsqueeze(2).to_broadcast([128, 16, 64]), ALU.mult, r=["xo%d" % tl2[i], "dtt"], w=[kxdt])
                    tt("pool", xdd[:, i, :].rearrange("p (h d) -> p h d", d=64), xdt_[:, i, :].rearrange("p (h d) -> p h d", d=64), dec[:, i, :].unsqueeze(2).to_broadcast([128, 16, 64]), ALU.mult, r=[kxdt, "dec"], w=["xdd"])
                for i in range(2):
                    if c == 0:
                        A("dve", lambda e, i=i, yb=yb: e.memset(yt[yb + i][:], 0.0), w=["yt%d" % (yb + i)])
                        continue
                    for g in range(2):
                        bq = 2 - g; bqk = "pb%d" % bq
                        mm(pb[bq][:], CT[:, g, tl2[i] * 128:(tl2[i] + 1) * 128], prevb[:, g * 512:(g + 1) * 512], True, True, r=["CT", "prevb"], w=[bqk])
                        tt("dve", yt[yb + i][:, g * 512:(g + 1) * 512].rearrange("p (h d) -> p h d", d=64), pb[bq][:].rearrange("p (h d) -> p h d", d=64),
                           ecs[:, i, g * 8:(g + 1) * 8].unsqueeze(2).to_broadcast([128, 8, 64]), ALU.mult, r=[bqk, "ecs"], w=["yt%d" % (yb + i)])
                if c < 7:
                    for g in range(2):
                        gs = slice(g * 512, (g + 1) * 512)
                        bq = 2 - g; bqk = "pb%d" % bq
                        mm(pb[bq][:], Btok[:, t0, g * 128:(g + 1) * 128], xdd[:, 0, gs], True, False, r=["Btok", "xdd"], w=[bqk])
                        mm(pb[bq][:], Btok[:, t1, g * 128:(g + 1) * 128], xdd[:, 1, gs], False, True, r=["Btok", "xdd"], w=[bqk])
                        tt("dve", prevT[:, gs].rearrange("p (h d) -> p h d", d=64), prevT[:, gs].rearrange("p (h d) -> p h d", d=64),
                           cdb[:, g * 8:(g + 1) * 8].unsqueeze(2).to_broadcast([128, 8, 64]), ALU.mult, r=["prevT", "cdb"], w=["prevT"])
                        tt("dve", prevT[:, gs], prevT[:, gs], pb[bq][:], ALU.add, r=["prevT", bqk], w=["prevT"])
                        cp("dve", prevb[:, gs], prevT[:, gs], r=["prevT"], w=["prevb"])

            ms2 = T(ph, "ms2", [128, 2]); rs2 = T(ph, "rs2", [128, 2])

            def combine(cc, i):
                t_ = 2 * cc + i
                yi = (cc % 2) * 2 + i; yk = "yt%d" % yi
                z_ = zsb[i]; zk = "zsb%d" % i
                dma("sp", z_[:], zss[t_ * 128:(t_ + 1) * 128, :], r=["zss%d" % t_], w=[zk])
                tt("pool", ytmp[:].rearrange("p (h d) -> p h d", d=64), xo[:, t_, :].rearrange("p (h d) -> p h d", d=64),
                   Db[:].unsqueeze(2).to_broadcast([128, 16, 64]), ALU.mult, r=["xo%d" % t_, "Db"], w=["ytmp"])
                tt("pool", yt[yi][:], yt[yi][:], ytmp[:], ALU.add, r=[yk, "ytmp"], w=[yk])
                tt("dve", yt[yi][:], yt[yi][:], z_[:], ALU.mult, r=[yk, zk], w=[yk])
                for g in range(2):
                    for q_ in range(2):
                        A("dve", lambda e, yi=yi, g=g, q_=q_: e.bn_stats(out=st6[:, q_, :], in_=yt[yi][:, g * 512 + q_ * 256:g * 512 + (q_ + 1) * 256]), r=[yk], w=["st6"])
                    A("dve", lambda e: e.bn_aggr(out=mv[:], in_=st6[:]), r=["st6"], w=["mv"])
                    stt("dve", ms2[:, g:g + 1], mv[:, 0:1], mv[:, 0:1], mv[:, 1:2], ALU.mult, ALU.add, r=["mv"], w=["ms2"])

            def combineN(cc, i):
                yi = (cc % 2) * 2 + i; yk = "yt%d" % yi
                act(rs2[:], ms2[:], AF.Sqrt, r=["ms2", "eps"], w=["rs2"], bias=epsln[:, 0:1], scale=1.0)
                A("dve", lambda e: e.reciprocal(out=rs2[:], in_=rs2[:]), r=["rs2"], w=["rs2"])
                for g in range(2):
                    gs = slice(g * 512, (g + 1) * 512)
                    stt("dve", ynb[:, gs], yt[yi][:, gs], rs2[:, g:g + 1], nwb[:, gs], ALU.mult, ALU.mult, r=[yk, "rs2", "nwb"], w=["ynb"])

            def combineT(cc, i):
                t_ = 2 * cc + i
                for fc in range(8):
                    A("pe", lambda e, fc=fc: e.transpose(pT[:, fc, :], ynb[:, fc * 128:(fc + 1) * 128], identb[:]), r=["ynb", "identb"], w=["pT"])
                cp("act", xo[:, t_, :].rearrange("p (f k) -> p f k", k=128), pT[:], r=["pT"], w=["xo%d" % t_])

            def heads(c):
                par = c % 2; P_ = "_%d" % par if par else ""
                yb = par * 2
                GT_, xdt_, R1_, R2_, L1_, L2_ = GTs[par], xdts[par], R1s[par], R2s[par], L1s[par], L2s[par]
                kR1, kR2, kL1, kL2, kGT, kxdt = ("R1" + P_, "R2" + P_, "L1" + P_, "L2" + P_, "GT%d" % par, "xdt%d" % par)

                def mask(h):
                    act(Lh1[h % 3][:], L1_[:], AF.Copy, r=[kL1, "tinc", "onesf", "sel", "neg384"], w=["Lh1_%d" % (h % 3)], scale=sel[:, h:h + 1])
                    act(Lh2[h % 3][:], L2_[:], AF.Copy, r=[kL2, "tinc", "onesf", "sel", "neg384"], w=["Lh2_%d" % (h % 3)], scale=sel[:, h:h + 1])

                def dexp(h):
                    a_ = Lh1[h % 3]; b_ = Lh2[h % 3]; ak_ = "Lh1_%d" % (h % 3); bk_ = "Lh2_%d" % (h % 3)
                    Dp = pb[5 + h % 2]; dk_ = "pb%d" % (5 + h % 2)
                    mm(Dp[:, 0:256], identb[:], neg384[:, 0:256], True, False, r=["identb", "tinc", "onesf", "sel", "neg384"], w=[dk_])
                    mm(Dp[:, 0:256], a_[:, 0:128], R1_[:, 0:256], False, False, r=[ak_, kR1], w=[dk_])
                    mm(Dp[:, 0:256], b_[:, 0:128], R2_[:, 0:256], False, True, r=[bk_, kR2], w=[dk_])
                    mm(Dp[:, 256:384], identb[:], neg384[:, 256:384], True, False, r=["identb"], w=[dk_])
                    mm(Dp[:, 256:384], a_[:, 128:256], R1_[:, 128:256], False, False, r=[ak_, kR1], w=[dk_])
                    mm(Dp[:, 256:384], b_[:, 128:256], R2_[:, 128:256], False, True, r=[bk_, kR2], w=[dk_])
                    act(LT[h % 2][:], Dp[:, 0:384], AF.Exp, r=[dk_], w=["LT%d" % (h % 2)])

                def mmul(h):
                    tt("dve", MT[h % 3][:], LT[h % 2][:], GT_[:, h // 8, :], ALU.mult, r=["LT%d" % (h % 2), kGT], w=["MT%d" % (h % 3)])

                def ydiag(h):
                    g = h // 8; hh = h % 8
                    mt = MT[h % 3]; mtk = "MT%d" % (h % 3)
                    hs = slice(h * 64, (h + 1) * 64)
                    mm(pb[3][:, hh * 64:(hh + 1) * 64], mt[:, 0:128], xdt_[:, 0, hs], True, True, r=[mtk, kxdt], w=["pb3"])
                    mm(pb[4][:, hh * 64:(hh + 1) * 64], mt[:, 128:256], xdt_[:, 0, hs], True, False, r=[mtk, kxdt], w=["pb4"])
                    mm(pb[4][:, hh * 64:(hh + 1) * 64], mt[:, 256:384], xdt_[:, 1, hs], False, True, r=[mtk, kxdt], w=["pb4"])
                    if hh == 7:
                        for i in range(2):
                            gs = slice(g * 512, (g + 1) * 512)
                            tt("dve", yt[yb + i][:, gs], yt[yb + i][:, gs], pb[3 + i][:], ALU.add, r=["yt%d" % (yb + i), "pb%d" % (3 + i)], w=["yt%d" % (yb + i)])

                mask(0)
                for s_ in range(18):
                    if s_ + 1 < 16:
                        mask(s_ + 1)
                    if s_ < 16:
                        dexp(s_)
                    if 1 <= s_ <= 16:
                        mmul(s_ - 1)
                    if 2 <= s_:
                        ydiag(s_ - 2)
                    if c >= 1:
                        if s_ == 0:
                            combine(c - 1, 0)
                        if s_ == 3:
                            combineN(c - 1, 0)
                        if s_ == 5:
                            combineT(c - 1, 0)
                            combine(c - 1, 1)
                        if s_ == 8:
                            combineN(c - 1, 1)
                        if s_ == 10:
                            combineT(c - 1, 1)
                    if s_ == 9 and c + 1 < 8:
                        prologue(c + 1)

            prologue(0)
            for c in range(8):
                heads(c)
            combine(7, 0)
            combineN(7, 0)
            combineT(7, 0)
            combine(7, 1)
            combineN(7, 1)
            combineT(7, 1)

        sc.barrier()
        if dbg and "xo" in dbg:
            dma("sp", dbgo["d_xo"], xo[:], r=["xo%d" % t for t in range(16)], w=["d_xo"])
        ckpt("ssm")

        def layer_norm(tl, tlk, gbt, bbt, gk, o_t, ok, st8, mv2, rstd):
            for q_ in range(4):
                A("dve", lambda e, q_=q_: e.bn_stats(out=st8[:, q_, :], in_=tl[:, q_ * 256:(q_ + 1) * 256]), r=[tlk], w=["st8"])
            A("dve", lambda e: e.bn_aggr(out=mv2[:], in_=st8[:]), r=["st8"], w=["mv2"])
            act(rstd[:], mv2[:, 1:2], AF.Sqrt, r=["mv2", "eps"], w=["rstd"], bias=epsln[:, 0:1], scale=1.0)
            A("dve", lambda e: e.reciprocal(out=rstd[:], in_=rstd[:]), r=["rstd"], w=["rstd"])
            ts("dve", tl[:], tl[:], mv2[:, 0:1], rstd[:, 0:1], ALU.subtract, ALU.mult, r=[tlk, "mv2", "rstd"], w=[tlk])
            tt("pool", tl[:], tl[:], gbt[:], ALU.mult, r=[tlk, gk], w=[tlk])
            tt("pool", o_t[:], tl[:], bbt[:], ALU.add, r=[tlk, gk], w=[ok])

        with ES() as ph:
            wO = T(ph, "wO", [128, 12, 1024], BF16)
            g1t = T(ph, "g1t", [128, 1024]); b1t = T(ph, "b1t", [128, 1024])
            xin = [T(ph, "xin%d" % i, [128, 1024]) for i in range(2)]
            tl = T(ph, "tl", [128, 1024]); x1t = [T(ph, "x1t%d" % i, [128, 1024]) for i in range(2)]
            h2f = T(ph, "h2f", [128, 1024]); h2b = T(ph, "h2b", [128, 1024], BF16)
            st8 = T(ph, "st8", [128, 4, 6]); mv2 = T(ph, "mv2", [128, 2]); rstd = T(ph, "rstd", [128, 1])
            for pc in range(6):
                dma("pool", wO[:].rearrange("p j c -> p (j c)")[:, pc * 2048:(pc + 1) * 2048],
                    din["w_out"].rearrange("p j c -> p (j c)")[:, pc * 2048:(pc + 1) * 2048], w=["wO%d" % pc])
            bload(g1t, din["ln1_g"], 1024, ["ln1"])
            bload(b1t, din["ln1_b"], 1024, ["ln1"])
            tt("dve", h2f[:], b1t[:], modB[:, 2048:3072], ALU.mult, r=["ln1", "modB"], w=["h2f"])
            tt("dve", modB[:, 1024:2048], modB[:, 1024:2048], h2f[:], ALU.add, r=["modB", "h2f"], w=["modB"])
            tt("dve", modB[:, 2048:3072], modB[:, 2048:3072], g1t[:], ALU.mult, r=["modB", "ln1"], w=["modB"])

            def mm6(t_):
                ts_ = slice(t_ * 128, (t_ + 1) * 128)
                dma("sp", xin[t_ % 2][:], din["x_tok"][ts_, :], w=["xin%d" % (t_ % 2)])
                for half in range(2):
                    hs = slice(half * 512, (half + 1) * 512)
                    bi = 2 * (t_ % 2) + half
                    for m in range(12):
                        if m < 4:
                            l_ = oTa[:, m, ts_]; lk_ = "oTa"
                        else:
                            l_ = xo[:, t_, :].rearrange("p (f k) -> p f k", k=128)[:, m - 4, :]; lk_ = "xo%d" % t_
                        mm(pb[bi][:], l_, wO[:, m, hs], m == 0, m == 11, r=[lk_, "wO%d" % (m // 2)], w=["pb%d" % bi])

            def evac6(t_):
                ts_ = slice(t_ * 128, (t_ + 1) * 128)
                xi = xin[t_ % 2]; xik = "xin%d" % (t_ % 2); tl_ = tls[t_ % 2]; tlk = "tl%d" % (t_ % 2)
                for half in range(2):
                    hs = slice(half * 512, (half + 1) * 512)
                    bi = 2 * (t_ % 2) + half
                    tt("dve", tl_[:, hs], pb[bi][:], modB[:, half * 512:(half + 1) * 512], ALU.mult, r=["pb%d" % bi, "modB"], w=[tlk])
                stt("dve", tl_[:], xi[:], ALPHA, tl_[:], ALU.mult, ALU.add, r=[xik, tlk], w=[tlk])

            def norm6(t_):
                ts_ = slice(t_ * 128, (t_ + 1) * 128)
                tl_ = tls[t_ % 2]; tlk = "tl%d" % (t_ % 2); xo_ = x1t[t_ % 2]; xok = "x1t%d" % (t_ % 2)
                for q_ in range(4):
                    A("dve", lambda e, q_=q_, tl_=tl_, st8=st8: e.bn_stats(out=st8[:, q_, :], in_=tl_[:, q_ * 256:(q_ + 1) * 256]), r=[tlk], w=["st8"])
                A("dve", lambda e, st8=st8, mv2=mv2: e.bn_aggr(out=mv2[:], in_=st8[:]), r=["st8"], w=["mv2"])
                act(rstd[:], mv2[:, 1:2], AF.Sqrt, r=["mv2", "eps"], w=["rstd"], bias=epsln[:, 0:1], scale=1.0)
                A("dve", lambda e, rstd=rstd: e.reciprocal(out=rstd[:], in_=rstd[:]), r=["rstd"], w=["rstd"])
                ts("dve", tl_[:], tl_[:], mv2[:, 0:1], rstd[:, 0:1], ALU.subtract, ALU.mult, r=[tlk, "mv2", "rstd"], w=[tlk])
                tt("dve", h2f[:], tl_[:], modB[:, 2048:3072], ALU.mult, r=[tlk, "modB"], w=["h2f"])
                tt("dve", h2b[:], h2f[:], modB[:, 1024:2048], ALU.add, r=["h2f", "modB"], w=["h2b"])
                tt("dve", xo_[:], tl_[:], g1t[:], ALU.mult, r=[tlk, "ln1"], w=[xok])
                tt("dve", xo_[:], xo_[:], b1t[:], ALU.add, r=[xok, "ln1"], w=[xok])
                dma("sp", x1s[ts_, :], xo_[:], r=[xok], w=["x1s%d" % t_])
                if dbg and "x1" in dbg:
                    dma("sp", dbgo["d_x1"][ts_, :], xo_[:], r=[xok], w=["d_x1"])

            def tr6(t_):
                ts_ = slice(t_ * 128, (t_ + 1) * 128)
                for k in range(8):
                    A("pe", lambda e, k=k: e.transpose(pT[:, k, :], h2b[:, k * 128:(k + 1) * 128], identb[:]), r=["h2b", "identb"], w=["pT"])
                cp("act", hT[:, :, ts_], pT[:], r=["pT"], w=["hT"])

            tls = [tl, T(ph, "tlB", [128, 1024])]
            mm6(0)
            evac6(0)
            for t_ in range(16):
                if t_ + 1 < 16:
                    mm6(t_ + 1)
                norm6(t_)
                if t_ + 1 < 16:
                    evac6(t_ + 1)
                tr6(t_)
        mix2.close()
        sc.barrier()
        ckpt("ln1")

        with ES() as ph:
            wD = T(ph, "wD", [128, NJ, 1024], BF16); aT = T(ph, "aT", [128, NJ, 1024], BF16)
            wU = [T(ph, "wU%d" % i, [128, 8, 256], BF16) for i in range(2)]
            ub = [[T(ph, "ub%d%d" % (gv, i), [128, 1026]) for i in range(2)] for gv in range(2)]
            cv = [[T(ph, "cv%d%d" % (gv, i), [128, 512]) for i in range(2)] for gv in range(2)]
            sg = [T(ph, "sg%d" % i, [128, 512]) for i in range(2)]
            halo = T(ph, "halo", [128, NJ, 2, 2]); fcw = T(ph, "fcw", [128, NJ, 2, 3]); fcb = T(ph, "fcb", [128, NJ, 2])
            g2t = T(ph, "g2t", [128, 1024]); b2t = T(ph, "b2t", [128, 1024])
            x1i = [T(ph, "x1i%d" % i, [128, 1024]) for i in range(2)]; tl = T(ph, "tl2", [128, 1024])
            ot = [T(ph, "ot%d" % i, [128, 1024]) for i in range(2)]
            st8 = T(ph, "st8b", [128, 4, 6]); mv2 = T(ph, "mv2b", [128, 2]); rstd = T(ph, "rstdb", [128, 1])
            tls7 = [tl, T(ph, "tl2B", [128, 1024])]
            dma("sp", fcw[:], din["fcw"], w=["fcw"]); dma("sp", fcb[:], din["fcb"], w=["fcb"])
            bload(g2t, din["ln2_g"], 1024, ["ln2"])
            bload(b2t, din["ln2_b"], 1024, ["ln2"])
            if not hasattr(sc, "groups"):
                sc.groups = {}
            sc.groups["wD"] = ["wD#%d" % i for i in range(11)]
            for blk in range(2):
                for j in range(NJ):
                    w_ = wU[j % 2]; wk = "wU%d" % (j % 2)
                    def load_wu(jj):
                        dma("pool", wU[jj % 2][:].rearrange("p k c -> p (k c)"), din["w_up"][jj].rearrange("p k c -> p (k c)"), w=["wU%d" % (jj % 2)])
                    if blk == 0 and j == 0:
                        load_wu(0)
                    jn = (j + 1) % NJ
                    if not (blk == 1 and j == NJ - 1):
                        load_wu(jn)
                    if blk == 0 and 1 <= j <= 11:
                        c0 = (j - 1) * 2048
                        dma("pool", wD[:].rearrange("p j c -> p (j c)")[:, c0:c0 + 2048], din["w_dn"].rearrange("p j c -> p (j c)")[:, c0:c0 + 2048], w=["wD#%d" % (j - 1)])
                    us = [ub[gv][j % 2] for gv in range(2)]
                    uks = ["ub%d%d" % (gv, j % 2) for gv in range(2)]
                    for gv in range(2):
                        if blk == 0:
                            A("dve", lambda e, u_=us[gv]: e.memset(u_[:, 0:2], 0.0), w=[uks[gv] + "h"])
                        else:
                            cp("pool", us[gv][:, 0:2], halo[:, j, gv, :], r=["halo%d_%d" % (j, gv)], w=[uks[gv] + "h"])
                    for hf in range(2):
                        c0 = blk * 1024 + hf * 512
                        for gv in range(2):
                            p_ = pb[gv * 2 + hf]; pk = "pb%d" % (gv * 2 + hf)
                            for k in range(8):
                                mm(p_[:], w_[:, k, gv * 128:(gv + 1) * 128], hT[:, k, c0:c0 + 512], k == 0, k == 7, r=[wk, "hT"], w=[pk])
                            cp("act", us[gv][:, 2 + hf * 512:2 + (hf + 1) * 512], p_[:], r=[pk], w=[uks[gv] + "ab"[hf]])
                            act(cv[gv][hf][:], p_[:], AF.Identity, r=[pk, "fcw", "fcb"], w=["cv%d%d" % (gv, hf)],
                                scale=fcw[:, j, gv, 2:3], bias=fcb[:, j, gv:gv + 1])
                    for hf in range(2):
                        for tap in (1, 0):
                            for gv in range(2):
                                rk = [uks[gv] + "a", "cv%d%d" % (gv, hf), "fcw"] + ([uks[gv] + "h"] if hf == 0 else [uks[gv] + "b"])
                                stt("dve", cv[gv][hf][:], us[gv][:, tap + hf * 512:tap + hf * 512 + 512], fcw[:, j, gv, tap:tap + 1], cv[gv][hf][:],
                                    ALU.mult, ALU.add, r=rk, w=["cv%d%d" % (gv, hf)])
                    for hf in range(2):
                        act(sg[hf][:], cv[0][hf][:], AF.Silu, r=["cv0%d" % hf], w=["sg%d" % hf])
                    for hf in range(2):
                        tt("dve", aT[:, j, hf * 512:(hf + 1) * 512], sg[hf][:], cv[1][hf][:], ALU.mult, r=["sg%d" % hf, "cv1%d" % hf], w=["aT"])
                    if blk == 0:
                        for gv in range(2):
                            cp("pool", halo[:, j, gv, :], us[gv][:, 1024:1026], r=[uks[gv] + "b"], w=["halo%d_%d" % (j, gv)])
                def front7(t8):
                    t_ = blk * 8 + t8
                    ts_ = slice(t_ * 128, (t_ + 1) * 128)
                    tl_ = tls7[t8 % 2]; tlk = "tl7%d" % (t8 % 2)
                    dma("sp", x1i[t8 % 2][:], x1s[ts_, :], r=["x1s%d" % t_], w=["x1i%d" % (t8 % 2)])
                    for half in range(2):
                        hs = slice(half * 512, (half + 1) * 512)
                        for j in range(NJ):
                            mm(pb[4 + half][:], aT[:, j, t8 * 128:(t8 + 1) * 128], wD[:, j, hs], j == 0, j == NJ - 1, r=["aT", "wD"], w=["pb%d" % (4 + half)])
                        tt("dve", tl_[:, hs], pb[4 + half][:], g2p1[:, half * 512:(half + 1) * 512], ALU.mult, r=["pb%d" % (4 + half), "g2p1"], w=[tlk])

                def back7(t8):
                    t_ = blk * 8 + t8
                    ts_ = slice(t_ * 128, (t_ + 1) * 128)
                    tl_ = tls7[t8 % 2]; tlk = "tl7%d" % (t8 % 2)
                    xi = x1i[t8 % 2]; xik = "x1i%d" % (t8 % 2); o_ = ot[t8 % 2]; ok_ = "ot%d" % (t8 % 2)
                    stt("dve", tl_[:], xi[:], ALPHA, tl_[:], ALU.mult, ALU.add, r=[xik, tlk], w=[tlk])
                    layer_norm(tl_, tlk, g2t, b2t, "ln2", o_, ok_, st8, mv2, rstd)
                    dma("sp", out[ts_, :], o_[:], r=[ok_], w=["out"])

                front7(0)
                for t8 in range(8):
                    if t8 + 1 < 8:
                        front7(t8 + 1)
                    back7(t8)
        outs = [op for op in sc.ops if op.dma]
        A("sp", lambda e: e.wait_ge(outs[0].sem, 16), r=["out", "d_x1", "d_mod", "d_oTa", "d_xo", "d_QT", "d_KT", "d_O", "d_rec", "d_Vp", "d_ones", "d_hT", "d_wA", "d_kf", "d_ge", "d_gm", "d_m8"], w=["done"], force=True)
        sc.emit(nc, top)
    return nc


_CACHE = {}


def _prep(inputs):
    sh = _shared_inputs(inputs)
    sh.update(_consts())
    x = np.asarray(inputs["x"], dtype=np.float32)
    c = np.asarray(inputs["c"], dtype=np.float32)
    maps = []
    for b in range(8):
        m = dict(sh)
        m["x_tok"] = np.ascontiguousarray(x[b])
        m["xT"] = np.ascontiguousarray(x[b].T)
        m["c_col"] = np.ascontiguousarray(c[b].reshape(8, 128).T)
        maps.append(m)
    return maps


def kernel(**inputs):
    maps = _prep(inputs)
    shapes = {n: (a.shape, _DT[a.dtype]) for n, a in maps[0].items()}
    nc = build(shapes)
    res = run_bass_kernel_spmd(nc, maps, core_ids=list(range(8)))
    return np.stack([np.asarray(r["out"], dtype=np.float32) for r in res.results], axis=0)
```

```python
import contextlib
import numpy as np
import ml_dtypes
import concourse.bass as bass
import concourse.mybir as mybir
from concourse.bass_utils import run_bass_kernel_spmd

F32 = mybir.dt.float32
BF16 = mybir.dt.bfloat16
AF = mybir.ActivationFunctionType
ALU = mybir.AluOpType
AX = mybir.AxisListType

S = 2048
D = 1024
NT = S // 128
DFF = 2816
NJ = DFF // 128
NEG = -30000.0
ALPHA = 2.0 ** 0.25
STRICT = {"act", "dve", "pool"}


class _Op:
    __slots__ = ("eng", "fn", "deps", "signal", "sem", "val", "dma", "idx", "prev_val")


class Sched:
    ENGS = ("pe", "act", "dve", "pool", "sp")

    def __init__(self):
        self.ops = []
        self.lw = {}
        self.rd = {}

    frozen = False

    def add(self, eng, fn, r=(), w=(), dma=False, force=False):
        if self.frozen and not force:
            return None
        groups = getattr(self, "groups", {})
        if groups:
            r = [m for k in r for m in groups.get(k, (k,))]
        op = _Op()
        op.eng, op.fn, op.dma, op.signal, op.idx = eng, fn, dma, dma, len(self.ops)
        deps = set()
        for k in r:
            x = self.lw.get(k)
            if x is not None:
                deps.add(x)
        for k in w:
            x = self.lw.get(k)
            if x is not None:
                deps.add(x)
            deps.update(self.rd.get(k, ()))
        for k in r:
            self.rd.setdefault(k, []).append(op.idx)
        for k in w:
            self.lw[k] = op.idx
            self.rd[k] = []
        op.deps = deps
        self.ops.append(op)
        return op

    def barrier(self):
        if self.frozen:
            return
        start = getattr(self, "_bar_from", 0)
        deps = set()
        last = {}
        for op in self.ops[start:]:
            if op.dma:
                deps.add(op.idx)
            last[op.eng] = op.idx
        deps.update(last.values())
        deps.update(getattr(self, "_bar_ops", ()))
        n0 = len(self.ops)
        bar_ops = []
        for eng in self.ENGS:
            op = self.add(eng, lambda e: e.wait_ge(self.esem["pe"], 0))
            op.deps = set(deps)
            bar_ops.append(op.idx)
        self._bar_from = n0
        self._bar_ops = bar_ops

    def emit(self, nc, stack, n_dma_sems=8):
        ops = self.ops
        for op in ops:
            for d in op.deps:
                dop = ops[d]
                if dop.eng != op.eng or dop.dma or dop.eng in STRICT:
                    dop.signal = True
        esem = {e: stack.enter_context(nc.semaphore("s_" + e)) for e in self.ENGS}
        self.esem = esem
        dsem = {e: [stack.enter_context(nc.semaphore("d_%s%d" % (e, i))) for i in range(n_dma_sems)]
                for e in ("sp", "pool", "act")}
        cnt = {e: 0 for e in self.ENGS}
        dcnt = {e: 0 for e in dsem}
        for op in ops:
            if not op.signal:
                continue
            if op.dma:
                k = dcnt[op.eng]
                dcnt[op.eng] += 1
                op.sem = dsem[op.eng][k % n_dma_sems]
                op.val = 16 * (k // n_dma_sems + 1)
            else:
                cnt[op.eng] += 1
                op.sem = esem[op.eng]
                op.val = cnt[op.eng]
        by_eng = {e: [op for op in ops if op.eng == e] for e in self.ENGS}

        def run(eng_name, e):
            waited = {}
            for op in by_eng[eng_name]:
                need = {}
                for d in op.deps:
                    dop = ops[d]
                    if not dop.signal:
                        continue
                    if dop.eng == op.eng and not dop.dma and dop.eng not in STRICT:
                        continue
                    key = id(dop.sem)
                    if key not in need or need[key][1] < dop.val:
                        need[key] = (dop.sem, dop.val)
                if op.dma and op.val > 16:
                    key = id(op.sem)
                    if key not in need or need[key][1] < op.val - 16:
                        need[key] = (op.sem, op.val - 16)
                for key, (sem, val) in need.items():
                    if waited.get(key, 0) >= val:
                        continue
                    e.wait_ge(sem, val)
                    waited[key] = val
                ins = op.fn(e)
                if op.signal:
                    ins.then_inc(op.sem, 16 if op.dma else 1)

        with nc.Block() as block:
            @block.tensor
            def _(e):
                run("pe", e)

            @block.scalar
            def _(e):
                run("act", e)

            @block.vector
            def _(e):
                run("dve", e)

            @block.gpsimd
            def _(e):
                run("pool", e)

            @block.sync
            def _(e):
                run("sp", e)


def _consts():
    c = {}
    c["identf"] = np.eye(128, dtype=np.float32)
    p = np.arange(128)
    tri = np.where(p[None, :] >= p[:, None], 0.0, NEG).astype(np.float32)
    c["negtri"] = tri
    c["negfull"] = np.full((128, 128), NEG, np.float32)
    qb = (np.arange(16) // 2)[:, None]
    kb = np.arange(8)[None, :]
    past = kb < qb
    c["pastneg"] = np.broadcast_to(np.where(past, 0.0, -1e30).astype(np.float32)[None], (128, 16, 8)).copy()
    base = np.where(kb == qb, 0.0, NEG).astype(np.float32)
    c["basetab"] = np.broadcast_to(base[None], (128, 16, 8)).copy()
    slopes = 2.0 ** (-(np.arange(8) + 1.0))
    A = np.zeros((8, 16, 8), np.float32)
    for h in range(8):
        A[h] = np.where(past, -slopes[h] * 256.0 * (qb - kb) - NEG, 0.0)
    c["atab"] = np.broadcast_to(A[None], (128, 8, 16, 8)).copy()
    t = np.arange(S)
    r = (t % 256).astype(np.float32)
    qst = np.zeros((2, 8, S), np.float32)
    kst = np.zeros((10, 8, S), np.float32)
    for h in range(8):
        qst[0, h] = 1.0
        qst[1, h] = -slopes[h] * r
        for b in range(8):
            kst[b, h] = (t // 256 == b)
        kst[8, h] = slopes[h] * r
        kst[9, h] = 1.0
    c["qstat"] = qst.astype(ml_dtypes.bfloat16)
    c["kstat"] = kst.astype(ml_dtypes.bfloat16)
    op = np.zeros((128, 2, 128), np.float32)
    op[:, 0, 0:64] = 1.0
    op[:, 1, 64:128] = 1.0
    c["onespad"] = op.astype(ml_dtypes.bfloat16)
    l = np.arange(256)
    tinc = np.zeros((128, 2, 256), np.float32)
    tinc[:, 0, :] = (p[:, None] <= l[None, :])
    tinc[:, 1, :] = (p[:, None] + 128 <= l[None, :])
    c["tinc"] = tinc
    c["onesf"] = np.ones((128, 128), np.float32)
    sel = np.zeros((64, 16), np.float32)
    for h in range(16):
        sel[h, h] = 1.0
        sel[32 + h, h] = 1.0
    c["sel"] = sel
    lb = np.zeros((64, 256), np.float32)
    lb[0:16] = 1.0
    c["lbase0"] = lb
    rb = np.zeros((64, 256), np.float32)
    rb[32:48] = 1.0
    c["rbase0"] = rb
    c["rbase0b"] = rb.astype(ml_dtypes.bfloat16)
    c["lbase0b"] = lb.astype(ml_dtypes.bfloat16)
    n384 = np.zeros((128, 384), np.float32)
    n384[:, 0:128] = tri
    n384[:, 256:384] = tri
    c["neg384"] = n384.astype(ml_dtypes.bfloat16)
    return c


_DT = {np.dtype(np.float32): F32, np.dtype(ml_dtypes.bfloat16): BF16}


def _shared_inputs(inp):
    g = lambda n: np.ascontiguousarray(inp[n][0], dtype=np.float32)
    sh = {}
    sh["ada_w"] = np.ascontiguousarray(g("ada_w").reshape(8, 128, 6144).transpose(1, 0, 2))
    sh["ada_b"] = g("ada_b").reshape(1, 6144)
    sh["ada_b_col"] = np.ascontiguousarray(g("ada_b").reshape(48, 128).T)
    w_in = g("mix_in_w").reshape(8, 128, 4112).transpose(1, 0, 2)
    sh["w_inA"] = np.ascontiguousarray(np.stack([np.concatenate([w_in[:, :, o + gI * 256:o + gI * 256 + 256] for o in (0, 512, 1024)], axis=2).reshape(128, 8 * 768)
                                                 for gI in range(2)], axis=0))
    sh["w_inBx"] = np.ascontiguousarray(w_in[:, :, 2560:4096].reshape(128, 8 * 1536))
    sh["w_inBz"] = np.ascontiguousarray(np.concatenate([w_in[:, :, 1536:2560], w_in[:, :, 4096:4112]], axis=2).reshape(128, 8 * 1040))
    sh["w_out"] = np.ascontiguousarray(g("mix_out_w").reshape(12, 128, 1024).transpose(1, 0, 2))
    wu = g("ffn_up_w").reshape(8, 128, 2, NJ, 128)
    sh["w_up"] = np.ascontiguousarray(wu.transpose(3, 1, 0, 2, 4).reshape(NJ, 128, 8, 256))
    sh["w_dn"] = np.ascontiguousarray(g("ffn_down_w").reshape(NJ, 128, 1024).transpose(1, 0, 2))
    fcw = g("ffn_conv_w").reshape(3, 2, NJ, 128)
    sh["fcw"] = np.ascontiguousarray(fcw.transpose(3, 2, 1, 0))
    sh["fcb"] = np.ascontiguousarray(g("ffn_conv_b").reshape(2, NJ, 128).transpose(2, 1, 0))
    sh["scw"] = np.ascontiguousarray(g("ssm_conv_w").reshape(4, 12, 128).transpose(2, 1, 0))
    sh["scb"] = np.ascontiguousarray(g("ssm_conv_b").reshape(12, 128).transpose(1, 0))
    for n in ("ln1_g", "ln1_b", "ln2_g", "ln2_b", "ssm_norm_w"):
        sh[n] = g(n).reshape(1, 1024)
    for n in ("ssm_dt_bias", "ssm_a_log", "ssm_d"):
        sh[n] = g(n).reshape(1, 16)
    return sh


def build(shapes, stop_after=None, dbg=False):
    nc = bass.Bass("TRN2", target_bir_lowering=False)
    sc = Sched()
    A = sc.add
    ES = contextlib.ExitStack
    din = {n: nc.dram_tensor(n, list(shp), dt, kind="ExternalInput").ap() for n, (shp, dt) in shapes.items()}
    out = nc.dram_tensor("out", [S, D], F32, kind="ExternalOutput").ap()
    x1s = nc.dram_tensor("x1s", [S, D], F32, kind="Internal").ap()
    zss = nc.dram_tensor("zss", [S, D], F32, kind="Internal").ap()
    dbgo = {}
    if dbg:
        for n, shp, dt_ in (("d_oTa", [128, 4, S], BF16), ("d_Vp", [128, 16, 4, 128], BF16), ("d_ones", [128, 2, 128], BF16), ("d_rec", [128, 512], F32), ("d_O", [128, 512], F32), ("d_hT", [128, 8, S], BF16), ("d_wA", [128, 8, 768], BF16), ("d_kf", [64, S], F32), ("d_QT", [128, 4, S], BF16), ("d_KT", [128, 4, S], BF16), ("d_ge", [128, 16, 8], F32), ("d_gm", [128, 128], F32), ("d_m8", [128, 16, 8], F32), ("d_xo", [128, 16, 1024], BF16), ("d_x1", [S, D], F32), ("d_mod", [128, 6144], F32)):
            dbgo[n] = nc.dram_tensor(n, shp, dt_, kind="ExternalOutput").ap()

    def ckpt(name):
        if stop_after == name:
            sc.frozen = True

    def T(st, name, shape, dt=F32):
        return st.enter_context(nc.sbuf_tensor("sb_" + name, list(shape), dt))

    def dma(eng, o, i, r=(), w=()):
        return A(eng, lambda e: e.dma_start(out=o, in_=i), r=r, w=w, dma=True)

    def bload(tile, src, width, w):
        keys = []
        for c0 in range(0, width, 512):
            c1 = min(width, c0 + 512)
            key = "%s#%d" % (w[0], c0 // 512)
            dma("sp", tile[:, c0:c1], src[:, c0:c1].partition_broadcast(128), w=[key])
            keys.append(key)
        if not hasattr(sc, "groups"):
            sc.groups = {}
        sc.groups[w[0]] = keys

    def cast_load(dst_flat, src_flat, n, w):
        keys = []
        for c0 in range(0, n, 2048):
            c1 = min(n, c0 + 2048)
            key = "%s#%d" % (w[0], c0 // 2048)
            dma("pool", dst_flat[:, c0:c1], src_flat[:, c0:c1], w=[key])
            keys.append(key)
        if not hasattr(sc, "groups"):
            sc.groups = {}
        sc.groups[w[0]] = keys

    def cp(eng, o, i, r=(), w=()):
        if eng == "act":
            return A("act", lambda e: e.activation(out=o, in_=i, func=AF.Copy), r=r, w=w)
        return A(eng, lambda e: e.tensor_copy(out=o, in_=i), r=r, w=w)

    def mm(o, l, rr, start, stop, r=(), w=()):
        return A("pe", lambda e: e.matmul(o, l, rr, start=start, stop=stop), r=r, w=w)

    def tt(eng, o, a, b, op, r=(), w=()):
        return A(eng, lambda e: e.tensor_tensor(out=o, in0=a, in1=b, op=op), r=r, w=w)

    def ts(eng, o, a, s1, s2, op0, op1=None, r=(), w=()):
        if op1 is None:
            return A(eng, lambda e: e.tensor_scalar(out=o, in0=a, scalar1=s1, scalar2=None, op0=op0), r=r, w=w)
        return A(eng, lambda e: e.tensor_scalar(out=o, in0=a, scalar1=s1, scalar2=s2, op0=op0, op1=op1), r=r, w=w)

    def stt(eng, o, a, s, b, op0, op1, r=(), w=()):
        return A(eng, lambda e: e.scalar_tensor_tensor(out=o, in0=a, scalar=s, in1=b, op0=op0, op1=op1), r=r, w=w)

    def act(o, i, func, r=(), w=(), **kw):
        return A("act", lambda e: e.activation(out=o, in_=i, func=func, **kw), r=r, w=w)

    with ES() as top:
        pb = [top.enter_context(nc.psum_tensor("pb%d" % i, [128, 512], F32)) for i in range(7)]
        pT = top.enter_context(nc.psum_tensor("pT", [128, 8, 128], BF16))
        g2p1 = T(top, "g2p1", [128, 1024])
        identf = T(top, "identf", [128, 128]); identb = T(top, "identb", [128, 128], BF16)
        epsln = T(top, "epsln", [128, 1])
        dma("sp", identf[:], din["identf"], w=["identf"])
        cp("dve", identb[:], identf[:], r=["identf"], w=["identb"])
        A("dve", lambda e: e.memset(epsln[:], 1e-5), w=["eps"])
        hT = T(top, "hT", [128, 8, S], BF16)
        mix2 = top.enter_context(ES())
        oTa = T(mix2, "oTa", [128, 4, S], BF16)
        modB = T(mix2, "modB", [128, 3072])

        with ES() as ph:
            aw = [T(ph, "aw%d" % i, [128, 8, 512]) for i in range(2)]
            ccol = T(ph, "ccol", [128, 8]); cact = T(ph, "cact", [128, 8])
            colall = T(ph, "colall", [128, 48]); abcol = T(ph, "abcol", [128, 48])
            ones1 = T(ph, "ones1", [128, 128]); dg = [T(ph, "dg%d" % i, [128, 128]) for i in range(4)]
            xt = [T(ph, "xt%d" % i, [128, S]) for i in range(4)]
            dma("sp", ccol[:], din["c_col"], w=["ccol"])
            dma("sp", abcol[:], din["ada_b_col"], w=["abcol"])
            A("dve", lambda e: e.memset(ones1[:], 1.0), w=["ones1"])
            act(cact[:], ccol[:], AF.Silu, r=["ccol"], w=["cact"])
            awb = [T(ph, "awb%d" % i, [128, 8, 512], BF16) for i in range(2)]
            cactb = T(ph, "cactb", [128, 8], BF16)
            cp("dve", cactb[:], cact[:], r=["cact"], w=["cactb"])
            for k in range(4):
                dma("pool", xt[k][:], din["xT"][k * 128:(k + 1) * 128, :], w=["xt%d" % k])
            for n in range(12):
                a_ = aw[n % 2]; ak = "aw%d" % (n % 2); b_ = awb[n % 2]; bk = "awb%d" % (n % 2)
                dma("sp", a_[:], din["ada_w"][:, :, n * 512:(n + 1) * 512], w=[ak])
                cp("act", b_[:, 0:4, :], a_[:, 0:4, :], r=[ak], w=[bk + "a"])
                cp("dve", b_[:, 4:8, :], a_[:, 4:8, :], r=[ak], w=[bk + "b"])
                for fcl in range(4):
                    for kk in range(8):
                        mm(pb[2][:, n * 4 + fcl:n * 4 + fcl + 1], b_[:, kk, fcl * 128:(fcl + 1) * 128], cactb[:, kk:kk + 1], kk == 0, kk == 7,
                           r=[bk + "a", bk + "b", "cactb"], w=["pb2"])
                if n == 3:
                    tt("dve", colall[:, 0:16], pb[2][:, 0:16], abcol[:, 0:16], ALU.add, r=["pb2", "abcol"], w=["colA"])
                    ts("dve", colall[:, 8:16], colall[:, 8:16], 1.0, None, ALU.add, r=["colA"], w=["colA"])
            tt("dve", colall[:, 16:48], pb[2][:, 16:48], abcol[:, 16:48], ALU.add, r=["pb2", "abcol"], w=["colB"])
            ts("dve", colall[:, 16:24], colall[:, 16:24], 1.0, None, ALU.add, r=["colB"], w=["colB"])
            ts("dve", colall[:, 32:48], colall[:, 32:48], 1.0, None, ALU.add, r=["colB"], w=["colB"])
            colsh = colall[:, 0:8]; colsc = colall[:, 8:16]
            ckpt("s1a")
            for q4 in range(8):
                p_ = pb[q4 % 2]; pk = "pb%d" % (q4 % 2)
                for i in range(4):
                    c_ = 16 + q4 * 4 + i
                    d_ = dg[i]; dk_ = "dg%d" % i
                    ts("dve", d_[:], identf[:], colall[:, c_:c_ + 1], None, ALU.mult, r=["identf", "colB"], w=[dk_])
                    mm(p_[:, i * 128:(i + 1) * 128], ones1[:], d_[:], True, True, r=["ones1", dk_], w=[pk])
                if q4 < 6:
                    cp("act", modB[:, q4 * 512:(q4 + 1) * 512], p_[:], r=[pk], w=["modB"])
                else:
                    cp("act", g2p1[:, (q4 - 6) * 512:(q4 - 5) * 512], p_[:], r=[pk], w=["g2p1"])
            ckpt("s1b")
            for k in range(8):
                x_ = xt[k % 4]; xk = "xt%d" % (k % 4)
                if k >= 4:
                    dma("pool", x_[:], din["xT"][k * 128:(k + 1) * 128, :], w=[xk])
                ts("dve", hT[:, k, :], x_[:], colsc[:, k:k + 1], colsh[:, k:k + 1], ALU.mult, ALU.add, r=[xk, "colA"], w=["hT"])

        sc.barrier()
        ckpt("s12")
        with ES() as ph:
            wA = T(ph, "wA", [128, 8, 768], BF16)
            QT = T(ph, "QT", [128, 4, S], BF16); KT = T(ph, "KT", [128, 4, S], BF16)
            Vp = T(ph, "Vp", [128, 16, 4, 128], BF16)
            kf = T(ph, "kf", [64, S]); qf = T(ph, "qf", [64, S]); km = T(ph, "km", [64, 8])
            kfs = [kf, T(ph, "kfB", [64, S])]; qfs = [qf, T(ph, "qfB", [64, S])]; kms = [km, T(ph, "kmB", [64, 8])]
            gm = T(ph, "gm", [128, 128]); m8 = T(ph, "m8", [128, 16, 8]); ge = T(ph, "ge", [128, 16, 8])
            mpad = T(ph, "mpad", [128, 16, 72], BF16)
            pastneg = T(ph, "pastneg", [128, 128]); basetab = T(ph, "basetab", [128, 16, 8]); atab = T(ph, "atab", [128, 8, 16, 8])
            negtri = T(ph, "negtri", [128, 128]); negfull = T(ph, "negfull", [128, 128]); onesp = T(ph, "onesp", [128, 2, 128], BF16)
            PT = [T(ph, "PT%d" % i, [128, 512], BF16) for i in range(6)]
            rec = T(ph, "rec", [128, 512])
            dma("sp", pastneg[:], din["pastneg"].rearrange("p a b -> p (a b)"), w=["pastneg"])
            dma("sp", basetab[:], din["basetab"], w=["basetab"]); dma("sp", atab[:], din["atab"], w=["atab"])
            dma("sp", negtri[:], din["negtri"], w=["negtri"]); dma("sp", negfull[:], din["negfull"], w=["negfull"])
            dma("sp", onesp[:], din["onespad"], w=["onesp"])
            zf = T(ph, "zf", [128, 1152])
            A("dve", lambda e: e.memset(zf[:], 0.0), w=["zf"])
            for t_ in range(16):
                cp("dve", Vp[:, t_], zf[:, 0:512].rearrange("p (a b) -> p a b", b=128), r=["zf"], w=["Vp"])
            cp("dve", mpad[:], zf[:].rearrange("p (a b) -> p a b", b=72), r=["zf"], w=["mpad"])
            sring = 0
            pring = 0
            for gI in range(2):
                cast_load(wA[:].rearrange("p k c -> p (k c)"), din["w_inA"][gI], 8 * 768, ["wA"])
                dma("sp", QT[72:74, :, :], din["qstat"][:, 4 * gI:4 * gI + 4, :], w=["QT"])
                dma("sp", KT[64:74, :, :], din["kstat"][:, 4 * gI:4 * gI + 4, :], w=["KT"])
                for t_ in range(16):
                    p_ = pb[t_ % 2]; pk = "pb%d" % (t_ % 2)
                    for k in range(8):
                        mm(p_[:, 0:256], hT[:, k, t_ * 128:(t_ + 1) * 128], wA[:, k, 512:768], k == 0, k == 7, r=["hT", "wA"], w=[pk])
                    pv = p_[:, 0:256].rearrange("p (a t d) -> p a t d", t=2, d=64)
                    vv = Vp[:, t_].rearrange("p (a t) c -> p a t c", t=2)
                    cp("dve", vv[:, :, 0, 0:64], pv[:, :, 0, :], r=[pk], w=["Vp"])
                    cp("dve", vv[:, :, 1, 64:128], pv[:, :, 1, :], r=[pk], w=["Vp"])
                ckpt("a1")
                def prepA(hl):
                    kf_ = kfs[hl % 2]; qf_ = qfs[hl % 2]; km_ = kms[hl % 2]
                    kfk = "kf%d" % (hl % 2); qfk = "qf%d" % (hl % 2); kmk = "km%d" % (hl % 2)
                    for tb in range(4):
                        p_ = pb[tb % 2]; pk = "pb%d" % (tb % 2)
                        for k in range(8):
                            mm(p_[:], wA[:, k, 256 + hl * 64:256 + hl * 64 + 128], hT[:, k, tb * 512:(tb + 1) * 512], k == 0, k == 7, r=["hT", "wA"], w=[pk])
                        cp("act", kf_[:, tb * 512:(tb + 1) * 512], p_[0:64, :], r=[pk], w=[kfk])
                        cp("dve", KT[0:64, hl, tb * 512:(tb + 1) * 512], kf_[:, tb * 512:(tb + 1) * 512], r=[kfk], w=["KT"])
                    A("dve", lambda e, kf_=kf_, km_=km_: e.tensor_reduce(out=km_[:], in_=kf_[:].rearrange("p (b t) -> p b t", t=256), axis=AX.X, op=ALU.add), r=[kfk], w=[kmk])
                    ts("dve", km_[:], km_[:], 1.0 / 256.0, None, ALU.mult, r=[kmk], w=[kmk])
                    for tb in range(4):
                        p_ = pb[tb % 2]; pk = "pb%d" % (tb % 2)
                        for k in range(8):
                            mm(p_[:], wA[:, k, hl * 64:hl * 64 + 128], hT[:, k, tb * 512:(tb + 1) * 512], k == 0, k == 7, r=["hT", "wA"], w=[pk])
                        cp("act", qf_[:, tb * 512:(tb + 1) * 512], p_[0:64, :], r=[pk], w=[qfk])
                        ts("dve", QT[0:64, hl, tb * 512:(tb + 1) * 512], qf_[:, tb * 512:(tb + 1) * 512], 0.125, None, ALU.mult, r=[qfk], w=["QT"])

                def prepB(hl):
                    h = 4 * gI + hl
                    qf_ = qfs[hl % 2]; km_ = kms[hl % 2]; qfk = "qf%d" % (hl % 2); kmk = "km%d" % (hl % 2)
                    for t_ in range(16):
                        mm(pb[2][:, t_ * 8:(t_ + 1) * 8], qf_[:, t_ * 128:(t_ + 1) * 128], km_[:], True, True, r=[qfk, kmk], w=["pb2"])
                    tt("dve", gm[:], pb[2][:, 0:128], pastneg[:], ALU.add, r=["pb2", "pastneg"], w=["gm"])
                    for t_ in range(16):
                        A("dve", lambda e, t_=t_: e.max(out=m8[:, t_, :], in_=gm[:, t_ * 8:(t_ + 1) * 8]), r=["gm"], w=["m8"])
                    tt("dve", ge[:], gm[:].rearrange("p (a b) -> p a b", b=8), m8[:, :, 2:3].to_broadcast([128, 16, 8]), ALU.is_ge, r=["gm", "m8"], w=["ge"])
                    tt("dve", ge[:], ge[:], atab[:, h], ALU.mult, r=["ge", "atab"], w=["ge"])
                    tt("dve", mpad[:, :, 64:72], ge[:], basetab[:], ALU.add, r=["ge", "basetab"], w=["mpad"])

                def prepC(hl):
                    for r4 in range(4):
                        for i in range(4):
                            mm(pb[3][0:72, i * 128:(i + 1) * 128], mpad[:, r4 * 4 + i, :], identb[:], True, True, r=["mpad", "identb"], w=["pb3"])
                        cp("act", QT[64:72, hl, r4 * 512:(r4 + 1) * 512], pb[3][64:72, :], r=["pb3"], w=["QT"])

                prepA(0)
                for hl in range(4):
                    prepB(hl)
                    if hl + 1 < 4:
                        prepA(hl + 1)
                    prepC(hl)
                if dbg and "prep" in dbg and gI == 0:
                    dma("sp", dbgo["d_QT"], QT[:], r=["QT"], w=["d_QT"])
                    dma("sp", dbgo["d_hT"], hT[:], r=["hT"], w=["d_hT"])
                    dma("sp", dbgo["d_wA"], wA[:], r=["wA"], w=["d_wA"])
                    dma("sp", dbgo["d_kf"], kf[:], r=["kf"], w=["d_kf"])
                    dma("sp", dbgo["d_KT"], KT[:], r=["KT"], w=["d_KT"])
                    dma("sp", dbgo["d_ge"], ge[:], r=["ge"], w=["d_ge"])
                    dma("sp", dbgo["d_gm"], gm[:], r=["gm"], w=["d_gm"])
                    dma("sp", dbgo["d_m8"], m8[:], r=["m8"], w=["d_m8"])
                ckpt("a3")
                for pr in range(2):
                    for c in range(4):
                        ntk = 4 * c + 4
                        cs_ = slice(c * 512, (c + 1) * 512)
                        n_mm = 2 * ntk
                        cnt = 0
                        ob_, db_ = ((3, 2), (1, 0))[(pr * 4 + c) % 2]
                        steps = [(eo, 2 * pr + eo, i) for eo in range(2) for i in range(ntk)]
                        SK = 3
                        pts = {}
                        for s_ in range(n_mm + SK):
                            if s_ < n_mm:
                                eo, hl, i = steps[s_]
                                bi_ = 4 + sring % 3; sb = pb[bi_]; sk = "pb%d" % bi_; sring += 1
                                mm(sb[:], KT[0:74, hl, i * 128:(i + 1) * 128], QT[0:74, hl, cs_], True, True, r=["KT", "QT"], w=[sk])
                                if i >= 4 * c:
                                    j0 = i - 4 * c
                                    tt("dve", sb[:, j0 * 128:(j0 + 1) * 128], sb[:, j0 * 128:(j0 + 1) * 128], negtri[:], ALU.add, r=[sk, "negtri"], w=[sk])
                                    if i % 2 == 1:
                                        tt("dve", sb[:, (j0 - 1) * 128:j0 * 128], sb[:, (j0 - 1) * 128:j0 * 128], negfull[:], ALU.add, r=[sk, "negfull"], w=[sk])
                                pt = PT[pring % 6]; ptk = "PT%d" % (pring % 6); pring += 1
                                act(pt[:], sb[:], AF.Exp, r=[sk], w=[ptk])
                                pts[s_] = (pt, ptk)
                            if s_ >= SK:
                                eo, hl, i = steps[s_ - SK]
                                pt, ptk = pts.pop(s_ - SK)
                                mm(pb[ob_][:], Vp[:, i, hl, :], pt[:], cnt == 0, cnt == n_mm - 1, r=["Vp", ptk], w=["pb%d" % ob_])
                                mm(pb[db_][:], onesp[:, eo, :], pt[:], cnt == 0, cnt == n_mm - 1, r=["onesp", ptk], w=["pb%d" % db_])
                                cnt += 1
                        A("dve", lambda e, db_=db_: e.reciprocal(out=rec[:], in_=pb[db_][:]), r=["pb%d" % db_], w=["rec"])
                        if dbg and "last" in dbg and gI == 1 and pr == 1 and c == 3:
                            dbt = T(ph, "dbt", [128, 512])
                            cp("dve", dbt[:], pb[3][:], r=["pb3"], w=["dbt"])
                            dma("sp", dbgo["d_O"], dbt[:], r=["dbt"], w=["d_O"])
                            dma("sp", dbgo["d_rec"], rec[:], r=["rec"], w=["d_rec"])
                            dma("sp", dbgo["d_Vp"], Vp[:], r=["Vp"], w=["d_Vp"])
                            dma("sp", dbgo["d_ones"], onesp[:], r=["onesp"], w=["d_ones"])
                        tt("dve", oTa[:, 2 * gI + pr, cs_], pb[ob_][:], rec[:], ALU.mult, r=["pb%d" % ob_, "rec"], w=["oTa"])

        sc.barrier()
        if dbg and "oTa" in dbg:
            dma("sp", dbgo["d_oTa"], oTa[:], r=["oTa"], w=["d_oTa"])
        ckpt("attn")
        xo = T(mix2, "xo", [128, 16, 1024], BF16)
        Btok = T(mix2, "Btok", [128, 16, 256], BF16)
        BT = T(mix2, "BT", [128, 2, S], BF16); CT = T(mix2, "CT", [128, 2, S], BF16)
        dtt = T(mix2, "dtt", [128, 16, 16]); dAx = T(mix2, "dAx", [128, 16, 64])
        onec = T(mix2, "onec", [128, 1])
        A("dve", lambda e: e.memset(onec[:], 1.0), w=["onec"])
        with ES() as ph:
            wB = T(ph, "wB", [128, 8, 1536], BF16)
            scw = T(ph, "scw", [128, 12, 4]); scb = T(ph, "scb", [128, 12])
            xpre = T(ph, "xpre", [128, 3 + S]); xcv = T(ph, "xcv", [128, S]); xact = T(ph, "xact", [128, S], BF16)
            zt = [T(ph, "zt%d" % i, [128, 512]) for i in range(2)]
            dtb = T(ph, "dtb", [128, 16]); ab = T(ph, "ab", [128, 16])
            cast_load(wB[:].rearrange("p k c -> p (k c)"), din["w_inBx"], 8 * 1536, ["wB"])
            wBz_t = T(ph, "wBz", [128, 8, 1040], BF16)
            wBz = wBz_t[:]
            cast_load(wBz_t[:].rearrange("p k c -> p (k c)"), din["w_inBz"], 8 * 1040, ["wBz"])
            dma("sp", scw[:], din["scw"], w=["scw"]); dma("sp", scb[:], din["scb"], w=["scb"])
            dma("sp", dtb[:], din["ssm_dt_bias"].partition_broadcast(128), w=["dtb"])
            dma("sp", ab[:], din["ssm_a_log"].partition_broadcast(128), w=["ab"])
            act(ab[:], ab[:], AF.Exp, r=["ab"], w=["ab"])
            ts("dve", ab[:], ab[:], -1.0, None, ALU.mult, r=["ab"], w=["ab"])
            xpres = [xpre, T(ph, "xpreB", [128, 3 + S])]
            for i_ in range(2):
                A("dve", lambda e, i_=i_: e.memset(xpres[i_][:, 0:3], 0.0), w=["xpre%d" % i_])

            def projA(fc):
                xp = xpres[fc % 2]; xk_ = "xpre%d" % (fc % 2)
                for tb in range(4):
                    p_ = pb[tb % 2]; pk = "pb%d" % (tb % 2)
                    for k in range(8):
                        mm(p_[:], wB[:, k, fc * 128:(fc + 1) * 128], hT[:, k, tb * 512:(tb + 1) * 512], k == 0, k == 7, r=["wB", "hT"], w=[pk])
                    cp("act", xp[:, 3 + tb * 512:3 + (tb + 1) * 512], p_[:], r=[pk], w=[xk_])

            def projB(fc):
                xp = xpres[fc % 2]; xk_ = "xpre%d" % (fc % 2)
                ts("dve", xcv[:], xp[:, 3:3 + S], scw[:, fc, 3:4], scb[:, fc:fc + 1], ALU.mult, ALU.add, r=[xk_, "scw", "scb"], w=["xcv"])
                for j in (2, 1, 0):
                    stt("dve", xcv[:], xp[:, j:j + S], scw[:, fc, j:j + 1], xcv[:], ALU.mult, ALU.add, r=[xk_, "xcv", "scw"], w=["xcv"])
                if fc < 8:
                    dst, dk = xact[:], "xact"
                elif fc < 10:
                    dst, dk = BT[:, fc - 8, :], "BT"
                else:
                    dst, dk = CT[:, fc - 10, :], "CT"
                act(dst, xcv[:], AF.Silu, r=["xcv"], w=[dk])
                if fc < 10:
                    for half in range(2):
                        for i in range(8):
                            t_ = half * 8 + i
                            sl = slice(t_ * 128, (t_ + 1) * 128)
                            srcap = xact[:, sl] if fc < 8 else BT[:, fc - 8, sl]
                            A("pe", lambda e, i=i, srcap=srcap: e.transpose(pT[:, i, :], srcap, identb[:]), r=[dk, "identb"], w=["pT"])
                        if fc < 8:
                            cp("act", xo[:, half * 8:(half + 1) * 8, fc * 128:(fc + 1) * 128], pT[:], r=["pT"], w=["xo%d" % t for t in range(half * 8, half * 8 + 8)])
                        else:
                            cp("act", Btok[:, half * 8:(half + 1) * 8, (fc - 8) * 128:(fc - 7) * 128], pT[:], r=["pT"], w=["Btok"])

            projA(0)
            for fc in range(12):
                if fc + 1 < 12:
                    projA(fc + 1)
                projB(fc)
            for t_ in range(16):
                for half in range(2):
                    p_ = pb[2 + half]; pk = "pb%d" % (2 + half)
                    for k in range(8):
                        mm(p_[:], hT[:, k, t_ * 128:(t_ + 1) * 128], wBz[:, k, half * 512:(half + 1) * 512], k == 0, k == 7, r=["wBz", "hT"], w=[pk])
                    act(zt[half][:], p_[:], AF.Silu, r=[pk], w=["zt%d" % half])
                    dma("sp", zss[t_ * 128:(t_ + 1) * 128, half * 512:(half + 1) * 512], zt[half][:], r=["zt%d" % half], w=["zss%d" % t_])
            for t_ in range(16):
                for k in range(8):
                    mm(pb[4][:, t_ * 16:(t_ + 1) * 16], hT[:, k, t_ * 128:(t_ + 1) * 128], wBz[:, k, 1024:1040], k == 0, k == 7, r=["wBz", "hT"], w=["pb4"])
            tt("dve", dtt[:], pb[4][:, 0:256].rearrange("p (a b) -> p a b", b=16), dtb[:].unsqueeze(1).to_broadcast([128, 16, 16]), ALU.add, r=["pb4", "dtb"], w=["dtt"])
            act(dtt[:], dtt[:], AF.Exp, r=["dtt"], w=["dtt"])
            act(dtt[:], dtt[:], AF.Ln, r=["dtt", "onec"], w=["dtt"], bias=onec[:, 0:1], scale=1.0)
            A("dve", lambda e: e.memset(dAx[:], 0.0), w=["dAx"])
            tt("dve", dAx[:, :, 0:16], dtt[:], ab[:].unsqueeze(1).to_broadcast([128, 16, 16]), ALU.mult, r=["dtt", "ab", "dAx"], w=["dAx"])
            ts("dve", dAx[:, :, 32:48], dAx[:, :, 0:16], -1.0, None, ALU.mult, r=["dAx"], w=["dAx"])

        sc.barrier()
        ckpt("ssmin")
        with ES() as ph:
            tinc = T(ph, "tinc", [128, 2, 256]); onesf = T(ph, "onesf", [128, 128]); sel = T(ph, "sel", [64, 16])
            HI = T(ph, "HI", [64, 256], BF16); LO = T(ph, "LO", [64, 256], BF16)
            R1 = T(ph, "R1", [64, 256], BF16); R2 = T(ph, "R2", [64, 256], BF16); L1 = T(ph, "L1", [64, 256], BF16); L2 = T(ph, "L2", [64, 256], BF16)
            Lh1 = [T(ph, "Lh1_%d" % i, [64, 256], BF16) for i in range(3)]; Lh2 = [T(ph, "Lh2_%d" % i, [64, 256], BF16) for i in range(3)]
            cst = T(ph, "cst", [128, 2, 16]); ecs = T(ph, "ecs", [128, 2, 16]); dec = T(ph, "dec", [128, 2, 16])
            ctb = T(ph, "ctb", [128, 16]); cdb = T(ph, "cdb", [128, 16])
            GT = T(ph, "GT", [128, 2, 384]); xdt = T(ph, "xdt", [128, 2, 1024], BF16); xdd = T(ph, "xdd", [128, 2, 1024], BF16)
            prevT = T(ph, "prevT", [128, 1024]); prevb = T(ph, "prevb", [128, 1024], BF16)
            neg384 = T(ph, "neg384", [128, 384], BF16)
            LT = [T(ph, "LT%d" % i, [128, 384]) for i in range(2)]; MT = [T(ph, "MT%d" % i, [128, 384], BF16) for i in range(3)]
            yt = [T(ph, "yt%d" % i, [128, 1024]) for i in range(4)]; ytmp = T(ph, "ytmp", [128, 1024])
            zsb = [T(ph, "zsb%d" % i, [128, 1024]) for i in range(2)]; ynb = T(ph, "ynb", [128, 1024], BF16)
            nwb = T(ph, "nwb", [128, 1024]); Db = T(ph, "Db", [128, 16])
            st6 = T(ph, "st6", [128, 2, 6]); mv = T(ph, "mv", [128, 2]); ms = T(ph, "ms", [128, 1]); rs = T(ph, "rs", [128, 1])
            for tl_, nm, key in ((tinc, "tinc", "tinc"), (onesf, "onesf", "onesf"), (sel, "sel", "sel"), (R1, "rbase0b", "R1"), (R2, "rbase0b", "R2"),
                                 (L1, "lbase0b", "L1"), (L2, "lbase0b", "L2"), (neg384, "neg384", "neg384")):
                dma("sp", tl_[:], din[nm], w=[key])
            bload(nwb, din["ssm_norm_w"], 1024, ["nwb"])
            dma("sp", Db[:], din["ssm_d"].partition_broadcast(128), w=["Db"])
            A("dve", lambda e: e.memset(prevT[:], 0.0), w=["prevT"])
            GTs = [GT, T(ph, "GTB", [128, 2, 384])]; xdts = [xdt, T(ph, "xdtB", [128, 2, 1024], BF16)]
            R1s = [R1, T(ph, "R1B", [64, 256], BF16)]; R2s = [R2, T(ph, "R2B", [64, 256], BF16)]
            L1s = [L1, T(ph, "L1B", [64, 256], BF16)]; L2s = [L2, T(ph, "L2B", [64, 256], BF16)]
            for tl_, nm, key in ((R1s[1], "rbase0b", "R1_1"), (R2s[1], "rbase0b", "R2_1"), (L1s[1], "lbase0b", "L1_1"), (L2s[1], "lbase0b", "L2_1")):
                dma("sp", tl_[:], din[nm], w=[key])

            def prologue(c):
                par = c % 2; P_ = "_%d" % par if par else ""
                t0, t1 = 2 * c, 2 * c + 1
                tl2 = (t0, t1)
                csl = slice(c * 256, (c + 1) * 256)
                yb = par * 2
                GT_, xdt_, R1_, R2_, L1_, L2_ = GTs[par], xdts[par], R1s[par], R2s[par], L1s[par], L2s[par]
                kR1, kR2, kL1, kL2, kGT, kxdt = ("R1" + P_, "R2" + P_, "L1" + P_, "L2" + P_, "GT%d" % par, "xdt%d" % par)
                mm(pb[0][0:64, 0:256], dAx[:, t0, :], tinc[:, 0, :], True, False, r=["dAx", "tinc", "onesf", "sel", "neg384"], w=["pb0"])
                mm(pb[0][0:64, 0:256], dAx[:, t1, :], tinc[:, 1, :], False, True, r=["dAx", "tinc", "onesf", "sel", "neg384"], w=["pb0"])
                mm(pb[0][:, 256:272], tinc[:, 0, 0:128], dAx[:, t0, 0:16], True, True, r=["dAx"], w=["pb0"])
                mm(pb[0][:, 272:288], tinc[:, 0, 128:256], dAx[:, t0, 0:16], True, False, r=["dAx"], w=["pb0"])
                mm(pb[0][:, 272:288], tinc[:, 1, 128:256], dAx[:, t1, 0:16], False, True, r=["dAx"], w=["pb0"])
                mm(pb[0][:, 288:304], onesf[:], dAx[:, t0, 0:16], True, False, r=["dAx"], w=["pb0"])
                mm(pb[0][:, 288:304], onesf[:], dAx[:, t1, 0:16], False, True, r=["dAx"], w=["pb0"])
                cp("act", HI[:], pb[0][0:64, 0:256], r=["pb0"], w=["HI"])
                cp("act", cst[:].rearrange("p a b -> p (a b)"), pb[0][:, 256:288], r=["pb0"], w=["cst"])
                cp("act", ctb[:], pb[0][:, 288:304], r=["pb0"], w=["ctb"])
                tt("dve", LO[:], pb[0][0:64, 0:256], HI[:], ALU.subtract, r=["pb0", "HI", "cst", "ctb"], w=["LO"])
                cp("act", R1_[0:16, :], HI[0:16, :], r=["HI", "tinc", "onesf", "sel", "neg384"], w=[kR1])
                cp("act", L1_[32:48, :], HI[32:48, :], r=["HI", "tinc", "onesf", "sel", "neg384"], w=[kL1])
                cp("act", R2_[0:16, :], LO[0:16, :], r=["LO", "tinc", "onesf", "sel", "neg384"], w=[kR2])
                cp("act", L2_[32:48, :], LO[32:48, :], r=["LO", "tinc", "onesf", "sel", "neg384"], w=[kL2])
                act(ecs[:], cst[:], AF.Exp, r=["cst"], w=["ecs"])
                tt("dve", dec[:], ctb[:].unsqueeze(1).to_broadcast([128, 2, 16]), cst[:], ALU.subtract, r=["ctb", "cst"], w=["dec"])
                act(dec[:], dec[:], AF.Exp, r=["dec"], w=["dec"])
                act(cdb[:], ctb[:], AF.Exp, r=["ctb"], w=["cdb"])
                for g in range(2):
                    mm(pb[1][:, 0:256], BT[:, g, t0 * 128:(t0 + 1) * 128], CT[:, g, csl], True, True, r=["BT", "CT"], w=["pb1"])
                    mm(pb[1][:, 256:384], BT[:, g, t1 * 128:(t1 + 1) * 128], CT[:, g, t1 * 128:(t1 + 1) * 128], True, True, r=["BT", "CT"], w=["pb1"])
                    cp("act", GT_[:, g, :], pb[1][:, 0:384], r=["pb1"], w=[kGT])
                for i in range(2):
                    xv = xo[:, tl2[i], :].rearrange("p (h d) -> p h d", d=64)
                    tt("dve", xdt_[:, i, :].rearrange("p (h d) -> p h d", d=64), xv, dtt[:, tl2[i], :].unsqueeze(2).to_broadcast([128, 16, 64]), ALU.mult, r=["xo%d" % tl2[i], "dtt"], w=[kxdt])
                    tt("dve", xdd[:, i, :].rearrange("p (h d) -> p h d", d=64), xdt_[:, i, :].rearrange("p (h d) -> p h d", d=64), dec[:, i, :].unsqueeze(2).to_broadcast([128, 16, 64]), ALU.mult, r=[kxdt, "dec"], w=["xdd"])
                for i in range(2):
                    if c == 0:
                        A("dve", lambda e, i=i, yb=yb: e.memset(yt[yb + i][:], 0.0), w=["yt%d" % (yb + i)])
                        continue
                    for g in range(2):
                        bq = 2 - g; bqk = "pb%d" % bq
                        mm(pb[bq][:], CT[:, g, tl2[i] * 128:(tl2[i] + 1) * 128], prevb[:, g * 512:(g + 1) * 512], True, True, r=["CT", "prevb"], w=[bqk])
                        tt("dve", yt[yb + i][:, g * 512:(g + 1) * 512].rearrange("p (h d) -> p h d", d=64), pb[bq][:].rearrange("p (h d) -> p h d", d=64),
                           ecs[:, i, g * 8:(g + 1) * 8].unsqueeze(2).to_broadcast([128, 8, 64]), ALU.mult, r=[bqk, "ecs"], w=["yt%d" % (yb + i)])
                if c < 7:
                    for g in range(2):
                        gs = slice(g * 512, (g + 1) * 512)
                        bq = 2 - g; bqk = "pb%d" % bq
                        mm(pb[bq][:], Btok[:, t0, g * 128:(g + 1) * 128], xdd[:, 0, gs], True, False, r=["Btok", "xdd"], w=[bqk])
                        mm(pb[bq][:], Btok[:, t1, g * 128:(g + 1) * 128], xdd[:, 1, gs], False, True, r=["Btok", "xdd"], w=[bqk])
                        tt("dve", prevT[:, gs].rearrange("p (h d) -> p h d", d=64), prevT[:, gs].rearrange("p (h d) -> p h d", d=64),
                           cdb[:, g * 8:(g + 1) * 8].unsqueeze(2).to_broadcast([128, 8, 64]), ALU.mult, r=["prevT", "cdb"], w=["prevT"])
                        tt("dve", prevT[:, gs], prevT[:, gs], pb[bq][:], ALU.add, r=["prevT", bqk], w=["prevT"])
                        cp("dve", prevb[:, gs], prevT[:, gs], r=["prevT"], w=["prevb"])

            ms2 = T(ph, "ms2", [128, 2]); rs2 = T(ph, "rs2", [128, 2])

            def combine(cc, i):
                t_ = 2 * cc + i
                yi = (cc % 2) * 2 + i; yk = "yt%d" % yi
                z_ = zsb[i]; zk = "zsb%d" % i
                dma("sp", z_[:], zss[t_ * 128:(t_ + 1) * 128, :], r=["zss%d" % t_], w=[zk])
                tt("dve", ytmp[:].rearrange("p (h d) -> p h d", d=64), xo[:, t_, :].rearrange("p (h d) -> p h d", d=64),
                   Db[:].unsqueeze(2).to_broadcast([128, 16, 64]), ALU.mult, r=["xo%d" % t_, "Db"], w=["ytmp"])
                tt("dve", yt[yi][:], yt[yi][:], ytmp[:], ALU.add, r=[yk, "ytmp"], w=[yk])
                tt("dve", yt[yi][:], yt[yi][:], z_[:], ALU.mult, r=[yk, zk], w=[yk])
                for g in range(2):
                    for q_ in range(2):
                        A("dve", lambda e, yi=yi, g=g, q_=q_: e.bn_stats(out=st6[:, q_, :], in_=yt[yi][:, g * 512 + q_ * 256:g * 512 + (q_ + 1) * 256]), r=[yk], w=["st6"])
                    A("dve", lambda e: e.bn_aggr(out=mv[:], in_=st6[:]), r=["st6"], w=["mv"])
                    stt("dve", ms2[:, g:g + 1], mv[:, 0:1], mv[:, 0:1], mv[:, 1:2], ALU.mult, ALU.add, r=["mv"], w=["ms2"])

            def combineN(cc, i):
                yi = (cc % 2) * 2 + i; yk = "yt%d" % yi
                act(rs2[:], ms2[:], AF.Sqrt, r=["ms2", "eps"], w=["rs2"], bias=epsln[:, 0:1], scale=1.0)
                A("dve", lambda e: e.reciprocal(out=rs2[:], in_=rs2[:]), r=["rs2"], w=["rs2"])
                for g in range(2):
                    gs = slice(g * 512, (g + 1) * 512)
                    stt("dve", ynb[:, gs], yt[yi][:, gs], rs2[:, g:g + 1], nwb[:, gs], ALU.mult, ALU.mult, r=[yk, "rs2", "nwb"], w=["ynb"])

            def combineT(cc, i):
                t_ = 2 * cc + i
                for fc in range(8):
                    A("pe", lambda e, fc=fc: e.transpose(pT[:, fc, :], ynb[:, fc * 128:(fc + 1) * 128], identb[:]), r=["ynb", "identb"], w=["pT"])
                cp("act", xo[:, t_, :].rearrange("p (f k) -> p f k", k=128), pT[:], r=["pT"], w=["xo%d" % t_])

            def heads(c):
                par = c % 2; P_ = "_%d" % par if par else ""
                yb = par * 2
                GT_, xdt_, R1_, R2_, L1_, L2_ = GTs[par], xdts[par], R1s[par], R2s[par], L1s[par], L2s[par]
                kR1, kR2, kL1, kL2, kGT, kxdt = ("R1" + P_, "R2" + P_, "L1" + P_, "L2" + P_, "GT%d" % par, "xdt%d" % par)

                def mask(h):
                    act(Lh1[h % 3][:], L1_[:], AF.Copy, r=[kL1, "tinc", "onesf", "sel", "neg384"], w=["Lh1_%d" % (h % 3)], scale=sel[:, h:h + 1])
                    act(Lh2[h % 3][:], L2_[:], AF.Copy, r=[kL2, "tinc", "onesf", "sel", "neg384"], w=["Lh2_%d" % (h % 3)], scale=sel[:, h:h + 1])

                def dexp(h):
                    a_ = Lh1[h % 3]; b_ = Lh2[h % 3]; ak_ = "Lh1_%d" % (h % 3); bk_ = "Lh2_%d" % (h % 3)
                    Dp = pb[5 + h % 2]; dk_ = "pb%d" % (5 + h % 2)
                    mm(Dp[:, 0:256], identb[:], neg384[:, 0:256], True, False, r=["identb", "tinc", "onesf", "sel", "neg384"], w=[dk_])
                    mm(Dp[:, 0:256], a_[:, 0:128], R1_[:, 0:256], False, False, r=[ak_, kR1], w=[dk_])
                    mm(Dp[:, 0:256], b_[:, 0:128], R2_[:, 0:256], False, True, r=[bk_, kR2], w=[dk_])
                    mm(Dp[:, 256:384], identb[:], neg384[:, 256:384], True, False, r=["identb"], w=[dk_])
                    mm(Dp[:, 256:384], a_[:, 128:256], R1_[:, 128:256], False, False, r=[ak_, kR1], w=[dk_])
                    mm(Dp[:, 256:384], b_[:, 128:256], R2_[:, 128:256], False, True, r=[bk_, kR2], w=[dk_])
                    act(LT[h % 2][:], Dp[:, 0:384], AF.Exp, r=[dk_], w=["LT%d" % (h % 2)])

                def mmul(h):
                    tt("dve", MT[h % 3][:], LT[h % 2][:], GT_[:, h // 8, :], ALU.mult, r=["LT%d" % (h % 2), kGT], w=["MT%d" % (h % 3)])

                def ydiag(h):
                    g = h // 8; hh = h % 8
                    mt = MT[h % 3]; mtk = "MT%d" % (h % 3)
                    hs = slice(h * 64, (h + 1) * 64)
                    mm(pb[3][:, hh * 64:(hh + 1) * 64], mt[:, 0:128], xdt_[:, 0, hs], True, True, r=[mtk, kxdt], w=["pb3"])
                    mm(pb[4][:, hh * 64:(hh + 1) * 64], mt[:, 128:256], xdt_[:, 0, hs], True, False, r=[mtk, kxdt], w=["pb4"])
                    mm(pb[4][:, hh * 64:(hh + 1) * 64], mt[:, 256:384], xdt_[:, 1, hs], False, True, r=[mtk, kxdt], w=["pb4"])
                    if hh == 7:
                        for i in range(2):
                            gs = slice(g * 512, (g + 1) * 512)
                            tt("dve", yt[yb + i][:, gs], yt[yb + i][:, gs], pb[3 + i][:], ALU.add, r=["yt%d" % (yb + i), "pb%d" % (3 + i)], w=["yt%d" % (yb + i)])

                mask(0)
                for s_ in range(18):
                    if s_ + 1 < 16:
                        mask(s_ + 1)
                    if s_ < 16:
                        dexp(s_)
                    if 1 <= s_ <= 16:
                        mmul(s_ - 1)
                    if 2 <= s_:
                        ydiag(s_ - 2)
                    if c >= 1:
                        if s_ == 0:
                            combine(c - 1, 0)
                        if s_ == 3:
                            combineN(c - 1, 0)
                        if s_ == 5:
                            combineT(c - 1, 0)
                            combine(c - 1, 1)
                        if s_ == 8:
                            combineN(c - 1, 1)
                        if s_ == 10:
                            combineT(c - 1, 1)
                    if s_ == 9 and c + 1 < 8:
                        prologue(c + 1)

            prologue(0)
            for c in range(8):
                heads(c)
            combine(7, 0)
            combineN(7, 0)
            combineT(7, 0)
            combine(7, 1)
            combineN(7, 1)
            combineT(7, 1)

        sc.barrier()
        if dbg and "xo" in dbg:
            dma("sp", dbgo["d_xo"], xo[:], r=["xo%d" % t for t in range(16)], w=["d_xo"])
        ckpt("ssm")

        def layer_norm(tl, tlk, gbt, bbt, gk, o_t, ok, st8, mv2, rstd):
            for q_ in range(4):
                A("dve", lambda e, q_=q_: e.bn_stats(out=st8[:, q_, :], in_=tl[:, q_ * 256:(q_ + 1) * 256]), r=[tlk], w=["st8"])
            A("dve", lambda e: e.bn_aggr(out=mv2[:], in_=st8[:]), r=["st8"], w=["mv2"])
            act(rstd[:], mv2[:, 1:2], AF.Sqrt, r=["mv2", "eps"], w=["rstd"], bias=epsln[:, 0:1], scale=1.0)
            A("dve", lambda e: e.reciprocal(out=rstd[:], in_=rstd[:]), r=["rstd"], w=["rstd"])
            ts("dve", tl[:], tl[:], mv2[:, 0:1], rstd[:, 0:1], ALU.subtract, ALU.mult, r=[tlk, "mv2", "rstd"], w=[tlk])
            tt("dve", tl[:], tl[:], gbt[:], ALU.mult, r=[tlk, gk], w=[tlk])
            tt("dve", o_t[:], tl[:], bbt[:], ALU.add, r=[tlk, gk], w=[ok])

        with ES() as ph:
            wO = T(ph, "wO", [128, 12, 1024], BF16)
            g1t = T(ph, "g1t", [128, 1024]); b1t = T(ph, "b1t", [128, 1024])
            xin = [T(ph, "xin%d" % i, [128, 1024]) for i in range(2)]
            tl = T(ph, "tl", [128, 1024]); x1t = [T(ph, "x1t%d" % i, [128, 1024]) for i in range(2)]
            h2f = T(ph, "h2f", [128, 1024]); h2b = T(ph, "h2b", [128, 1024], BF16)
            st8 = T(ph, "st8", [128, 4, 6]); mv2 = T(ph, "mv2", [128, 2]); rstd = T(ph, "rstd", [128, 1])
            for pc in range(6):
                dma("pool", wO[:].rearrange("p j c -> p (j c)")[:, pc * 2048:(pc + 1) * 2048],
                    din["w_out"].rearrange("p j c -> p (j c)")[:, pc * 2048:(pc + 1) * 2048], w=["wO%d" % pc])
            bload(g1t, din["ln1_g"], 1024, ["ln1"])
            bload(b1t, din["ln1_b"], 1024, ["ln1"])
            tt("dve", h2f[:], b1t[:], modB[:, 2048:3072], ALU.mult, r=["ln1", "modB"], w=["h2f"])
            tt("dve", modB[:, 1024:2048], modB[:, 1024:2048], h2f[:], ALU.add, r=["modB", "h2f"], w=["modB"])
            tt("dve", modB[:, 2048:3072], modB[:, 2048:3072], g1t[:], ALU.mult, r=["modB", "ln1"], w=["modB"])

            def mm6(t_):
                ts_ = slice(t_ * 128, (t_ + 1) * 128)
                dma("sp", xin[t_ % 2][:], din["x_tok"][ts_, :], w=["xin%d" % (t_ % 2)])
                for half in range(2):
                    hs = slice(half * 512, (half + 1) * 512)
                    bi = 2 * (t_ % 2) + half
                    for m in range(12):
                        if m < 4:
                            l_ = oTa[:, m, ts_]; lk_ = "oTa"
                        else:
                            l_ = xo[:, t_, :].rearrange("p (f k) -> p f k", k=128)[:, m - 4, :]; lk_ = "xo%d" % t_
                        mm(pb[bi][:], l_, wO[:, m, hs], m == 0, m == 11, r=[lk_, "wO%d" % (m // 2)], w=["pb%d" % bi])

            def evac6(t_):
                ts_ = slice(t_ * 128, (t_ + 1) * 128)
                xi = xin[t_ % 2]; xik = "xin%d" % (t_ % 2); tl_ = tls[t_ % 2]; tlk = "tl%d" % (t_ % 2)
                for half in range(2):
                    hs = slice(half * 512, (half + 1) * 512)
                    bi = 2 * (t_ % 2) + half
                    tt("dve", tl_[:, hs], pb[bi][:], modB[:, half * 512:(half + 1) * 512], ALU.mult, r=["pb%d" % bi, "modB"], w=[tlk])
                stt("dve", tl_[:], xi[:], ALPHA, tl_[:], ALU.mult, ALU.add, r=[xik, tlk], w=[tlk])

            def norm6(t_):
                ts_ = slice(t_ * 128, (t_ + 1) * 128)
                tl_ = tls[t_ % 2]; tlk = "tl%d" % (t_ % 2); xo_ = x1t[t_ % 2]; xok = "x1t%d" % (t_ % 2)
                for q_ in range(4):
                    A("dve", lambda e, q_=q_, tl_=tl_, st8=st8: e.bn_stats(out=st8[:, q_, :], in_=tl_[:, q_ * 256:(q_ + 1) * 256]), r=[tlk], w=["st8"])
                A("dve", lambda e, st8=st8, mv2=mv2: e.bn_aggr(out=mv2[:], in_=st8[:]), r=["st8"], w=["mv2"])
                act(rstd[:], mv2[:, 1:2], AF.Sqrt, r=["mv2", "eps"], w=["rstd"], bias=epsln[:, 0:1], scale=1.0)
                A("dve", lambda e, rstd=rstd: e.reciprocal(out=rstd[:], in_=rstd[:]), r=["rstd"], w=["rstd"])
                ts("dve", tl_[:], tl_[:], mv2[:, 0:1], rstd[:, 0:1], ALU.subtract, ALU.mult, r=[tlk, "mv2", "rstd"], w=[tlk])
                tt("dve", h2f[:], tl_[:], modB[:, 2048:3072], ALU.mult, r=[tlk, "modB"], w=["h2f"])
                tt("dve", h2b[:], h2f[:], modB[:, 1024:2048], ALU.add, r=["h2f", "modB"], w=["h2b"])
                tt("dve", xo_[:], tl_[:], g1t[:], ALU.mult, r=[tlk, "ln1"], w=[xok])
                tt("dve", xo_[:], xo_[:], b1t[:], ALU.add, r=[xok, "ln1"], w=[xok])
                dma("sp", x1s[ts_, :], xo_[:], r=[xok], w=["x1s%d" % t_])
                if dbg and "x1" in dbg:
                    dma("sp", dbgo["d_x1"][ts_, :], xo_[:], r=[xok], w=["d_x1"])

            def tr6(t_):
                ts_ = slice(t_ * 128, (t_ + 1) * 128)
                for k in range(8):
                    A("pe", lambda e, k=k: e.transpose(pT[:, k, :], h2b[:, k * 128:(k + 1) * 128], identb[:]), r=["h2b", "identb"], w=["pT"])
                cp("act", hT[:, :, ts_], pT[:], r=["pT"], w=["hT"])

            tls = [tl, T(ph, "tlB", [128, 1024])]
            mm6(0)
            evac6(0)
            for t_ in range(16):
                if t_ + 1 < 16:
                    mm6(t_ + 1)
                norm6(t_)
                if t_ + 1 < 16:
                    evac6(t_ + 1)
                tr6(t_)
        mix2.close()
        sc.barrier()
        ckpt("ln1")

        with ES() as ph:
            wD = T(ph, "wD", [128, NJ, 1024], BF16); aT = T(ph, "aT", [128, NJ, 1024], BF16)
            wU = [T(ph, "wU%d" % i, [128, 8, 256], BF16) for i in range(2)]
            ub = [[T(ph, "ub%d%d" % (gv, i), [128, 1026]) for i in range(2)] for gv in range(2)]
            cv = [[T(ph, "cv%d%d" % (gv, i), [128, 512]) for i in range(2)] for gv in range(2)]
            sg = [T(ph, "sg%d" % i, [128, 512]) for i in range(2)]
            halo = T(ph, "halo", [128, NJ, 2, 2]); fcw = T(ph, "fcw", [128, NJ, 2, 3]); fcb = T(ph, "fcb", [128, NJ, 2])
            g2t = T(ph, "g2t", [128, 1024]); b2t = T(ph, "b2t", [128, 1024])
            x1i = [T(ph, "x1i%d" % i, [128, 1024]) for i in range(2)]; tl = T(ph, "tl2", [128, 1024])
            ot = [T(ph, "ot%d" % i, [128, 1024]) for i in range(2)]
            st8 = T(ph, "st8b", [128, 4, 6]); mv2 = T(ph, "mv2b", [128, 2]); rstd = T(ph, "rstdb", [128, 1])
            tls7 = [tl, T(ph, "tl2B", [128, 1024])]
            dma("sp", fcw[:], din["fcw"], w=["fcw"]); dma("sp", fcb[:], din["fcb"], w=["fcb"])
            bload(g2t, din["ln2_g"], 1024, ["ln2"])
            bload(b2t, din["ln2_b"], 1024, ["ln2"])
            if not hasattr(sc, "groups"):
                sc.groups = {}
            sc.groups["wD"] = ["wD#%d" % i for i in range(11)]
            for blk in range(2):
                for j in range(NJ):
                    w_ = wU[j % 2]; wk = "wU%d" % (j % 2)
                    def load_wu(jj):
                        dma("pool", wU[jj % 2][:].rearrange("p k c -> p (k c)"), din["w_up"][jj].rearrange("p k c -> p (k c)"), w=["wU%d" % (jj % 2)])
                    if blk == 0 and j == 0:
                        load_wu(0)
                    jn = (j + 1) % NJ
                    if not (blk == 1 and j == NJ - 1):
                        load_wu(jn)
                    if blk == 0 and 1 <= j <= 11:
                        c0 = (j - 1) * 2048
                        dma("pool", wD[:].rearrange("p j c -> p (j c)")[:, c0:c0 + 2048], din["w_dn"].rearrange("p j c -> p (j c)")[:, c0:c0 + 2048], w=["wD#%d" % (j - 1)])
                    us = [ub[gv][j % 2] for gv in range(2)]
                    uks = ["ub%d%d" % (gv, j % 2) for gv in range(2)]
                    for gv in range(2):
                        if blk == 0:
                            A("dve", lambda e, u_=us[gv]: e.memset(u_[:, 0:2], 0.0), w=[uks[gv] + "h"])
                        else:
                            cp("pool", us[gv][:, 0:2], halo[:, j, gv, :], r=["halo%d_%d" % (j, gv)], w=[uks[gv] + "h"])
                    for hf in range(2):
                        c0 = blk * 1024 + hf * 512
                        for gv in range(2):
                            p_ = pb[gv * 2 + hf]; pk = "pb%d" % (gv * 2 + hf)
                            for k in range(8):
                                mm(p_[:], w_[:, k, gv * 128:(gv + 1) * 128], hT[:, k, c0:c0 + 512], k == 0, k == 7, r=[wk, "hT"], w=[pk])
                            cp("act", us[gv][:, 2 + hf * 512:2 + (hf + 1) * 512], p_[:], r=[pk], w=[uks[gv] + "ab"[hf]])
                            act(cv[gv][hf][:], p_[:], AF.Identity, r=[pk, "fcw", "fcb"], w=["cv%d%d" % (gv, hf)],
                                scale=fcw[:, j, gv, 2:3], bias=fcb[:, j, gv:gv + 1])
                    for hf in range(2):
                        for tap in (1, 0):
                            for gv in range(2):
                                rk = [uks[gv] + "a", "cv%d%d" % (gv, hf), "fcw"] + ([uks[gv] + "h"] if hf == 0 else [uks[gv] + "b"])
                                stt("dve", cv[gv][hf][:], us[gv][:, tap + hf * 512:tap + hf * 512 + 512], fcw[:, j, gv, tap:tap + 1], cv[gv][hf][:],
                                    ALU.mult, ALU.add, r=rk, w=["cv%d%d" % (gv, hf)])
                    for hf in range(2):
                        act(sg[hf][:], cv[0][hf][:], AF.Silu, r=["cv0%d" % hf], w=["sg%d" % hf])
                    for hf in range(2):
                        tt("dve", aT[:, j, hf * 512:(hf + 1) * 512], sg[hf][:], cv[1][hf][:], ALU.mult, r=["sg%d" % hf, "cv1%d" % hf], w=["aT"])
                    if blk == 0:
                        for gv in range(2):
                            cp("pool", halo[:, j, gv, :], us[gv][:, 1024:1026], r=[uks[gv] + "b"], w=["halo%d_%d" % (j, gv)])
                def front7(t8):
                    t_ = blk * 8 + t8
                    ts_ = slice(t_ * 128, (t_ + 1) * 128)
                    tl_ = tls7[t8 % 2]; tlk = "tl7%d" % (t8 % 2)
                    dma("sp", x1i[t8 % 2][:], x1s[ts_, :], r=["x1s%d" % t_], w=["x1i%d" % (t8 % 2)])
                    for half in range(2):
                        hs = slice(half * 512, (half + 1) * 512)
                        for j in range(NJ):
                            mm(pb[4 + half][:], aT[:, j, t8 * 128:(t8 + 1) * 128], wD[:, j, hs], j == 0, j == NJ - 1, r=["aT", "wD"], w=["pb%d" % (4 + half)])
                        tt("dve", tl_[:, hs], pb[4 + half][:], g2p1[:, half * 512:(half + 1) * 512], ALU.mult, r=["pb%d" % (4 + half), "g2p1"], w=[tlk])

                def back7(t8):
                    t_ = blk * 8 + t8
                    ts_ = slice(t_ * 128, (t_ + 1) * 128)
                    tl_ = tls7[t8 % 2]; tlk = "tl7%d" % (t8 % 2)
                    xi = x1i[t8 % 2]; xik = "x1i%d" % (t8 % 2); o_ = ot[t8 % 2]; ok_ = "ot%d" % (t8 % 2)
                    stt("dve", tl_[:], xi[:], ALPHA, tl_[:], ALU.mult, ALU.add, r=[xik, tlk], w=[tlk])
                    layer_norm(tl_, tlk, g2t, b2t, "ln2", o_, ok_, st8, mv2, rstd)
                    dma("sp", out[ts_, :], o_[:], r=[ok_], w=["out"])

                front7(0)
                for t8 in range(8):
                    if t8 + 1 < 8:
                        front7(t8 + 1)
                    back7(t8)
        outs = [op for op in sc.ops if op.dma]
        A("sp", lambda e: e.wait_ge(outs[0].sem, 16), r=["out", "d_x1", "d_mod", "d_oTa", "d_xo", "d_QT", "d_KT", "d_O", "d_rec", "d_Vp", "d_ones", "d_hT", "d_wA", "d_kf", "d_ge", "d_gm", "d_m8"], w=["done"], force=True)
        sc.emit(nc, top)
    return nc


_CACHE = {}


def _prep(inputs):
    sh = _shared_inputs(inputs)
    sh.update(_consts())
    x = np.asarray(inputs["x"], dtype=np.float32)
    c = np.asarray(inputs["c"], dtype=np.float32)
    maps = []
    for b in range(8):
        m = dict(sh)
        m["x_tok"] = np.ascontiguousarray(x[b])
        m["xT"] = np.ascontiguousarray(x[b].T)
        m["c_col"] = np.ascontiguousarray(c[b].reshape(8, 128).T)
        maps.append(m)
    return maps


def kernel(**inputs):
    maps = _prep(inputs)
    shapes = {n: (a.shape, _DT[a.dtype]) for n, a in maps[0].items()}
    nc = build(shapes)
    res = run_bass_kernel_spmd(nc, maps, core_ids=list(range(8)))
    return np.stack([np.asarray(r["out"], dtype=np.float32) for r in res.results], axis=0)
```

```python
import contextlib
import numpy as np
import ml_dtypes
import concourse.bass as bass
import concourse.mybir as mybir
from concourse.bass_utils import run_bass_kernel_spmd

F32 = mybir.dt.float32
BF16 = mybir.dt.bfloat16
AF = mybir.ActivationFunctionType
ALU = mybir.AluOpType
AX = mybir.AxisListType

S = 2048
D = 1024
NT = S // 128
DFF = 2816
NJ = DFF // 128
NEG = -30000.0
ALPHA = 2.0 ** 0.25
STRICT = {"dve"}


class _Op:
    __slots__ = ("eng", "fn", "deps", "signal", "sem", "val", "dma", "idx", "prev_val")


class Sched:
    ENGS = ("pe", "act", "dve", "pool", "sp")

    def __init__(self):
        self.ops = []
        self.lw = {}
        self.rd = {}

    frozen = False

    def add(self, eng, fn, r=(), w=(), dma=False, force=False):
        if self.frozen and not force:
            return None
        groups = getattr(self, "groups", {})
        if groups:
            r = [m for k in r for m in groups.get(k, (k,))]
        op = _Op()
        op.eng, op.fn, op.dma, op.signal, op.idx = eng, fn, dma, dma, len(self.ops)
        deps = set()
        for k in r:
            x = self.lw.get(k)
            if x is not None:
                deps.add(x)
        for k in w:
            x = self.lw.get(k)
            if x is not None:
                deps.add(x)
            deps.update(self.rd.get(k, ()))
        for k in r:
            self.rd.setdefault(k, []).append(op.idx)
        for k in w:
            self.lw[k] = op.idx
            self.rd[k] = []
        op.deps = deps
        self.ops.append(op)
        return op

    def barrier(self):
        if self.frozen:
            return
        start = getattr(self, "_bar_from", 0)
        deps = set()
        last = {}
        for op in self.ops[start:]:
            if op.dma:
                deps.add(op.idx)
            last[op.eng] = op.idx
        deps.update(last.values())
        deps.update(getattr(self, "_bar_ops", ()))
        n0 = len(self.ops)
        bar_ops = []
        for eng in self.ENGS:
            op = self.add(eng, lambda e: e.wait_ge(self.esem["pe"], 0))
            op.deps = set(deps)
            bar_ops.append(op.idx)
        self._bar_from = n0
        self._bar_ops = bar_ops

    def emit(self, nc, stack, n_dma_sems=8):
        ops = self.ops
        for op in ops:
            for d in op.deps:
                dop = ops[d]
                if dop.eng != op.eng or dop.dma or dop.eng in STRICT:
                    dop.signal = True
        esem = {e: stack.enter_context(nc.semaphore("s_" + e)) for e in self.ENGS}
        self.esem = esem
        dsem = {e: [stack.enter_context(nc.semaphore("d_%s%d" % (e, i))) for i in range(n_dma_sems)]
                for e in ("sp", "pool", "act")}
        cnt = {e: 0 for e in self.ENGS}
        dcnt = {e: 0 for e in dsem}
        for op in ops:
            if not op.signal:
                continue
            if op.dma:
                k = dcnt[op.eng]
                dcnt[op.eng] += 1
                op.sem = dsem[op.eng][k % n_dma_sems]
                op.val = 16 * (k // n_dma_sems + 1)
            else:
                cnt[op.eng] += 1
                op.sem = esem[op.eng]
                op.val = cnt[op.eng]
        by_eng = {e: [op for op in ops if op.eng == e] for e in self.ENGS}

        def run(eng_name, e):
            waited = {}
            for op in by_eng[eng_name]:
                need = {}
                for d in op.deps:
                    dop = ops[d]
                    if not dop.signal:
                        continue
                    if dop.eng == op.eng and not dop.dma and dop.eng not in STRICT:
                        continue
                    key = id(dop.sem)
                    if key not in need or need[key][1] < dop.val:
                        need[key] = (dop.sem, dop.val)
                if op.dma and op.val > 16:
                    key = id(op.sem)
                    if key not in need or need[key][1] < op.val - 16:
                        need[key] = (op.sem, op.val - 16)
                for key, (sem, val) in need.items():
                    if waited.get(key, 0) >= val:
                        continue
                    e.wait_ge(sem, val)
                    waited[key] = val
                ins = op.fn(e)
                if op.signal:
                    ins.then_inc(op.sem, 16 if op.dma else 1)

        with nc.Block() as block:
            @block.tensor
            def _(e):
                run("pe", e)

            @block.scalar
            def _(e):
                run("act", e)

            @block.vector
            def _(e):
                run("dve", e)

            @block.gpsimd
            def _(e):
                run("pool", e)

            @block.sync
            def _(e):
                run("sp", e)


def _consts():
    c = {}
    c["identf"] = np.eye(128, dtype=np.float32)
    p = np.arange(128)
    tri = np.where(p[None, :] >= p[:, None], 0.0, NEG).astype(np.float32)
    c["negtri"] = tri
    c["negfull"] = np.full((128, 128), NEG, np.float32)
    qb = (np.arange(16) // 2)[:, None]
    kb = np.arange(8)[None, :]
    past = kb < qb
    c["pastneg"] = np.broadcast_to(np.where(past, 0.0, -1e30).astype(np.float32)[None], (128, 16, 8)).copy()
    base = np.where(kb == qb, 0.0, NEG).astype(np.float32)
    c["basetab"] = np.broadcast_to(base[None], (128, 16, 8)).copy()
    slopes = 2.0 ** (-(np.arange(8) + 1.0))
    A = np.zeros((8, 16, 8), np.float32)
    for h in range(8):
        A[h] = np.where(past, -slopes[h] * 256.0 * (qb - kb) - NEG, 0.0)
    c["atab"] = np.broadcast_to(A[None], (128, 8, 16, 8)).copy()
    t = np.arange(S)
    r = (t % 256).astype(np.float32)
    qst = np.zeros((2, 8, S), np.float32)
    kst = np.zeros((10, 8, S), np.float32)
    for h in range(8):
        qst[0, h] = 1.0
        qst[1, h] = -slopes[h] * r
        for b in range(8):
            kst[b, h] = (t // 256 == b)
        kst[8, h] = slopes[h] * r
        kst[9, h] = 1.0
    c["qstat"] = qst.astype(ml_dtypes.bfloat16)
    c["kstat"] = kst.astype(ml_dtypes.bfloat16)
    op = np.zeros((128, 2, 128), np.float32)
    op[:, 0, 0:64] = 1.0
    op[:, 1, 64:128] = 1.0
    c["onespad"] = op.astype(ml_dtypes.bfloat16)
    l = np.arange(256)
    tinc = np.zeros((128, 2, 256), np.float32)
    tinc[:, 0, :] = (p[:, None] <= l[None, :])
    tinc[:, 1, :] = (p[:, None] + 128 <= l[None, :])
    c["tinc"] = tinc
    c["onesf"] = np.ones((128, 128), np.float32)
    sel = np.zeros((64, 16), np.float32)
    for h in range(16):
        sel[h, h] = 1.0
        sel[32 + h, h] = 1.0
    c["sel"] = sel
    lb = np.zeros((64, 256), np.float32)
    lb[0:16] = 1.0
    c["lbase0"] = lb
    rb = np.zeros((64, 256), np.float32)
    rb[32:48] = 1.0
    c["rbase0"] = rb
    c["rbase0b"] = rb.astype(ml_dtypes.bfloat16)
    c["lbase0b"] = lb.astype(ml_dtypes.bfloat16)
    n384 = np.zeros((128, 384), np.float32)
    n384[:, 0:128] = tri
    n384[:, 256:384] = tri
    c["neg384"] = n384.astype(ml_dtypes.bfloat16)
    return c


_DT = {np.dtype(np.float32): F32, np.dtype(ml_dtypes.bfloat16): BF16}


def _shared_inputs(inp):
    g = lambda n: np.ascontiguousarray(inp[n][0], dtype=np.float32)
    sh = {}
    sh["ada_w"] = np.ascontiguousarray(g("ada_w").reshape(8, 128, 6144).transpose(1, 0, 2))
    sh["ada_b"] = g("ada_b").reshape(1, 6144)
    sh["ada_b_col"] = np.ascontiguousarray(g("ada_b").reshape(48, 128).T)
    w_in = g("mix_in_w").reshape(8, 128, 4112).transpose(1, 0, 2)
    sh["w_inA"] = np.ascontiguousarray(np.stack([np.concatenate([w_in[:, :, o + gI * 256:o + gI * 256 + 256] for o in (0, 512, 1024)], axis=2).reshape(128, 8 * 768)
                                                 for gI in range(2)], axis=0))
    sh["w_inBx"] = np.ascontiguousarray(w_in[:, :, 2560:4096].reshape(128, 8 * 1536))
    sh["w_inBz"] = np.ascontiguousarray(np.concatenate([w_in[:, :, 1536:2560], w_in[:, :, 4096:4112]], axis=2).reshape(128, 8 * 1040))
    sh["w_out"] = np.ascontiguousarray(g("mix_out_w").reshape(12, 128, 1024).transpose(1, 0, 2))
    wu = g("ffn_up_w").reshape(8, 128, 2, NJ, 128)
    sh["w_up"] = np.ascontiguousarray(wu.transpose(3, 1, 0, 2, 4).reshape(NJ, 128, 8, 256))
    sh["w_dn"] = np.ascontiguousarray(g("ffn_down_w").reshape(NJ, 128, 1024).transpose(1, 0, 2))
    fcw = g("ffn_conv_w").reshape(3, 2, NJ, 128)
    sh["fcw"] = np.ascontiguousarray(fcw.transpose(3, 2, 1, 0))
    sh["fcb"] = np.ascontiguousarray(g("ffn_conv_b").reshape(2, NJ, 128).transpose(2, 1, 0))
    sh["scw"] = np.ascontiguousarray(g("ssm_conv_w").reshape(4, 12, 128).transpose(2, 1, 0))
    sh["scb"] = np.ascontiguousarray(g("ssm_conv_b").reshape(12, 128).transpose(1, 0))
    for n in ("ln1_g", "ln1_b", "ln2_g", "ln2_b", "ssm_norm_w"):
        sh[n] = g(n).reshape(1, 1024)
    for n in ("ssm_dt_bias", "ssm_a_log", "ssm_d"):
        sh[n] = g(n).reshape(1, 16)
    return sh


def build(shapes, stop_after=None, dbg=False):
    nc = bass.Bass("TRN2", target_bir_lowering=False)
    sc = Sched()
    A = sc.add
    ES = contextlib.ExitStack
    din = {n: nc.dram_tensor(n, list(shp), dt, kind="ExternalInput").ap() for n, (shp, dt) in shapes.items()}
    out = nc.dram_tensor("out", [S, D], F32, kind="ExternalOutput").ap()
    x1s = nc.dram_tensor("x1s", [S, D], F32, kind="Internal").ap()
    zss = nc.dram_tensor("zss", [S, D], F32, kind="Internal").ap()
    dbgo = {}
    if dbg:
        for n, shp, dt_ in (("d_oTa", [128, 4, S], BF16), ("d_Vp", [128, 16, 4, 128], BF16), ("d_ones", [128, 2, 128], BF16), ("d_rec", [128, 512], F32), ("d_O", [128, 512], F32), ("d_hT", [128, 8, S], BF16), ("d_wA", [128, 8, 768], BF16), ("d_kf", [64, S], F32), ("d_QT", [128, 4, S], BF16), ("d_KT", [128, 4, S], BF16), ("d_ge", [128, 16, 8], F32), ("d_gm", [128, 128], F32), ("d_m8", [128, 16, 8], F32), ("d_xo", [128, 16, 1024], BF16), ("d_x1", [S, D], F32), ("d_mod", [128, 6144], F32)):
            dbgo[n] = nc.dram_tensor(n, shp, dt_, kind="ExternalOutput").ap()

    def ckpt(name):
        if stop_after == name:
            sc.frozen = True

    def T(st, name, shape, dt=F32):
        return st.enter_context(nc.sbuf_tensor("sb_" + name, list(shape), dt))

    def dma(eng, o, i, r=(), w=()):
        return A(eng, lambda e: e.dma_start(out=o, in_=i), r=r, w=w, dma=True)

    def bload(tile, src, width, w):
        keys = []
        for c0 in range(0, width, 512):
            c1 = min(width, c0 + 512)
            key = "%s#%d" % (w[0], c0 // 512)
            dma("sp", tile[:, c0:c1], src[:, c0:c1].partition_broadcast(128), w=[key])
            keys.append(key)
        if not hasattr(sc, "groups"):
            sc.groups = {}
        sc.groups[w[0]] = keys

    def cast_load(dst_flat, src_flat, n, w):
        keys = []
        for c0 in range(0, n, 2048):
            c1 = min(n, c0 + 2048)
            key = "%s#%d" % (w[0], c0 // 2048)
            dma("pool", dst_flat[:, c0:c1], src_flat[:, c0:c1], w=[key])
            keys.append(key)
        if not hasattr(sc, "groups"):
            sc.groups = {}
        sc.groups[w[0]] = keys

    def cp(eng, o, i, r=(), w=()):
        if eng == "act":
            return A("act", lambda e: e.activation(out=o, in_=i, func=AF.Copy), r=r, w=w)
        return A(eng, lambda e: e.tensor_copy(out=o, in_=i), r=r, w=w)

    def mm(o, l, rr, start, stop, r=(), w=()):
        return A("pe", lambda e: e.matmul(o, l, rr, start=start, stop=stop), r=r, w=w)

    def tt(eng, o, a, b, op, r=(), w=()):
        return A(eng, lambda e: e.tensor_tensor(out=o, in0=a, in1=b, op=op), r=r, w=w)

    def ts(eng, o, a, s1, s2, op0, op1=None, r=(), w=()):
        if op1 is None:
            return A(eng, lambda e: e.tensor_scalar(out=o, in0=a, scalar1=s1, scalar2=None, op0=op0), r=r, w=w)
        return A(eng, lambda e: e.tensor_scalar(out=o, in0=a, scalar1=s1, scalar2=s2, op0=op0, op1=op1), r=r, w=w)

    def stt(eng, o, a, s, b, op0, op1, r=(), w=()):
        return A(eng, lambda e: e.scalar_tensor_tensor(out=o, in0=a, scalar=s, in1=b, op0=op0, op1=op1), r=r, w=w)

    def act(o, i, func, r=(), w=(), **kw):
        return A("act", lambda e: e.activation(out=o, in_=i, func=func, **kw), r=r, w=w)

    with ES() as top:
        pb = [top.enter_context(nc.psum_tensor("pb%d" % i, [128, 512], F32)) for i in range(7)]
        pT = top.enter_context(nc.psum_tensor("pT", [128, 8, 128], BF16))
        g2p1 = T(top, "g2p1", [128, 1024])
        identf = T(top, "identf", [128, 128]); identb = T(top, "identb", [128, 128], BF16)
        epsln = T(top, "epsln", [128, 1])
        dma("sp", identf[:], din["identf"], w=["identf"])
        cp("dve", identb[:], identf[:], r=["identf"], w=["identb"])
        A("dve", lambda e: e.memset(epsln[:], 1e-5), w=["eps"])
        hT = T(top, "hT", [128, 8, S], BF16)
        mix2 = top.enter_context(ES())
        oTa = T(mix2, "oTa", [128, 4, S], BF16)
        modB = T(mix2, "modB", [128, 3072])

        with ES() as ph:
            aw = [T(ph, "aw%d" % i, [128, 8, 512]) for i in range(2)]
            ccol = T(ph, "ccol", [128, 8]); cact = T(ph, "cact", [128, 8])
            colall = T(ph, "colall", [128, 48]); abcol = T(ph, "abcol", [128, 48])
            ones1 = T(ph, "ones1", [128, 128]); dg = [T(ph, "dg%d" % i, [128, 128]) for i in range(4)]
            xt = [T(ph, "xt%d" % i, [128, S]) for i in range(4)]
            dma("sp", ccol[:], din["c_col"], w=["ccol"])
            dma("sp", abcol[:], din["ada_b_col"], w=["abcol"])
            A("dve", lambda e: e.memset(ones1[:], 1.0), w=["ones1"])
            act(cact[:], ccol[:], AF.Silu, r=["ccol"], w=["cact"])
            awb = [T(ph, "awb%d" % i, [128, 8, 512], BF16) for i in range(2)]
            cactb = T(ph, "cactb", [128, 8], BF16)
            cp("dve", cactb[:], cact[:], r=["cact"], w=["cactb"])
            for k in range(4):
                dma("pool", xt[k][:], din["xT"][k * 128:(k + 1) * 128, :], w=["xt%d" % k])
            for n in range(12):
                a_ = aw[n % 2]; ak = "aw%d" % (n % 2); b_ = awb[n % 2]; bk = "awb%d" % (n % 2)
                dma("sp", a_[:], din["ada_w"][:, :, n * 512:(n + 1) * 512], w=[ak])
                cp("act", b_[:, 0:4, :], a_[:, 0:4, :], r=[ak], w=[bk + "a"])
                cp("dve", b_[:, 4:8, :], a_[:, 4:8, :], r=[ak], w=[bk + "b"])
                for fcl in range(4):
                    for kk in range(8):
                        mm(pb[2][:, n * 4 + fcl:n * 4 + fcl + 1], b_[:, kk, fcl * 128:(fcl + 1) * 128], cactb[:, kk:kk + 1], kk == 0, kk == 7,
                           r=[bk + "a", bk + "b", "cactb"], w=["pb2"])
                if n == 3:
                    tt("dve", colall[:, 0:16], pb[2][:, 0:16], abcol[:, 0:16], ALU.add, r=["pb2", "abcol"], w=["colA"])
                    ts("dve", colall[:, 8:16], colall[:, 8:16], 1.0, None, ALU.add, r=["colA"], w=["colA"])
            tt("dve", colall[:, 16:48], pb[2][:, 16:48], abcol[:, 16:48], ALU.add, r=["pb2", "abcol"], w=["colB"])
            ts("dve", colall[:, 16:24], colall[:, 16:24], 1.0, None, ALU.add, r=["colB"], w=["colB"])
            ts("dve", colall[:, 32:48], colall[:, 32:48], 1.0, None, ALU.add, r=["colB"], w=["colB"])
            colsh = colall[:, 0:8]; colsc = colall[:, 8:16]
            ckpt("s1a")
            for q4 in range(8):
                p_ = pb[q4 % 2]; pk = "pb%d" % (q4 % 2)
                for i in range(4):
                    c_ = 16 + q4 * 4 + i
                    d_ = dg[i]; dk_ = "dg%d" % i
                    ts("dve", d_[:], identf[:], colall[:, c_:c_ + 1], None, ALU.mult, r=["identf", "colB"], w=[dk_])
                    mm(p_[:, i * 128:(i + 1) * 128], ones1[:], d_[:], True, True, r=["ones1", dk_], w=[pk])
                if q4 < 6:
                    cp("act", modB[:, q4 * 512:(q4 + 1) * 512], p_[:], r=[pk], w=["modB"])
                else:
                    cp("act", g2p1[:, (q4 - 6) * 512:(q4 - 5) * 512], p_[:], r=[pk], w=["g2p1"])
            ckpt("s1b")
            for k in range(8):
                x_ = xt[k % 4]; xk = "xt%d" % (k % 4)
                if k >= 4:
                    dma("pool", x_[:], din["xT"][k * 128:(k + 1) * 128, :], w=[xk])
                ts("dve", hT[:, k, :], x_[:], colsc[:, k:k + 1], colsh[:, k:k + 1], ALU.mult, ALU.add, r=[xk, "colA"], w=["hT"])

        sc.barrier()
        ckpt("s12")
        with ES() as ph:
            wA = T(ph, "wA", [128, 8, 768], BF16)
            QT = T(ph, "QT", [128, 4, S], BF16); KT = T(ph, "KT", [128, 4, S], BF16)
            Vp = T(ph, "Vp", [128, 16, 4, 128], BF16)
            kf = T(ph, "kf", [64, S]); qf = T(ph, "qf", [64, S]); km = T(ph, "km", [64, 8])
            kfs = [kf, T(ph, "kfB", [64, S])]; qfs = [qf, T(ph, "qfB", [64, S])]; kms = [km, T(ph, "kmB", [64, 8])]
            gm = T(ph, "gm", [128, 128]); m8 = T(ph, "m8", [128, 16, 8]); ge = T(ph, "ge", [128, 16, 8])
            mpad = T(ph, "mpad", [128, 16, 72], BF16)
            pastneg = T(ph, "pastneg", [128, 128]); basetab = T(ph, "basetab", [128, 16, 8]); atab = T(ph, "atab", [128, 8, 16, 8])
            negtri = T(ph, "negtri", [128, 128]); negfull = T(ph, "negfull", [128, 128]); onesp = T(ph, "onesp", [128, 2, 128], BF16)
            PT = [T(ph, "PT%d" % i, [128, 512], BF16) for i in range(6)]
            rec = T(ph, "rec", [128, 512])
            dma("sp", pastneg[:], din["pastneg"].rearrange("p a b -> p (a b)"), w=["pastneg"])
            dma("sp", basetab[:], din["basetab"], w=["basetab"]); dma("sp", atab[:], din["atab"], w=["atab"])
            dma("sp", negtri[:], din["negtri"], w=["negtri"]); dma("sp", negfull[:], din["negfull"], w=["negfull"])
            dma("sp", onesp[:], din["onespad"], w=["onesp"])
            zf = T(ph, "zf", [128, 1152])
            A("dve", lambda e: e.memset(zf[:], 0.0), w=["zf"])
            for t_ in range(16):
                cp("dve", Vp[:, t_], zf[:, 0:512].rearrange("p (a b) -> p a b", b=128), r=["zf"], w=["Vp"])
            cp("dve", mpad[:], zf[:].rearrange("p (a b) -> p a b", b=72), r=["zf"], w=["mpad"])
            sring = 0
            pring = 0
            for gI in range(2):
                cast_load(wA[:].rearrange("p k c -> p (k c)"), din["w_inA"][gI], 8 * 768, ["wA"])
                dma("sp", QT[72:74, :, :], din["qstat"][:, 4 * gI:4 * gI + 4, :], w=["QT"])
                dma("sp", KT[64:74, :, :], din["kstat"][:, 4 * gI:4 * gI + 4, :], w=["KT"])
                for t_ in range(16):
                    p_ = pb[t_ % 2]; pk = "pb%d" % (t_ % 2)
                    for k in range(8):
                        mm(p_[:, 0:256], hT[:, k, t_ * 128:(t_ + 1) * 128], wA[:, k, 512:768], k == 0, k == 7, r=["hT", "wA"], w=[pk])
                    pv = p_[:, 0:256].rearrange("p (a t d) -> p a t d", t=2, d=64)
                    vv = Vp[:, t_].rearrange("p (a t) c -> p a t c", t=2)
                    cp("dve", vv[:, :, 0, 0:64], pv[:, :, 0, :], r=[pk], w=["Vp"])
                    cp("dve", vv[:, :, 1, 64:128], pv[:, :, 1, :], r=[pk], w=["Vp"])
                ckpt("a1")
                def prepA(hl):
                    kf_ = kfs[hl % 2]; qf_ = qfs[hl % 2]; km_ = kms[hl % 2]
                    kfk = "kf%d" % (hl % 2); qfk = "qf%d" % (hl % 2); kmk = "km%d" % (hl % 2)
                    for tb in range(4):
                        p_ = pb[tb % 2]; pk = "pb%d" % (tb % 2)
                        for k in range(8):
                            mm(p_[:], wA[:, k, 256 + hl * 64:256 + hl * 64 + 128], hT[:, k, tb * 512:(tb + 1) * 512], k == 0, k == 7, r=["hT", "wA"], w=[pk])
                        cp("act", kf_[:, tb * 512:(tb + 1) * 512], p_[0:64, :], r=[pk], w=[kfk])
                        cp("dve", KT[0:64, hl, tb * 512:(tb + 1) * 512], kf_[:, tb * 512:(tb + 1) * 512], r=[kfk], w=["KT"])
                    A("dve", lambda e, kf_=kf_, km_=km_: e.tensor_reduce(out=km_[:], in_=kf_[:].rearrange("p (b t) -> p b t", t=256), axis=AX.X, op=ALU.add), r=[kfk], w=[kmk])
                    ts("dve", km_[:], km_[:], 1.0 / 256.0, None, ALU.mult, r=[kmk], w=[kmk])
                    for tb in range(4):
                        p_ = pb[tb % 2]; pk = "pb%d" % (tb % 2)
                        for k in range(8):
                            mm(p_[:], wA[:, k, hl * 64:hl * 64 + 128], hT[:, k, tb * 512:(tb + 1) * 512], k == 0, k == 7, r=["hT", "wA"], w=[pk])
                        cp("act", qf_[:, tb * 512:(tb + 1) * 512], p_[0:64, :], r=[pk], w=[qfk])
                        ts("dve", QT[0:64, hl, tb * 512:(tb + 1) * 512], qf_[:, tb * 512:(tb + 1) * 512], 0.125, None, ALU.mult, r=[qfk], w=["QT"])

                def prepB(hl):
                    h = 4 * gI + hl
                    qf_ = qfs[hl % 2]; km_ = kms[hl % 2]; qfk = "qf%d" % (hl % 2); kmk = "km%d" % (hl % 2)
                    for t_ in range(16):
                        mm(pb[2][:, t_ * 8:(t_ + 1) * 8], qf_[:, t_ * 128:(t_ + 1) * 128], km_[:], True, True, r=[qfk, kmk], w=["pb2"])
                    tt("dve", gm[:], pb[2][:, 0:128], pastneg[:], ALU.add, r=["pb2", "pastneg"], w=["gm"])
                    for t_ in range(16):
                        A("dve", lambda e, t_=t_: e.max(out=m8[:, t_, :], in_=gm[:, t_ * 8:(t_ + 1) * 8]), r=["gm"], w=["m8"])
                    tt("dve", ge[:], gm[:].rearrange("p (a b) -> p a b", b=8), m8[:, :, 2:3].to_broadcast([128, 16, 8]), ALU.is_ge, r=["gm", "m8"], w=["ge"])
                    tt("dve", ge[:], ge[:], atab[:, h], ALU.mult, r=["ge", "atab"], w=["ge"])
                    tt("dve", mpad[:, :, 64:72], ge[:], basetab[:], ALU.add, r=["ge", "basetab"], w=["mpad"])

                def prepC(hl):
                    for r4 in range(4):
                        for i in range(4):
                            mm(pb[3][0:72, i * 128:(i + 1) * 128], mpad[:, r4 * 4 + i, :], identb[:], True, True, r=["mpad", "identb"], w=["pb3"])
                        cp("act", QT[64:72, hl, r4 * 512:(r4 + 1) * 512], pb[3][64:72, :], r=["pb3"], w=["QT"])

                prepA(0)
                for hl in range(4):
                    prepB(hl)
                    if hl + 1 < 4:
                        prepA(hl + 1)
                    prepC(hl)
                if dbg and "prep" in dbg and gI == 0:
                    dma("sp", dbgo["d_QT"], QT[:], r=["QT"], w=["d_QT"])
                    dma("sp", dbgo["d_hT"], hT[:], r=["hT"], w=["d_hT"])
                    dma("sp", dbgo["d_wA"], wA[:], r=["wA"], w=["d_wA"])
                    dma("sp", dbgo["d_kf"], kf[:], r=["kf"], w=["d_kf"])
                    dma("sp", dbgo["d_KT"], KT[:], r=["KT"], w=["d_KT"])
                    dma("sp", dbgo["d_ge"], ge[:], r=["ge"], w=["d_ge"])
                    dma("sp", dbgo["d_gm"], gm[:], r=["gm"], w=["d_gm"])
                    dma("sp", dbgo["d_m8"], m8[:], r=["m8"], w=["d_m8"])
                ckpt("a3")
                for pr in range(2):
                    for c in range(4):
                        ntk = 4 * c + 4
                        cs_ = slice(c * 512, (c + 1) * 512)
                        n_mm = 2 * ntk
                        cnt = 0
                        ob_, db_ = ((3, 2), (1, 0))[(pr * 4 + c) % 2]
                        steps = [(eo, 2 * pr + eo, i) for eo in range(2) for i in range(ntk)]
                        SK = 3
                        pts = {}
                        for s_ in range(n_mm + SK):
                            if s_ < n_mm:
                                eo, hl, i = steps[s_]
                                bi_ = 4 + sring % 3; sb = pb[bi_]; sk = "pb%d" % bi_; sring += 1
                                mm(sb[:], KT[0:74, hl, i * 128:(i + 1) * 128], QT[0:74, hl, cs_], True, True, r=["KT", "QT"], w=[sk])
                                if i >= 4 * c:
                                    j0 = i - 4 * c
                                    tt("dve", sb[:, j0 * 128:(j0 + 1) * 128], sb[:, j0 * 128:(j0 + 1) * 128], negtri[:], ALU.add, r=[sk, "negtri"], w=[sk])
                                    if i % 2 == 1:
                                        tt("dve", sb[:, (j0 - 1) * 128:j0 * 128], sb[:, (j0 - 1) * 128:j0 * 128], negfull[:], ALU.add, r=[sk, "negfull"], w=[sk])
                                pt = PT[pring % 6]; ptk = "PT%d" % (pring % 6); pring += 1
                                act(pt[:], sb[:], AF.Exp, r=[sk], w=[ptk])
                                pts[s_] = (pt, ptk)
                            if s_ >= SK:
                                eo, hl, i = steps[s_ - SK]
                                pt, ptk = pts.pop(s_ - SK)
                                mm(pb[ob_][:], Vp[:, i, hl, :], pt[:], cnt == 0, cnt == n_mm - 1, r=["Vp", ptk], w=["pb%d" % ob_])
                                mm(pb[db_][:], onesp[:, eo, :], pt[:], cnt == 0, cnt == n_mm - 1, r=["onesp", ptk], w=["pb%d" % db_])
                                cnt += 1
                        A("dve", lambda e, db_=db_: e.reciprocal(out=rec[:], in_=pb[db_][:]), r=["pb%d" % db_], w=["rec"])
                        if dbg and "last" in dbg and gI == 1 and pr == 1 and c == 3:
                            dbt = T(ph, "dbt", [128, 512])
                            cp("dve", dbt[:], pb[3][:], r=["pb3"], w=["dbt"])
                            dma("sp", dbgo["d_O"], dbt[:], r=["dbt"], w=["d_O"])
                            dma("sp", dbgo["d_rec"], rec[:], r=["rec"], w=["d_rec"])
                            dma("sp", dbgo["d_Vp"], Vp[:], r=["Vp"], w=["d_Vp"])
                            dma("sp", dbgo["d_ones"], onesp[:], r=["onesp"], w=["d_ones"])
                        tt("dve", oTa[:, 2 * gI + pr, cs_], pb[ob_][:], rec[:], ALU.mult, r=["pb%d" % ob_, "rec"], w=["oTa"])

        sc.barrier()
        if dbg and "oTa" in dbg:
            dma("sp", dbgo["d_oTa"], oTa[:], r=["oTa"], w=["d_oTa"])
        ckpt("attn")
        xo = T(mix2, "xo", [128, 16, 1024], BF16)
        Btok = T(mix2, "Btok", [128, 16, 256], BF16)
        BT = T(mix2, "BT", [128, 2, S], BF16); CT = T(mix2, "CT", [128, 2, S], BF16)
        dtt = T(mix2, "dtt", [128, 16, 16]); dAx = T(mix2, "dAx", [128, 16, 64])
        onec = T(mix2, "onec", [128, 1])
        A("dve", lambda e: e.memset(onec[:], 1.0), w=["onec"])
        with ES() as ph:
            wB = T(ph, "wB", [128, 8, 1536], BF16)
            scw = T(ph, "scw", [128, 12, 4]); scb = T(ph, "scb", [128, 12])
            xpre = T(ph, "xpre", [128, 3 + S]); xcv = T(ph, "xcv", [128, S]); xact = T(ph, "xact", [128, S], BF16)
            zt = [T(ph, "zt%d" % i, [128, 512]) for i in range(2)]
            dtb = T(ph, "dtb", [128, 16]); ab = T(ph, "ab", [128, 16])
            cast_load(wB[:].rearrange("p k c -> p (k c)"), din["w_inBx"], 8 * 1536, ["wB"])
            wBz_t = T(ph, "wBz", [128, 8, 1040], BF16)
            wBz = wBz_t[:]
            cast_load(wBz_t[:].rearrange("p k c -> p (k c)"), din["w_inBz"], 8 * 1040, ["wBz"])
            dma("sp", scw[:], din["scw"], w=["scw"]); dma("sp", scb[:], din["scb"], w=["scb"])
            dma("sp", dtb[:], din["ssm_dt_bias"].partition_broadcast(128), w=["dtb"])
            dma("sp", ab[:], din["ssm_a_log"].partition_broadcast(128), w=["ab"])
            act(ab[:], ab[:], AF.Exp, r=["ab"], w=["ab"])
            ts("dve", ab[:], ab[:], -1.0, None, ALU.mult, r=["ab"], w=["ab"])
            xpres = [xpre, T(ph, "xpreB", [128, 3 + S])]
            for i_ in range(2):
                A("dve", lambda e, i_=i_: e.memset(xpres[i_][:, 0:3], 0.0), w=["xpre%d" % i_])

            def projA(fc):
                xp = xpres[fc % 2]; xk_ = "xpre%d" % (fc % 2)
                for tb in range(4):
                    p_ = pb[tb % 2]; pk = "pb%d" % (tb % 2)
                    for k in range(8):
                        mm(p_[:], wB[:, k, fc * 128:(fc + 1) * 128], hT[:, k, tb * 512:(tb + 1) * 512], k == 0, k == 7, r=["wB", "hT"], w=[pk])
                    cp("act", xp[:, 3 + tb * 512:3 + (tb + 1) * 512], p_[:], r=[pk], w=[xk_])

            def projB(fc):
                xp = xpres[fc % 2]; xk_ = "xpre%d" % (fc % 2)
                ts("dve", xcv[:], xp[:, 3:3 + S], scw[:, fc, 3:4], scb[:, fc:fc + 1], ALU.mult, ALU.add, r=[xk_, "scw", "scb"], w=["xcv"])
                for j in (2, 1, 0):
                    stt("dve", xcv[:], xp[:, j:j + S], scw[:, fc, j:j + 1], xcv[:], ALU.mult, ALU.add, r=[xk_, "xcv", "scw"], w=["xcv"])
                if fc < 8:
                    dst, dk = xact[:], "xact"
                elif fc < 10:
                    dst, dk = BT[:, fc - 8, :], "BT"
                else:
                    dst, dk = CT[:, fc - 10, :], "CT"
                act(dst, xcv[:], AF.Silu, r=["xcv"], w=[dk])
                if fc < 10:
                    for half in range(2):
                        for i in range(8):
                            t_ = half * 8 + i
                            sl = slice(t_ * 128, (t_ + 1) * 128)
                            srcap = xact[:, sl] if fc < 8 else BT[:, fc - 8, sl]
                            A("pe", lambda e, i=i, srcap=srcap: e.transpose(pT[:, i, :], srcap, identb[:]), r=[dk, "identb"], w=["pT"])
                        if fc < 8:
                            cp("act", xo[:, half * 8:(half + 1) * 8, fc * 128:(fc + 1) * 128], pT[:], r=["pT"], w=["xo%d" % t for t in range(half * 8, half * 8 + 8)])
                        else:
                            cp("act", Btok[:, half * 8:(half + 1) * 8, (fc - 8) * 128:(fc - 7) * 128], pT[:], r=["pT"], w=["Btok"])

            projA(0)
            for fc in range(12):
                if fc + 1 < 12:
                    projA(fc + 1)
                projB(fc)
            for t_ in range(16):
                for half in range(2):
                    p_ = pb[2 + half]; pk = "pb%d" % (2 + half)
                    for k in range(8):
                        mm(p_[:], hT[:, k, t_ * 128:(t_ + 1) * 128], wBz[:, k, half * 512:(half + 1) * 512], k == 0, k == 7, r=["wBz", "hT"], w=[pk])
                    act(zt[half][:], p_[:], AF.Silu, r=[pk], w=["zt%d" % half])
                    dma("sp", zss[t_ * 128:(t_ + 1) * 128, half * 512:(half + 1) * 512], zt[half][:], r=["zt%d" % half], w=["zss%d" % t_])
            for t_ in range(16):
                for k in range(8):
                    mm(pb[4][:, t_ * 16:(t_ + 1) * 16], hT[:, k, t_ * 128:(t_ + 1) * 128], wBz[:, k, 1024:1040], k == 0, k == 7, r=["wBz", "hT"], w=["pb4"])
            tt("dve", dtt[:], pb[4][:, 0:256].rearrange("p (a b) -> p a b", b=16), dtb[:].unsqueeze(1).to_broadcast([128, 16, 16]), ALU.add, r=["pb4", "dtb"], w=["dtt"])
            act(dtt[:], dtt[:], AF.Exp, r=["dtt"], w=["dtt"])
            act(dtt[:], dtt[:], AF.Ln, r=["dtt", "onec"], w=["dtt"], bias=onec[:, 0:1], scale=1.0)
            A("dve", lambda e: e.memset(dAx[:], 0.0), w=["dAx"])
            tt("dve", dAx[:, :, 0:16], dtt[:], ab[:].unsqueeze(1).to_broadcast([128, 16, 16]), ALU.mult, r=["dtt", "ab", "dAx"], w=["dAx"])
            ts("dve", dAx[:, :, 32:48], dAx[:, :, 0:16], -1.0, None, ALU.mult, r=["dAx"], w=["dAx"])

        sc.barrier()
        ckpt("ssmin")
        with ES() as ph:
            tinc = T(ph, "tinc", [128, 2, 256]); onesf = T(ph, "onesf", [128, 128]); sel = T(ph, "sel", [64, 16])
            HI = T(ph, "HI", [64, 256], BF16); LO = T(ph, "LO", [64, 256], BF16)
            R1 = T(ph, "R1", [64, 256], BF16); R2 = T(ph, "R2", [64, 256], BF16); L1 = T(ph, "L1", [64, 256], BF16); L2 = T(ph, "L2", [64, 256], BF16)
            Lh1 = [T(ph, "Lh1_%d" % i, [64, 256], BF16) for i in range(3)]; Lh2 = [T(ph, "Lh2_%d" % i, [64, 256], BF16) for i in range(3)]
            cst = T(ph, "cst", [128, 2, 16]); ecs = T(ph, "ecs", [128, 2, 16]); dec = T(ph, "dec", [128, 2, 16])
            ctb = T(ph, "ctb", [128, 16]); cdb = T(ph, "cdb", [128, 16])
            GT = T(ph, "GT", [128, 2, 384]); xdt = T(ph, "xdt", [128, 2, 1024], BF16); xdd = T(ph, "xdd", [128, 2, 1024], BF16)
            prevT = T(ph, "prevT", [128, 1024]); prevb = T(ph, "prevb", [128, 1024], BF16)
            neg384 = T(ph, "neg384", [128, 384], BF16)
            LT = [T(ph, "LT%d" % i, [128, 384]) for i in range(2)]; MT = [T(ph, "MT%d" % i, [128, 384], BF16) for i in range(3)]
            yt = [T(ph, "yt%d" % i, [128, 1024]) for i in range(4)]; ytmp = T(ph, "ytmp", [128, 1024])
            zsb = [T(ph, "zsb%d" % i, [128, 1024]) for i in range(2)]; ynb = T(ph, "ynb", [128, 1024], BF16)
            nwb = T(ph, "nwb", [128, 1024]); Db = T(ph, "Db", [128, 16])
            st6 = T(ph, "st6", [128, 2, 6]); mv = T(ph, "mv", [128, 2]); ms = T(ph, "ms", [128, 1]); rs = T(ph, "rs", [128, 1])
            for tl_, nm, key in ((tinc, "tinc", "tinc"), (onesf, "onesf", "onesf"), (sel, "sel", "sel"), (R1, "rbase0b", "R1"), (R2, "rbase0b", "R2"),
                                 (L1, "lbase0b", "L1"), (L2, "lbase0b", "L2"), (neg384, "neg384", "neg384")):
                dma("sp", tl_[:], din[nm], w=[key])
            bload(nwb, din["ssm_norm_w"], 1024, ["nwb"])
            dma("sp", Db[:], din["ssm_d"].partition_broadcast(128), w=["Db"])
            A("dve", lambda e: e.memset(prevT[:], 0.0), w=["prevT"])
            GTs = [GT, T(ph, "GTB", [128, 2, 384])]; xdts = [xdt, T(ph, "xdtB", [128, 2, 1024], BF16)]
            R1s = [R1, T(ph, "R1B", [64, 256], BF16)]; R2s = [R2, T(ph, "R2B", [64, 256], BF16)]
            L1s = [L1, T(ph, "L1B", [64, 256], BF16)]; L2s = [L2, T(ph, "L2B", [64, 256], BF16)]
            for tl_, nm, key in ((R1s[1], "rbase0b", "R1_1"), (R2s[1], "rbase0b", "R2_1"), (L1s[1], "lbase0b", "L1_1"), (L2s[1], "lbase0b", "L2_1")):
                dma("sp", tl_[:], din[nm], w=[key])

            def prologue(c):
                par = c % 2; P_ = "_%d" % par if par else ""
                t0, t1 = 2 * c, 2 * c + 1
                tl2 = (t0, t1)
                csl = slice(c * 256, (c + 1) * 256)
                yb = par * 2
                GT_, xdt_, R1_, R2_, L1_, L2_ = GTs[par], xdts[par], R1s[par], R2s[par], L1s[par], L2s[par]
                kR1, kR2, kL1, kL2, kGT, kxdt = ("R1" + P_, "R2" + P_, "L1" + P_, "L2" + P_, "GT%d" % par, "xdt%d" % par)
                mm(pb[0][0:64, 0:256], dAx[:, t0, :], tinc[:, 0, :], True, False, r=["dAx", "tinc", "onesf", "sel", "neg384"], w=["pb0"])
                mm(pb[0][0:64, 0:256], dAx[:, t1, :], tinc[:, 1, :], False, True, r=["dAx", "tinc", "onesf", "sel", "neg384"], w=["pb0"])
                mm(pb[0][:, 256:272], tinc[:, 0, 0:128], dAx[:, t0, 0:16], True, True, r=["dAx"], w=["pb0"])
                mm(pb[0][:, 272:288], tinc[:, 0, 128:256], dAx[:, t0, 0:16], True, False, r=["dAx"], w=["pb0"])
                mm(pb[0][:, 272:288], tinc[:, 1, 128:256], dAx[:, t1, 0:16], False, True, r=["dAx"], w=["pb0"])
                mm(pb[0][:, 288:304], onesf[:], dAx[:, t0, 0:16], True, False, r=["dAx"], w=["pb0"])
                mm(pb[0][:, 288:304], onesf[:], dAx[:, t1, 0:16], False, True, r=["dAx"], w=["pb0"])
                cp("act", HI[:], pb[0][0:64, 0:256], r=["pb0"], w=["HI"])
                cp("act", cst[:].rearrange("p a b -> p (a b)"), pb[0][:, 256:288], r=["pb0"], w=["cst"])
                cp("act", ctb[:], pb[0][:, 288:304], r=["pb0"], w=["ctb"])
                tt("dve", LO[:], pb[0][0:64, 0:256], HI[:], ALU.subtract, r=["pb0", "HI", "cst", "ctb"], w=["LO"])
                cp("act", R1_[0:16, :], HI[0:16, :], r=["HI", "tinc", "onesf", "sel", "neg384"], w=[kR1])
                cp("act", L1_[32:48, :], HI[32:48, :], r=["HI", "tinc", "onesf", "sel", "neg384"], w=[kL1])
                cp("act", R2_[0:16, :], LO[0:16, :], r=["LO", "tinc", "onesf", "sel", "neg384"], w=[kR2])
                cp("act", L2_[32:48, :], LO[32:48, :], r=["LO", "tinc", "onesf", "sel", "neg384"], w=[kL2])
                act(ecs[:], cst[:], AF.Exp, r=["cst"], w=["ecs"])
                tt("dve", dec[:], ctb[:].unsqueeze(1).to_broadcast([128, 2, 16]), cst[:], ALU.subtract, r=["ctb", "cst"], w=["dec"])
                act(dec[:], dec[:], AF.Exp, r=["dec"], w=["dec"])
                act(cdb[:], ctb[:], AF.Exp, r=["ctb"], w=["cdb"])
                for g in range(2):
                    mm(pb[1][:, 0:256], BT[:, g, t0 * 128:(t0 + 1) * 128], CT[:, g, csl], True, True, r=["BT", "CT"], w=["pb1"])
                    mm(pb[1][:, 256:384], BT[:, g, t1 * 128:(t1 + 1) * 128], CT[:, g, t1 * 128:(t1 + 1) * 128], True, True, r=["BT", "CT"], w=["pb1"])
                    cp("act", GT_[:, g, :], pb[1][:, 0:384], r=["pb1"], w=[kGT])
                for i in range(2):
                    xv = xo[:, tl2[i], :].rearrange("p (h d) -> p h d", d=64)
                    tt("pool", xdt_[:, i, :].rearrange("p (h d) -> p h d", d=64), xv, dtt[:, tl2[i], :].unsqueeze(2).to_broadcast([128, 16, 64]), ALU.mult, r=["xo%d" % tl2[i], "dtt"], w=[kxdt])
                    tt("pool", xdd[:, i, :].rearrange("p (h d) -> p h d", d=64), xdt_[:, i, :].rearrange("p (h d) -> p h d", d=64), dec[:, i, :].unsqueeze(2).to_broadcast([128, 16, 64]), ALU.mult, r=[kxdt, "dec"], w=["xdd"])
                for i in range(2):
                    if c == 0:
                        A("dve", lambda e, i=i, yb=yb: e.memset(yt[yb + i][:], 0.0), w=["yt%d" % (yb + i)])
                        continue
                    for g in range(2):
                        bq = 2 - g; bqk = "pb%d" % bq
                        mm(pb[bq][:], CT[:, g, tl2[i] * 128:(tl2[i] + 1) * 128], prevb[:, g * 512:(g + 1) * 512], True, True, r=["CT", "prevb"], w=[bqk])
                        tt("dve", yt[yb + i][:, g * 512:(g + 1) * 512].rearrange("p (h d) -> p h d", d=64), pb[bq][:].rearrange("p (h d) -> p h d", d=64),
                           ecs[:, i, g * 8:(g + 1) * 8].unsqueeze(2).to_broadcast([128, 8, 64]), ALU.mult, r=[bqk, "ecs"], w=["yt%d" % (yb + i)])
                if c < 7:
                    for g in range(2):
                        gs = slice(g * 512, (g + 1) * 512)
                        bq = 2 - g; bqk = "pb%d" % bq
                        mm(pb[bq][:], Btok[:, t0, g * 128:(g + 1) * 128], xdd[:, 0, gs], True, False, r=["Btok", "xdd"], w=[bqk])
                        mm(pb[bq][:], Btok[:, t1, g * 128:(g + 1) * 128], xdd[:, 1, gs], False, True, r=["Btok", "xdd"], w=[bqk])
                        tt("dve", prevT[:, gs].rearrange("p (h d) -> p h d", d=64), prevT[:, gs].rearrange("p (h d) -> p h d", d=64),
                           cdb[:, g * 8:(g + 1) * 8].unsqueeze(2).to_broadcast([128, 8, 64]), ALU.mult, r=["prevT", "cdb"], w=["prevT"])
                        tt("dve", prevT[:, gs], prevT[:, gs], pb[bq][:], ALU.add, r=["prevT", bqk], w=["prevT"])
                        cp("dve", prevb[:, gs], prevT[:, gs], r=["prevT"], w=["prevb"])

            ms2 = T(ph, "ms2", [128, 2]); rs2 = T(ph, "rs2", [128, 2])

            def combine(cc, i):
                t_ = 2 * cc + i
                yi = (cc % 2) * 2 + i; yk = "yt%d" % yi
                z_ = zsb[i]; zk = "zsb%d" % i
                dma("sp", z_[:], zss[t_ * 128:(t_ + 1) * 128, :], r=["zss%d" % t_], w=[zk])
                tt("pool", ytmp[:].rearrange("p (h d) -> p h d", d=64), xo[:, t_, :].rearrange("p (h d) -> p h d", d=64),
                   Db[:].unsqueeze(2).to_broadcast([128, 16, 64]), ALU.mult, r=["xo%d" % t_, "Db"], w=["ytmp"])
                tt("pool", yt[yi][:], yt[yi][:], ytmp[:], ALU.add, r=[yk, "ytmp"], w=[yk])
                tt("dve", yt[yi][:], yt[yi][:], z_[:], ALU.mult, r=[yk, zk], w=[yk])
                for g in range(2):
                    for q_ in range(2):
                        A("dve", lambda e, yi=yi, g=g, q_=q_: e.bn_stats(out=st6[:, q_, :], in_=yt[yi][:, g * 512 + q_ * 256:g * 512 + (q_ + 1) * 256]), r=[yk], w=["st6"])
                    A("dve", lambda e: e.bn_aggr(out=mv[:], in_=st6[:]), r=["st6"], w=["mv"])
                    stt("dve", ms2[:, g:g + 1], mv[:, 0:1], mv[:, 0:1], mv[:, 1:2], ALU.mult, ALU.add, r=["mv"], w=["ms2"])

            def combineN(cc, i):
                yi = (cc % 2) * 2 + i; yk = "yt%d" % yi
                act(rs2[:], ms2[:], AF.Sqrt, r=["ms2", "eps"], w=["rs2"], bias=epsln[:, 0:1], scale=1.0)
                A("dve", lambda e: e.reciprocal(out=rs2[:], in_=rs2[:]), r=["rs2"], w=["rs2"])
                for g in range(2):
                    gs = slice(g * 512, (g + 1) * 512)
                    stt("dve", ynb[:, gs], yt[yi][:, gs], rs2[:, g:g + 1], nwb[:, gs], ALU.mult, ALU.mult, r=[yk, "rs2", "nwb"], w=["ynb"])

            def combineT(cc, i):
                t_ = 2 * cc + i
                for fc in range(8):
                    A("pe", lambda e, fc=fc: e.transpose(pT[:, fc, :], ynb[:, fc * 128:(fc + 1) * 128], identb[:]), r=["ynb", "identb"], w=["pT"])
                cp("act", xo[:, t_, :].rearrange("p (f k) -> p f k", k=128), pT[:], r=["pT"], w=["xo%d" % t_])

            def heads(c):
                par = c % 2; P_ = "_%d" % par if par else ""
                yb = par * 2
                GT_, xdt_, R1_, R2_, L1_, L2_ = GTs[par], xdts[par], R1s[par], R2s[par], L1s[par], L2s[par]
                kR1, kR2, kL1, kL2, kGT, kxdt = ("R1" + P_, "R2" + P_, "L1" + P_, "L2" + P_, "GT%d" % par, "xdt%d" % par)

                def mask(h):
                    act(Lh1[h % 3][:], L1_[:], AF.Copy, r=[kL1, "tinc", "onesf", "sel", "neg384"], w=["Lh1_%d" % (h % 3)], scale=sel[:, h:h + 1])
                    act(Lh2[h % 3][:], L2_[:], AF.Copy, r=[kL2, "tinc", "onesf", "sel", "neg384"], w=["Lh2_%d" % (h % 3)], scale=sel[:, h:h + 1])

                def dexp(h):
                    a_ = Lh1[h % 3]; b_ = Lh2[h % 3]; ak_ = "Lh1_%d" % (h % 3); bk_ = "Lh2_%d" % (h % 3)
                    Dp = pb[5 + h % 2]; dk_ = "pb%d" % (5 + h % 2)
                    mm(Dp[:, 0:256], identb[:], neg384[:, 0:256], True, False, r=["identb", "tinc", "onesf", "sel", "neg384"], w=[dk_])
                    mm(Dp[:, 0:256], a_[:, 0:128], R1_[:, 0:256], False, False, r=[ak_, kR1], w=[dk_])
                    mm(Dp[:, 0:256], b_[:, 0:128], R2_[:, 0:256], False, True, r=[bk_, kR2], w=[dk_])
                    mm(Dp[:, 256:384], identb[:], neg384[:, 256:384], True, False, r=["identb"], w=[dk_])
                    mm(Dp[:, 256:384], a_[:, 128:256], R1_[:, 128:256], False, False, r=[ak_, kR1], w=[dk_])
                    mm(Dp[:, 256:384], b_[:, 128:256], R2_[:, 128:256], False, True, r=[bk_, kR2], w=[dk_])
                    act(LT[h % 2][:], Dp[:, 0:384], AF.Exp, r=[dk_], w=["LT%d" % (h % 2)])

                def mmul(h):
                    tt("dve", MT[h % 3][:], LT[h % 2][:], GT_[:, h // 8, :], ALU.mult, r=["LT%d" % (h % 2), kGT], w=["MT%d" % (h % 3)])

                def ydiag(h):
                    g = h // 8; hh = h % 8
                    mt = MT[h % 3]; mtk = "MT%d" % (h % 3)
                    hs = slice(h * 64, (h + 1) * 64)
                    mm(pb[3][:, hh * 64:(hh + 1) * 64], mt[:, 0:128], xdt_[:, 0, hs], True, True, r=[mtk, kxdt], w=["pb3"])
                    mm(pb[4][:, hh * 64:(hh + 1) * 64], mt[:, 128:256], xdt_[:, 0, hs], True, False, r=[mtk, kxdt], w=["pb4"])
                    mm(pb[4][:, hh * 64:(hh + 1) * 64], mt[:, 256:384], xdt_[:, 1, hs], False, True, r=[mtk, kxdt], w=["pb4"])
                    if hh == 7:
                        for i in range(2):
                            gs = slice(g * 512, (g + 1) * 512)
                            tt("dve", yt[yb + i][:, gs], yt[yb + i][:, gs], pb[3 + i][:], ALU.add, r=["yt%d" % (yb + i), "pb%d" % (3 + i)], w=["yt%d" % (yb + i)])

                mask(0)
                for s_ in range(18):
                    if s_ + 1 < 16:
                        mask(s_ + 1)
                    if s_ < 16:
                        dexp(s_)
                    if 1 <= s_ <= 16:
                        mmul(s_ - 1)
                    if 2 <= s_:
                        ydiag(s_ - 2)
                    if c >= 1:
                        if s_ == 0:
                            combine(c - 1, 0)
                        if s_ == 3:
                            combineN(c - 1, 0)
                        if s_ == 5:
                            combineT(c - 1, 0)
                            combine(c - 1, 1)
                        if s_ == 8:
                            combineN(c - 1, 1)
                        if s_ == 10:
                            combineT(c - 1, 1)
                    if s_ == 9 and c + 1 < 8:
                        prologue(c + 1)

            prologue(0)
            for c in range(8):
                heads(c)
            combine(7, 0)
            combineN(7, 0)
            combineT(7, 0)
            combine(7, 1)
            combineN(7, 1)
            combineT(7, 1)

        sc.barrier()
        if dbg and "xo" in dbg:
            dma("sp", dbgo["d_xo"], xo[:], r=["xo%d" % t for t in range(16)], w=["d_xo"])
        ckpt("ssm")

        def layer_norm(tl, tlk, gbt, bbt, gk, o_t, ok, st8, mv2, rstd):
            for q_ in range(4):
                A("dve", lambda e, q_=q_: e.bn_stats(out=st8[:, q_, :], in_=tl[:, q_ * 256:(q_ + 1) * 256]), r=[tlk], w=["st8"])
            A("dve", lambda e: e.bn_aggr(out=mv2[:], in_=st8[:]), r=["st8"], w=["mv2"])
            act(rstd[:], mv2[:, 1:2], AF.Sqrt, r=["mv2", "eps"], w=["rstd"], bias=epsln[:, 0:1], scale=1.0)
            A("dve", lambda e: e.reciprocal(out=rstd[:], in_=rstd[:]), r=["rstd"], w=["rstd"])
            ts("dve", tl[:], tl[:], mv2[:, 0:1], rstd[:, 0:1], ALU.subtract, ALU.mult, r=[tlk, "mv2", "rstd"], w=[tlk])
            tt("dve", tl[:], tl[:], gbt[:], ALU.mult, r=[tlk, gk], w=[tlk])
            tt("dve", o_t[:], tl[:], bbt[:], ALU.add, r=[tlk, gk], w=[ok])

        with ES() as ph:
            wO = T(ph, "wO", [128, 12, 1024], BF16)
            g1t = T(ph, "g1t", [128, 1024]); b1t = T(ph, "b1t", [128, 1024])
            xin = [T(ph, "xin%d" % i, [128, 1024]) for i in range(2)]
            tl = T(ph, "tl", [128, 1024]); x1t = [T(ph, "x1t%d" % i, [128, 1024]) for i in range(2)]
            h2f = T(ph, "h2f", [128, 1024]); h2b = T(ph, "h2b", [128, 1024], BF16)
            st8 = T(ph, "st8", [128, 4, 6]); mv2 = T(ph, "mv2", [128, 2]); rstd = T(ph, "rstd", [128, 1])
            for pc in range(6):
                dma("pool", wO[:].rearrange("p j c -> p (j c)")[:, pc * 2048:(pc + 1) * 2048],
                    din["w_out"].rearrange("p j c -> p (j c)")[:, pc * 2048:(pc + 1) * 2048], w=["wO%d" % pc])
            bload(g1t, din["ln1_g"], 1024, ["ln1"])
            bload(b1t, din["ln1_b"], 1024, ["ln1"])
            tt("dve", h2f[:], b1t[:], modB[:, 2048:3072], ALU.mult, r=["ln1", "modB"], w=["h2f"])
            tt("dve", modB[:, 1024:2048], modB[:, 1024:2048], h2f[:], ALU.add, r=["modB", "h2f"], w=["modB"])
            tt("dve", modB[:, 2048:3072], modB[:, 2048:3072], g1t[:], ALU.mult, r=["modB", "ln1"], w=["modB"])

            def mm6(t_):
                ts_ = slice(t_ * 128, (t_ + 1) * 128)
                dma("sp", xin[t_ % 2][:], din["x_tok"][ts_, :], w=["xin%d" % (t_ % 2)])
                for half in range(2):
                    hs = slice(half * 512, (half + 1) * 512)
                    bi = 2 * (t_ % 2) + half
                    for m in range(12):
                        if m < 4:
                            l_ = oTa[:, m, ts_]; lk_ = "oTa"
                        else:
                            l_ = xo[:, t_, :].rearrange("p (f k) -> p f k", k=128)[:, m - 4, :]; lk_ = "xo%d" % t_
                        mm(pb[bi][:], l_, wO[:, m, hs], m == 0, m == 11, r=[lk_, "wO%d" % (m // 2)], w=["pb%d" % bi])

            def evac6(t_):
                ts_ = slice(t_ * 128, (t_ + 1) * 128)
                xi = xin[t_ % 2]; xik = "xin%d" % (t_ % 2); tl_ = tls[t_ % 2]; tlk = "tl%d" % (t_ % 2)
                for half in range(2):
                    hs = slice(half * 512, (half + 1) * 512)
                    bi = 2 * (t_ % 2) + half
                    tt("dve", tl_[:, hs], pb[bi][:], modB[:, half * 512:(half + 1) * 512], ALU.mult, r=["pb%d" % bi, "modB"], w=[tlk])
                stt("dve", tl_[:], xi[:], ALPHA, tl_[:], ALU.mult, ALU.add, r=[xik, tlk], w=[tlk])

            def norm6(t_):
                ts_ = slice(t_ * 128, (t_ + 1) * 128)
                tl_ = tls[t_ % 2]; tlk = "tl%d" % (t_ % 2); xo_ = x1t[t_ % 2]; xok = "x1t%d" % (t_ % 2)
                for q_ in range(4):
                    A("dve", lambda e, q_=q_, tl_=tl_, st8=st8: e.bn_stats(out=st8[:, q_, :], in_=tl_[:, q_ * 256:(q_ + 1) * 256]), r=[tlk], w=["st8"])
                A("dve", lambda e, st8=st8, mv2=mv2: e.bn_aggr(out=mv2[:], in_=st8[:]), r=["st8"], w=["mv2"])
                act(rstd[:], mv2[:, 1:2], AF.Sqrt, r=["mv2", "eps"], w=["rstd"], bias=epsln[:, 0:1], scale=1.0)
                A("dve", lambda e, rstd=rstd: e.reciprocal(out=rstd[:], in_=rstd[:]), r=["rstd"], w=["rstd"])
                ts("dve", tl_[:], tl_[:], mv2[:, 0:1], rstd[:, 0:1], ALU.subtract, ALU.mult, r=[tlk, "mv2", "rstd"], w=[tlk])
                tt("dve", h2f[:], tl_[:], modB[:, 2048:3072], ALU.mult, r=[tlk, "modB"], w=["h2f"])
                tt("dve", h2b[:], h2f[:], modB[:, 1024:2048], ALU.add, r=["h2f", "modB"], w=["h2b"])
                tt("dve", xo_[:], tl_[:], g1t[:], ALU.mult, r=[tlk, "ln1"], w=[xok])
                tt("dve", xo_[:], xo_[:], b1t[:], ALU.add, r=[xok, "ln1"], w=[xok])
                dma("sp", x1s[ts_, :], xo_[:], r=[xok], w=["x1s%d" % t_])
                if dbg and "x1" in dbg:
                    dma("sp", dbgo["d_x1"][ts_, :], xo_[:], r=[xok], w=["d_x1"])

            def tr6(t_):
                ts_ = slice(t_ * 128, (t_ + 1) * 128)
                for k in range(8):
                    A("pe", lambda e, k=k: e.transpose(pT[:, k, :], h2b[:, k * 128:(k + 1) * 128], identb[:]), r=["h2b", "identb"], w=["pT"])
                cp("act", hT[:, :, ts_], pT[:], r=["pT"], w=["hT"])

            tls = [tl, T(ph, "tlB", [128, 1024])]
            mm6(0)
            evac6(0)
            for t_ in range(16):
                if t_ + 1 < 16:
                    mm6(t_ + 1)
                norm6(t_)
                if t_ + 1 < 16:
                    evac6(t_ + 1)
                tr6(t_)
        mix2.close()
        sc.barrier()
        ckpt("ln1")

        with ES() as ph:
            wD = T(ph, "wD", [128, NJ, 1024], BF16); aT = T(ph, "aT", [128, NJ, 1024], BF16)
            wU = [T(ph, "wU%d" % i, [128, 8, 256], BF16) for i in range(2)]
            ub = [[T(ph, "ub%d%d" % (gv, i), [128, 1026]) for i in range(2)] for gv in range(2)]
            cv = [[T(ph, "cv%d%d" % (gv, i), [128, 512]) for i in range(2)] for gv in range(2)]
            sg = [T(ph, "sg%d" % i, [128, 512]) for i in range(2)]
            halo = T(ph, "halo", [128, NJ, 2, 2]); fcw = T(ph, "fcw", [128, NJ, 2, 3]); fcb = T(ph, "fcb", [128, NJ, 2])
            g2t = T(ph, "g2t", [128, 1024]); b2t = T(ph, "b2t", [128, 1024])
            x1i = [T(ph, "x1i%d" % i, [128, 1024]) for i in range(2)]; tl = T(ph, "tl2", [128, 1024])
            ot = [T(ph, "ot%d" % i, [128, 1024]) for i in range(2)]
            st8 = T(ph, "st8b", [128, 4, 6]); mv2 = T(ph, "mv2b", [128, 2]); rstd = T(ph, "rstdb", [128, 1])
            tls7 = [tl, T(ph, "tl2B", [128, 1024])]
            dma("sp", fcw[:], din["fcw"], w=["fcw"]); dma("sp", fcb[:], din["fcb"], w=["fcb"])
            bload(g2t, din["ln2_g"], 1024, ["ln2"])
            bload(b2t, din["ln2_b"], 1024, ["ln2"])
            if not hasattr(sc, "groups"):
                sc.groups = {}
            sc.groups["wD"] = ["wD#%d" % i for i in range(11)]
            for blk in range(2):
                for j in range(NJ):
                    w_ = wU[j % 2]; wk = "wU%d" % (j % 2)
                    def load_wu(jj):
                        dma("pool", wU[jj % 2][:].rearrange("p k c -> p (k c)"), din["w_up"][jj].rearrange("p k c -> p (k c)"), w=["wU%d" % (jj % 2)])
                    if blk == 0 and j == 0:
                        load_wu(0)
                    jn = (j + 1) % NJ
                    if not (blk == 1 and j == NJ - 1):
                        load_wu(jn)
                    if blk == 0 and 1 <= j <= 11:
                        c0 = (j - 1) * 2048
                        dma("pool", wD[:].rearrange("p j c -> p (j c)")[:, c0:c0 + 2048], din["w_dn"].rearrange("p j c -> p (j c)")[:, c0:c0 + 2048], w=["wD#%d" % (j - 1)])
                    us = [ub[gv][j % 2] for gv in range(2)]
                    uks = ["ub%d%d" % (gv, j % 2) for gv in range(2)]
                    for gv in range(2):
                        if blk == 0:
                            A("dve", lambda e, u_=us[gv]: e.memset(u_[:, 0:2], 0.0), w=[uks[gv] + "h"])
                        else:
                            cp("pool", us[gv][:, 0:2], halo[:, j, gv, :], r=["halo%d_%d" % (j, gv)], w=[uks[gv] + "h"])
                    for hf in range(2):
                        c0 = blk * 1024 + hf * 512
                        for gv in range(2):
                            p_ = pb[gv * 2 + hf]; pk = "pb%d" % (gv * 2 + hf)
                            for k in range(8):
                                mm(p_[:], w_[:, k, gv * 128:(gv + 1) * 128], hT[:, k, c0:c0 + 512], k == 0, k == 7, r=[wk, "hT"], w=[pk])
                            cp("act", us[gv][:, 2 + hf * 512:2 + (hf + 1) * 512], p_[:], r=[pk], w=[uks[gv] + "ab"[hf]])
                            act(cv[gv][hf][:], p_[:], AF.Identity, r=[pk, "fcw", "fcb"], w=["cv%d%d" % (gv, hf)],
                                scale=fcw[:, j, gv, 2:3], bias=fcb[:, j, gv:gv + 1])
                    for hf in range(2):
                        for tap in (1, 0):
                            for gv in range(2):
                                rk = [uks[gv] + "a", "cv%d%d" % (gv, hf), "fcw"] + ([uks[gv] + "h"] if hf == 0 else [uks[gv] + "b"])
                                stt("dve", cv[gv][hf][:], us[gv][:, tap + hf * 512:tap + hf * 512 + 512], fcw[:, j, gv, tap:tap + 1], cv[gv][hf][:],
                                    ALU.mult, ALU.add, r=rk, w=["cv%d%d" % (gv, hf)])
                    for hf in range(2):
                        act(sg[hf][:], cv[0][hf][:], AF.Silu, r=["cv0%d" % hf], w=["sg%d" % hf])
                    for hf in range(2):
                        tt("dve", aT[:, j, hf * 512:(hf + 1) * 512], sg[hf][:], cv[1][hf][:], ALU.mult, r=["sg%d" % hf, "cv1%d" % hf], w=["aT"])
                    if blk == 0:
                        for gv in range(2):
                            cp("pool", halo[:, j, gv, :], us[gv][:, 1024:1026], r=[uks[gv] + "b"], w=["halo%d_%d" % (j, gv)])
                def front7(t8):
                    t_ = blk * 8 + t8
                    ts_ = slice(t_ * 128, (t_ + 1) * 128)
                    tl_ = tls7[t8 % 2]; tlk = "tl7%d" % (t8 % 2)
                    dma("sp", x1i[t8 % 2][:], x1s[ts_, :], r=["x1s%d" % t_], w=["x1i%d" % (t8 % 2)])
                    for half in range(2):
                        hs = slice(half * 512, (half + 1) * 512)
                        for j in range(NJ):
                            mm(pb[4 + half][:], aT[:, j, t8 * 128:(t8 + 1) * 128], wD[:, j, hs], j == 0, j == NJ - 1, r=["aT", "wD"], w=["pb%d" % (4 + half)])
                        tt("dve", tl_[:, hs], pb[4 + half][:], g2p1[:, half * 512:(half + 1) * 512], ALU.mult, r=["pb%d" % (4 + half), "g2p1"], w=[tlk])

                def back7(t8):
                    t_ = blk * 8 + t8
                    ts_ = slice(t_ * 128, (t_ + 1) * 128)
                    tl_ = tls7[t8 % 2]; tlk = "tl7%d" % (t8 % 2)
                    xi = x1i[t8 % 2]; xik = "x1i%d" % (t8 % 2); o_ = ot[t8 % 2]; ok_ = "ot%d" % (t8 % 2)
                    stt("dve", tl_[:], xi[:], ALPHA, tl_[:], ALU.mult, ALU.add, r=[xik, tlk], w=[tlk])
                    layer_norm(tl_, tlk, g2t, b2t, "ln2", o_, ok_, st8, mv2, rstd)
                    dma("sp", out[ts_, :], o_[:], r=[ok_], w=["out"])

                front7(0)
                for t8 in range(8):
                    if t8 + 1 < 8:
                        front7(t8 + 1)
                    back7(t8)
        outs = [op for op in sc.ops if op.dma]
        A("sp", lambda e: e.wait_ge(outs[0].sem, 16), r=["out", "d_x1", "d_mod", "d_oTa", "d_xo", "d_QT", "d_KT", "d_O", "d_rec", "d_Vp", "d_ones", "d_hT", "d_wA", "d_kf", "d_ge", "d_gm", "d_m8"], w=["done"], force=True)
        sc.emit(nc, top)
    return nc


_CACHE = {}


def _prep(inputs):
    sh = _shared_inputs(inputs)
    sh.update(_consts())
    x = np.asarray(inputs["x"], dtype=np.float32)
    c = np.asarray(inputs["c"], dtype=np.float32)
    maps = []
    for b in range(8):
        m = dict(sh)
        m["x_tok"] = np.ascontiguousarray(x[b])
        m["xT"] = np.ascontiguousarray(x[b].T)
        m["c_col"] = np.ascontiguousarray(c[b].reshape(8, 128).T)
        maps.append(m)
    return maps


def kernel(**inputs):
    maps = _prep(inputs)
    shapes = {n: (a.shape, _DT[a.dtype]) for n, a in maps[0].items()}
    nc = build(shapes)
    res = run_bass_kernel_spmd(nc, maps, core_ids=list(range(8)))
    return np.stack([np.asarray(r["out"], dtype=np.float32) for r in res.results], axis=0)
```
